# Optimizing a Trainium2 kernel written in Bass

```python
import jax, jax.numpy as jnp
from jax import lax
import numpy as np

D_MODEL = 1024
BATCH = 16
SEQ = 2048
DEPTH = 2

D_MIX = D_MODEL
D_CF = D_MIX // 4
D_SC = D_MIX // 4
D_ATT = D_MIX - D_CF - D_SC
HEAD_DIM = 64
N_ATT_HEADS = D_ATT // HEAD_DIM
CF_WIDTH = 31
SC_WIDTH = 3
Q_BLOCK = 128
EPS = 1e-6
SPLITS = (D_CF, D_CF, D_CF, D_SC, D_SC, D_SC, D_SC, D_ATT, D_ATT, D_ATT, D_ATT, N_ATT_HEADS)
N_IN = sum(SPLITS)
SPLIT_IDX = tuple(int(i) for i in np.cumsum(SPLITS)[:-1])

kernel_name = 'hymba_style_conformer_shortconv_fox_hybrid'


def rmsnorm(x, g):
    xf = x.astype(jnp.float32)
    y = xf * lax.rsqrt(jnp.mean(xf * xf, axis=-1, keepdims=True) + EPS)
    return (y * g.astype(jnp.float32)).astype(x.dtype)


def layernorm(x, g, b):
    xf = x.astype(jnp.float32)
    mu = jnp.mean(xf, axis=-1, keepdims=True)
    xc = xf - mu
    y = xc * lax.rsqrt(jnp.mean(xc * xc, axis=-1, keepdims=True) + EPS)
    return (y * g.astype(jnp.float32) + b.astype(jnp.float32)).astype(x.dtype)


def causal_depthwise_conv(x, w):
    width, c = w.shape
    return lax.conv_general_dilated(
        x, w[:, None, :].astype(x.dtype), window_strides=(1,),
        padding=[(width - 1, 0)], dimension_numbers=('NWC', 'WIO', 'NWC'),
        feature_group_count=c)


def forgetting_attention(q, k, v, log_f):
    seq = q.shape[1]
    scale = 1.0 / float(np.sqrt(HEAD_DIM))
    c = jnp.transpose(lax.cumsum(log_f, axis=1), (0, 2, 1))
    outs = []
    for start in range(0, seq, Q_BLOCK):
        end = start + Q_BLOCK
        qb = q[:, start:end].astype(jnp.float32)
        kb = k[:, :end].astype(jnp.float32)
        vb = v[:, :end]
        s = jnp.einsum('bqhd,bkhd->bhqk', qb, kb) * scale
        decay = c[:, :, start:end, None] - c[:, :, None, :end]
        qpos = jnp.arange(start, end)[:, None]
        kpos = jnp.arange(end)[None, :]
        s = jnp.where(kpos <= qpos, s + decay, -jnp.inf)
        p = jax.nn.softmax(s, axis=-1)
        outs.append(jnp.einsum('bhqk,bkhd->bqhd', p.astype(v.dtype), vb))
    return jnp.concatenate(outs, axis=1)


def hybrid_layer(x, norm_g, w_in, b_f, cf_dw, cf_dw_b, cf_ln_g, cf_ln_b, cf_pw,
                 sc_dw, q_norm_g, k_norm_g, w_out):
    bsz, seq, _ = x.shape
    h = rmsnorm(x, norm_g)
    proj = h @ w_in
    (cf_a, cf_g, cf_z, sc_b, sc_c, sc_x, sc_z,
     q, k, v, att_z, f_logit) = jnp.split(proj, SPLIT_IDX, axis=-1)

    u = cf_a * jax.nn.sigmoid(cf_g)
    u = causal_depthwise_conv(u, cf_dw) + cf_dw_b
    u = jax.nn.silu(layernorm(u, cf_ln_g, cf_ln_b))
    y_cf = (u @ cf_pw) * jax.nn.silu(cf_z)

    y_sc = sc_b * causal_depthwise_conv(sc_c * sc_x, sc_dw) * jax.nn.silu(sc_z)

    q = rmsnorm(q.reshape(bsz, seq, N_ATT_HEADS, HEAD_DIM), q_norm_g)
    k = rmsnorm(k.reshape(bsz, seq, N_ATT_HEADS, HEAD_DIM), k_norm_g)
    v = v.reshape(bsz, seq, N_ATT_HEADS, HEAD_DIM)
    log_f = jax.nn.log_sigmoid((f_logit + b_f).astype(jnp.float32))
    o = forgetting_attention(q, k, v, log_f).reshape(bsz, seq, D_ATT)
    y_att = o * jax.nn.silu(att_z)

    mixed = jnp.concatenate([y_cf, y_sc, y_att], axis=-1)
    return x + mixed @ w_out


def setup_inputs(seed: int = 0) -> dict:
    key = jax.random.key(seed)
    ks = jax.random.split(key, 14)
    f32 = jnp.float32
    x = jax.random.normal(ks[0], (BATCH, SEQ, D_MODEL), f32)
    norm_g = 1.0 + 0.05 * jax.random.normal(ks[1], (DEPTH, D_MODEL), f32)
    w_in = jax.random.normal(ks[2], (DEPTH, D_MODEL, N_IN), f32) * D_MODEL ** -0.5
    b_f = 2.0 + 0.5 * jax.random.normal(ks[3], (DEPTH, N_ATT_HEADS), f32)
    cf_dw = jax.random.normal(ks[4], (DEPTH, CF_WIDTH, D_CF), f32) * CF_WIDTH ** -0.5
    cf_dw_b = 0.02 * jax.random.normal(ks[5], (DEPTH, D_CF), f32)
    cf_ln_g = 1.0 + 0.05 * jax.random.normal(ks[6], (DEPTH, D_CF), f32)
    cf_ln_b = 0.02 * jax.random.normal(ks[7], (DEPTH, D_CF), f32)
    cf_pw = jax.random.normal(ks[8], (DEPTH, D_CF, D_CF), f32) * D_CF ** -0.5
    sc_dw = jax.random.normal(ks[9], (DEPTH, SC_WIDTH, D_SC), f32) * SC_WIDTH ** -0.5
    q_norm_g = 1.0 + 0.05 * jax.random.normal(ks[10], (DEPTH, N_ATT_HEADS, HEAD_DIM), f32)
    k_norm_g = 1.0 + 0.05 * jax.random.normal(ks[11], (DEPTH, N_ATT_HEADS, HEAD_DIM), f32)
    w_out = jax.random.normal(ks[12], (DEPTH, D_MIX, D_MODEL), f32) * D_MIX ** -0.5
    return {'x': x, 'norm_g': norm_g, 'w_in': w_in, 'b_f': b_f, 'cf_dw': cf_dw,
            'cf_dw_b': cf_dw_b, 'cf_ln_g': cf_ln_g, 'cf_ln_b': cf_ln_b, 'cf_pw': cf_pw,
            'sc_dw': sc_dw, 'q_norm_g': q_norm_g, 'k_norm_g': k_norm_g, 'w_out': w_out}


def reference(x, norm_g, w_in, b_f, cf_dw, cf_dw_b, cf_ln_g, cf_ln_b, cf_pw,
              sc_dw, q_norm_g, k_norm_g, w_out):
    for l in range(DEPTH):
        x = hybrid_layer(x, norm_g[l], w_in[l], b_f[l], cf_dw[l], cf_dw_b[l],
                         cf_ln_g[l], cf_ln_b[l], cf_pw[l], sc_dw[l],
                         q_norm_g[l], k_norm_g[l], w_out[l])
    return x
```

```python
import math
import numpy as np
from contextlib import ExitStack
import concourse.bass as bass
import concourse.mybir as mybir
from concourse.bass_utils import run_bass_kernel_spmd

F32 = mybir.dt.float32
BF16 = mybir.dt.bfloat16
AF = mybir.ActivationFunctionType
ALU = mybir.AluOpType

EPS = 1e-6
C_G, C_CFB, C_LNG, C_LNB, C_CFW, C_SCW, C_GQ, C_GK, NCOL = 0, 8, 10, 12, 14, 76, 82, 86, 90
D_CFWH, D_SCWH, D_GQ8, D_LNGH, D_LNBH, ND = 0, 62, 68, 72, 74, 76
NSLOT = 3
NFT = 8
NBT = 6
NDMASEM = 25
LA = 3


class Prog:
    ENGS = ["pe", "act", "dve", "pool", "sp"]

    def __init__(self, nc):
        self.nc = nc
        self.eng = {"pe": nc.tensor, "act": nc.scalar, "dve": nc.vector, "pool": nc.gpsimd, "sp": nc.sync}
        self.ops = []

    def add(self, eng, fn, reads=(), writes=(), dma=False):
        self.ops.append(dict(eng=eng, fn=fn, reads=list(reads), writes=list(writes), dma=dma))

    def finalize(self, stack):
        ops = self.ops
        last_w, readers = {}, {}
        for i, op in enumerate(ops):
            deps = set()
            for k in op["reads"]:
                if k in last_w:
                    deps.add(last_w[k])
            for k in op["writes"]:
                if k in last_w:
                    deps.add(last_w[k])
                for r in readers.get(k, ()):
                    deps.add(r)
            deps.discard(i)
            keep = set()
            for d in deps:
                dop = ops[d]
                if (not dop["dma"]) and dop["eng"] == "pe" and op["eng"] == "pe" and not op["dma"]:
                    continue
                keep.add(d)
            latest = {}
            kept2 = set()
            for d in keep:
                dop = ops[d]
                if dop["dma"]:
                    kept2.add(d)
                else:
                    latest[dop["eng"]] = max(latest.get(dop["eng"], -1), d)
            kept2 |= set(latest.values())
            op["deps"] = kept2
            for k in op["reads"]:
                readers.setdefault(k, []).append(i)
            for k in op["writes"]:
                last_w[k] = i
                readers[k] = []
        targets = set()
        for op in ops:
            targets |= op["deps"]
        cnt = {e: 0 for e in self.ENGS}
        dma_use = [0] * NDMASEM
        rr = 0
        for i, op in enumerate(ops):
            if op["dma"]:
                s = rr % NDMASEM
                rr += 1
                dma_use[s] += 1
                op["sem"] = ("dma", s)
                op["val"] = 16 * dma_use[s]
                op["prev"] = 16 * (dma_use[s] - 1)
            elif i in targets:
                cnt[op["eng"]] += 1
                op["sem"] = ("eng", op["eng"])
                op["val"] = cnt[op["eng"]]
            else:
                op["sem"] = None
        sems = {}

        def sem(key):
            if key not in sems:
                sems[key] = stack.enter_context(self.nc.semaphore("s_%s_%s" % key))
            return sems[key]

        waited = {e: {} for e in self.ENGS}
        nwait = 0
        for op in ops:
            e = op["eng"]
            E = self.eng[e]
            need = {}
            for d in op["deps"]:
                dop = ops[d]
                key = dop["sem"]
                need[key] = max(need.get(key, 0), dop["val"])
            if op["dma"] and op["prev"] > 0:
                need[op["sem"]] = max(need.get(op["sem"], 0), op["prev"])
            for key, val in need.items():
                if waited[e].get(key, 0) < val:
                    E.wait_ge(sem(key), val)
                    waited[e][key] = val
                    nwait += 1
            ins = op["fn"]()
            if op["sem"] is not None and ins is not None:
                ins.then_inc(sem(op["sem"]), 16 if op["dma"] else 1)
        return dict(n_ops=len(ops), n_wait=nwait, cnt=cnt)


class TPool:
    def __init__(self, tiles, name):
        self.tiles = tiles
        self.name = name
        self.free = list(range(len(tiles)))

    def alloc(self):
        assert self.free, "temp pool %s exhausted" % self.name
        i = self.free.pop(0)
        return self.tiles[i], (self.name, i)

    def release(self, key):
        assert key[0] == self.name and key[1] not in self.free
        self.free.append(key[1])


class Rot:
    def __init__(self, items):
        self.items = items
        self.i = 0

    def get(self):
        it = self.items[self.i % len(self.items)]
        self.i += 1
        return it


def chunk_order():
    o = []
    o += [("cfa", 0), ("cfg", 0), ("cfa", 1), ("cfg", 1)]
    o += [("k", j) for j in range(4)]
    o += [("q", j) for j in range(4)]
    o += [("cfz", 0), ("cfz", 1)]
    o += [("scC", 0), ("scx", 0), ("scC", 1), ("scx", 1)]
    o += [("scB", 0), ("scz", 0), ("scB", 1), ("scz", 1)]
    o += [("az", j) for j in range(4)]
    return o


_COL0 = {"cfa": 0, "cfg": 256, "cfz": 512, "scB": 768, "scC": 1024, "scx": 1280, "scz": 1536,
         "q": 1792, "k": 2304, "v": 2816, "az": 3328, "f": 3840}


def build(NSEQ, NBLK, NL):
    S = NBLK * 512
    NTT = NBLK * 4
    nc = bass.Bass("TRN2", target_bir_lowering=False)

    def dram(name, shape, dtype=F32, kind="ExternalInput"):
        return nc.dram_tensor(name, shape, dtype, kind=kind).ap()

    x_d = dram("x", [NSEQ, S, 1024])
    wfm_d = dram("wfm", [NL, 26, 128, 8, 128])
    wv_d = dram("wv", [NL, 128, 8, 512])
    wf_d = dram("wf", [NL, 128, 64])
    wo_d = dram("wo", [NL, 8, 128, 8, 128])
    pw_d = dram("pw", [NL, 128, 2, 256])
    cols_d = dram("cols", [128, NL, NCOL])
    bfb_d = dram("bfb", [128, NL, 32])
    cst_d = dram("cst", [128, 512])
    y_d = dram("y", [NSEQ, S, 1024], kind="ExternalOutput")
    scr_d = dram("scr", [2, 8, 3, 512], BF16, kind="Internal")
    wfm_b = dram("wfm_b", [NL, 26, 128, 1024], BF16, kind="Internal")
    wo_b = dram("wo_b", [NL, 8, 128, 1024], BF16, kind="Internal")
    wv_b = dram("wv_b", [NL, 128, 4096], BF16, kind="Internal")
    pw_b = dram("pw_b", [NL, 128, 512], BF16, kind="Internal")

    def A(name, shape, dtype):
        return nc.alloc_sbuf_tensor("sb_" + name, shape, dtype)

    xT = A("xT", [128, 8, S], F32)
    KA = A("KA", [128, 8, S], BF16)
    V = A("V", [128, NTT, 8, 128], BF16)
    QA = A("QA", [128, 8, 512], BF16)
    HMs = [A("HM0", [128, 8, 512], BF16), A("HM1", [128, 8, 512], BF16)]
    HMfs = [h_.bitcast(F32).reshape([128, 2, 1024]) for h_ in HMs]
    XSs = [[hf_[:, 0, :], hf_[:, 1, :]] for hf_ in HMfs]
    XSKs = [[[("HM", p_, kc, hf) for kc in range(4) for hf in (0, 1)], [("HM", p_, kc, hf) for kc in range(4, 8) for hf in (0, 1)]]
            for p_ in (0, 1)]
    WR = [A("WR%d" % i, [128, 8, 128], BF16) for i in range(NSLOT)]
    WVv = A("WVv", [128, 8, 512], BF16)
    WVf = A("WVf", [128, 8, 8], BF16)
    WFs = A("WFs", [128, 64], F32)
    PW = A("PW", [128, 2, 256], BF16)
    U0 = A("U0", [128, 2, 542], BF16)
    S0 = A("S0", [128, 2, 514], BF16)
    ZC = A("ZC", [128, 2, 512], BF16)
    GS = A("GS", [128, 2, 512], BF16)
    ZA = A("ZA", [128, 4, 512], BF16)
    FTbig = A("FT", [128, NFT, 512], F32)
    ft = TPool([FTbig[:, i, :] for i in range(NFT)], "FT")
    FTflat = FTbig.reshape([128, NFT // 2, 1024])
    FTb16 = FTbig.bitcast(BF16)
    bt = TPool([A("BT%d" % i, [128, 512], BF16) for i in range(NBT)], "BT")
    cols = A("cols", [128, NL, NCOL], F32)
    DC = A("DC", [128, NL, ND], F32)
    bfb = A("bfb", [128, NL, 32], F32)
    cstf = A("cstf", [128, 512], F32)
    cstb = A("cstb", [128, 512], BF16)
    onesb = A("onesb", [128, 128], BF16)
    nonesf = A("nonesf", [128, 128], F32)
    CB = A("CB", [128, 2], F32)
    ps = [nc.alloc_psum_tensor("ps%d" % i, [128, 512], F32) for i in range(8)]
    acc = Rot([(ps[i], ("ps", i)) for i in range(4)])
    misc = Rot([(ps[i], ("ps", i)) for i in (4, 5)])
    stat = Rot([(ps[i], ("ps", i)) for i in (6, 7)])

    ident_f = cstf[:, 0:128]
    negtri_f = cstf[:, 128:256]
    ident_b = cstb[:, 0:128]
    bdiag_b = cstb[:, 256:384]
    maskneg_b = cstb[:, 384:512]

    P = Prog(nc)
    V_ = nc.vector
    G_ = nc.gpsimd
    S_ = nc.scalar
    T_ = nc.tensor

    def mm(out, lhsT, rhs, start, stop, reads, writes):
        P.add("pe", lambda: T_.matmul(out, lhsT, rhs, start=start, stop=stop), reads, writes)

    hctx = dict(p=0)

    def hmk(kc):
        return [("HM", hctx["p"], kc, 0), ("HM", hctx["p"], kc, 1)]

    P.add("sp", lambda: nc.sync.dma_start(out=cols[:, :, :], in_=cols_d), [], ["cols"], dma=True)
    P.add("sp", lambda: nc.sync.dma_start(out=bfb[:, :, :], in_=bfb_d), [], ["bfb"], dma=True)
    P.add("sp", lambda: nc.sync.dma_start(out=cstf[:, :], in_=cst_d), [], ["cstf"], dma=True)
    P.add("dve", lambda: V_.tensor_copy(out=cstb[:, :], in_=cstf[:, :]), ["cstf"], ["cstb"])
    P.add("pool", lambda: G_.memset(onesb[:, :], 1.0), [], ["onesb"])
    P.add("pool", lambda: G_.memset(nonesf[:, :], -1.0), [], ["nonesf"])
    P.add("pool", lambda: G_.memset(V[:, :, :, :], 1.0), [], [("V", tt) for tt in range(NTT)])
    P.add("pool", lambda: G_.memset(QA[64:70, :, :], -1.0), [], [("QAc",)])
    P.add("pool", lambda: G_.memset(KA[64:70, :, :], 1.0), [], [("KAc",)])
    for (dst, src, n, sc) in ((D_CFWH, C_CFW, 62, 0.5), (D_SCWH, C_SCW, 6, 1.0), (D_GQ8, C_GQ, 4, 0.125),
                              (D_LNGH, C_LNG, 2, 0.5), (D_LNBH, C_LNB, 2, 0.5)):
        P.add("dve", lambda dst=dst, src=src, n=n, sc=sc: V_.tensor_scalar(
            out=DC[:, :, dst:dst + n], in0=cols[:, :, src:src + n], scalar1=sc, scalar2=None, op0=ALU.mult),
            ["cols"], [("DC", dst)])
    dc_all = [("DC", d) for d in (D_CFWH, D_SCWH, D_GQ8, D_LNGH, D_LNBH)]

    jobs = []
    for l in range(NL):
        for c in range(26):
            jobs.append((wfm_d[l, c].rearrange("p a b -> p (a b)"), wfm_b[l, c], 1024, ("wsc", l, c)))
        for m in range(8):
            jobs.append((wo_d[l, m].rearrange("p a b -> p (a b)"), wo_b[l, m], 1024, ("wsc", l, 26 + m)))
        for k in range(4):
            jobs.append((wv_d[l][:, 2 * k:2 * k + 2, :].rearrange("p a b -> p (a b)"), wv_b[l][:, 1024 * k:1024 * (k + 1)],
                         1024, ("wvsc", l, k)))
        jobs.append((pw_d[l].rearrange("p a b -> p (a b)"), pw_b[l], 512, ("pwsc", l)))
    WRflat = [w.reshape([128, 1024]) for w in WR]
    cast_rot = ["pool", "act", "dve"]
    NST = NFT // 2

    def emit_load(n):
        src, dst, ne, key = jobs[n]
        j = n % NST
        P.add("sp", lambda: nc.sync.dma_start(out=FTflat[:, j, 0:ne], in_=src), [], [("FT", 2 * j), ("FT", 2 * j + 1)], dma=True)

    def emit_cast_store(n):
        src, dst, ne, key = jobs[n]
        j = n % NST
        slot = n % NSLOT
        en = cast_rot[n % 3]
        o_ = WRflat[slot][:, 0:ne]
        i_ = FTflat[:, j, 0:ne]
        if en == "pool":
            fn = lambda: G_.tensor_copy(out=o_, in_=i_)
        elif en == "act":
            fn = lambda: S_.copy(out=o_, in_=i_)
        else:
            fn = lambda: V_.tensor_copy(out=o_, in_=i_)
        P.add(en, fn, [("FT", 2 * j), ("FT", 2 * j + 1)], [("WR", slot)])
        P.add("sp", lambda: nc.sync.dma_start(out=dst, in_=o_), [("WR", slot)], [key], dma=True)

    NJ0 = 39 if (NL > 1 and NBLK >= 2) else len(jobs)
    PRE = min(3, NST - 1)
    for n in range(NJ0 + PRE):
        if n < NJ0:
            emit_load(n)
        if n - PRE >= 0:
            emit_cast_store(n - PRE)

    Vf32 = V.bitcast(F32).reshape([128, NTT * 512])
    Vb16 = V.reshape([128, NTT * 1024])
    stg_f = Vf32[:, (NTT - 4) * 512:(NTT - 4) * 512 + 1024]
    stg_fk = [("V", NTT - 4), ("V", NTT - 3)]
    stg_b = [Vb16[:, (NTT - 2) * 1024:(NTT - 1) * 1024], Vb16[:, (NTT - 1) * 1024:NTT * 1024]]
    stg_bk = [[("V", NTT - 2)], [("V", NTT - 1)]]

    def late_gen():
        late = jobs[NJ0:]

        def A(n):
            src, dst, ne, key = late[n]
            P.add("sp", lambda: nc.sync.dma_start(out=stg_f[:, 0:ne], in_=src), [], stg_fk, dma=True)

        def B(n):
            src, dst, ne, key = late[n]
            P.add("act", lambda: S_.copy(out=stg_b[n % 2][:, 0:ne], in_=stg_f[:, 0:ne]), stg_fk, stg_bk[n % 2])

        def C(n):
            src, dst, ne, key = late[n]
            P.add("sp", lambda: nc.sync.dma_start(out=dst, in_=stg_b[n % 2][:, 0:ne]), stg_bk[n % 2], [key], dma=True)

        if not late:
            return
        A(0)
        yield
        for n in range(len(late)):
            B(n)
            yield
            if n + 1 < len(late):
                A(n + 1)
                yield
            C(n)
            yield
        P.add("pool", lambda: G_.memset(V[:, NTT - 4:NTT, :, 64:128], 1.0), [], [("V", tt) for tt in range(NTT - 4, NTT)])

    lstate = dict(gen=late_gen(), active=False, done=False)

    def late_step(n=1):
        if not lstate["active"] or lstate["done"]:
            return
        for _ in range(n):
            try:
                next(lstate["gen"])
            except StopIteration:
                lstate["done"] = True
                return

    def late_flush():
        if lstate["done"]:
            return
        for _ in lstate["gen"]:
            pass
        lstate["done"] = True

    chunks = []
    for s in range(NSEQ):
        sb_ = [(l, b) for l in range(NL) for b in range(NBLK)]
        for n_, (l, b) in enumerate(sb_):
            if n_ == 0:
                for c in range(4):
                    chunks.append((wfm_b[l, c], ("wsc", l, c)))
            for c in range(4, 26):
                chunks.append((wfm_b[l, c], ("wsc", l, c)))
            if n_ + 1 < len(sb_):
                l2 = sb_[n_ + 1][0]
                for c in range(4):
                    chunks.append((wfm_b[l2, c], ("wsc", l2, c)))
            for m in range(8):
                chunks.append((wo_b[l, m], ("wsc", l, 26 + m)))
    cstate = dict(cons=0, issue=0)

    def get_chunk():
        while cstate["issue"] < min(len(chunks), cstate["cons"] + NSLOT):
            i = cstate["issue"]
            slot = i % NSLOT
            P.add("sp", lambda slot=slot, i=i: nc.sync.dma_start(out=WRflat[slot][:, :], in_=chunks[i][0]),
                  [chunks[i][1]], [("WR", slot)], dma=True)
            cstate["issue"] += 1
        slot = cstate["cons"] % NSLOT
        cstate["cons"] += 1
        late_step()
        return WR[slot], ("WR", slot)

    def proj_chunk(pp=None, rot=None):
        W, kW = get_chunk()
        pa, ka = (rot or acc).get()
        pp = hctx["p"] if pp is None else pp
        for kc in range(8):
            mm(pa[:, :], W[:, kc, :], HMs[pp][:, kc, :], kc == 0, kc == 7,
               [kW, ("HM", pp, kc, 0), ("HM", pp, kc, 1)], [ka])
        return pa, ka

    cv = dict(key=None, gen=None, U1=None)

    def make_conv(lt):
        U1 = []

        def conv_gen():
            for i in (0, 1):
                AA, kAA = ft.alloc()
                AB, kAB = ft.alloc()
                accs = [(AA, kAA), (AB, kAB)]
                for k in range(31):
                    At, kAt = accs[k % 2]
                    wc = DC[:, lt, D_CFWH + i * 31 + k:D_CFWH + i * 31 + k + 1]
                    src = U0[:, i, k:k + 512]
                    rk = [("U0", i), ("U0h", i), ("DC", D_CFWH)]
                    if k < 2:
                        P.add("dve", lambda At=At, src=src, wc=wc: V_.tensor_scalar(
                            out=At[:, :], in0=src, scalar1=wc, scalar2=None, op0=ALU.mult), rk, [kAt])
                    else:
                        P.add("dve", lambda At=At, src=src, wc=wc: V_.scalar_tensor_tensor(
                            out=At[:, :], in0=src, scalar=wc, in1=At[:, :], op0=ALU.mult, op1=ALU.add), rk + [kAt], [kAt])
                    yield
                P.add("dve", lambda AA=AA, AB=AB, i=i: V_.scalar_tensor_tensor(
                    out=AA[:, :], in0=AA[:, :], scalar=cols[:, lt, C_CFB + i:C_CFB + i + 1], in1=AB[:, :],
                    op0=ALU.add, op1=ALU.add), [kAA, kAB, "cols"], [kAA])
                ft.release(kAB)
                U1.append((AA, kAA))
                P.add("dve", lambda i=i: V_.tensor_copy(out=U0[:, i, 0:30], in_=U0[:, i, 512:542]), [("U0", i)], [("U0h", i)])
                yield

        return conv_gen(), U1

    def drain(gen, n):
        if gen is None:
            return
        for _ in range(n):
            try:
                next(gen)
            except StopIteration:
                return

    def emit_cf_pair(i, pp, rot):
        pa, ka = proj_chunk(pp, rot)
        pg, kg = proj_chunk(pp, rot)
        P.add("act", lambda: S_.activation(out=U0[:, i, 30:542], in_=pg[:, :], func=AF.Tanh, scale=0.5), [kg], [("U0", i)])
        P.add("dve", lambda: V_.scalar_tensor_tensor(
            out=U0[:, i, 30:542], in0=U0[:, i, 30:542], scalar=1.0, in1=pa[:, :], op0=ALU.add, op1=ALU.mult),
            [("U0", i), ka], [("U0", i)])


    ev = dict(i=0)

    def evac_copy(out, in_, reads, writes):
        ev["i"] += 1
        if ev["i"] % 2 == 0:
            P.add("act", lambda: S_.copy(out=out, in_=in_), reads, writes)
        else:
            P.add("dve", lambda: V_.tensor_copy(out=out, in_=in_), reads, writes)

    deferred = []

    def flush_deferred(n=100):
        while deferred and n > 0:
            fn, rd, wr = deferred.pop(0)
            P.add("sp", fn, rd, wr, dma=True)
            n -= 1

    def emit_A(l, b, p):
        HM = HMs[p]
        t0 = b * 512
        tts = [4 * b + i for i in range(4)]
        blk = slice(t0, t0 + 512)
        xk = lambda kc: [("xT", kc, tt) for tt in tts]
        hk = lambda kc: [("HM", p, kc, 0), ("HM", p, kc, 1)]
        P.add("act", lambda: S_.activation(out=HM[:, :, :], in_=xT[:, :, blk], func=AF.Square),
              [k for kc in range(8) for k in xk(kc)], [k for kc in range(8) for k in hk(kc)])
        pst, kst = stat.get()
        for kc in range(8):
            mm(pst[:, :], onesb[:, :], HM[:, kc, :], kc == 0, kc == 7, hk(kc) + ["onesb"], [kst])
        RS, kRS = ft.alloc()
        P.add("act", lambda: S_.activation(out=RS[:, :], in_=pst[:, :], func=AF.Ln, bias=EPS, scale=1.0 / 1024),
              [kst], [kRS])
        P.add("act", lambda: S_.activation(out=RS[:, :], in_=RS[:, :], func=AF.Exp, scale=-0.5), [kRS], [kRS])
        for kc in range(8):
            P.add("dve", lambda kc=kc: V_.scalar_tensor_tensor(
                out=HM[:, kc, :], in0=xT[:, kc, blk], scalar=cols[:, l, C_G + kc:C_G + kc + 1], in1=RS[:, :],
                op0=ALU.mult, op1=ALU.mult), xk(kc) + [kRS, "cols"], hk(kc))
        ft.release(kRS)

    def block(s, l, b, last_layer, p, nxt):
        hctx["p"] = p
        HM = HMs[p]
        XS = XSs[p]
        XSK = XSKs[p]
        t0 = b * 512
        tts = [4 * b + i for i in range(4)]
        blk = slice(t0, t0 + 512)
        xk = lambda kc: [("xT", kc, tt) for tt in tts]

        if cv["key"] == (s, l, b):
            cg, U1 = cv["gen"], cv["U1"]
        else:
            if b == 0:
                P.add("pool", lambda: G_.memset(U0[:, :, 0:30], 0.0), [], [("U0h", 0), ("U0h", 1)])
            for i in (0, 1):
                emit_cf_pair(i, p, acc)
            cg, U1 = make_conv(l)
        cv["key"] = None

        def drain_conv(n):
            drain(cg, n)

        def tanh_gate(pz, kz, out_ap, wkeys):
            P.add("act", lambda: S_.activation(out=out_ap, in_=pz[:, :], func=AF.Silu), [kz], wkeys)

        an = {}

        def a_sq():
            if nxt is None:
                return
            l2, b2 = nxt
            HM2 = HMs[1 - p]
            blk2 = slice(b2 * 512, b2 * 512 + 512)
            tts2 = [4 * b2 + i for i in range(4)]
            xk2 = lambda kc: [("xT", kc, tt) for tt in tts2]
            hk2 = lambda kc: [("HM", 1 - p, kc, 0), ("HM", 1 - p, kc, 1)]
            an["c"] = (l2, HM2, blk2, xk2, hk2)

        def a_sq_part(q):
            if nxt is None:
                return
            l2, HM2, blk2, xk2, hk2 = an["c"]
            for kc in (2 * q, 2 * q + 1):
                P.add("act", lambda kc=kc: S_.activation(out=HM2[:, kc, :], in_=xT[:, kc, blk2], func=AF.Square),
                      xk2(kc), hk2(kc))

        def a_mm():
            if nxt is None:
                return
            l2, HM2, blk2, xk2, hk2 = an["c"]
            pst, kst = stat.get()
            for kc in range(8):
                mm(pst[:, :], onesb[:, :], HM2[:, kc, :], kc == 0, kc == 7, hk2(kc) + ["onesb"], [kst])
            an["pst"] = (pst, kst)

        def a_rs():
            if nxt is None:
                return
            pst, kst = an["pst"]
            RS, kRS = ft.alloc()
            P.add("act", lambda: S_.activation(out=RS[:, :], in_=pst[:, :], func=AF.Ln, bias=EPS, scale=1.0 / 1024),
                  [kst], [kRS])
            P.add("act", lambda: S_.activation(out=RS[:, :], in_=RS[:, :], func=AF.Exp, scale=-0.5), [kRS], [kRS])
            an["rs"] = (RS, kRS)

        def a_ht():
            if nxt is None:
                return
            l2, HM2, blk2, xk2, hk2 = an["c"]
            RS, kRS = an["rs"]
            for kc in range(8):
                P.add("dve", lambda kc=kc: V_.scalar_tensor_tensor(
                    out=HM2[:, kc, :], in0=xT[:, kc, blk2], scalar=cols[:, l2, C_G + kc:C_G + kc + 1], in1=RS[:, :],
                    op0=ALU.mult, op1=ALU.mult), xk2(kc) + [kRS, "cols"], hk2(kc))
            ft.release(kRS)

        pF, kF = ps[7], ("ps", 7)
        for i, tt in enumerate(tts):
            pv, kv = misc.get()
            for kc in range(8):
                mm(pv[:, :], HM[:, kc, i * 128:(i + 1) * 128], WVv[:, kc, :], kc == 0, kc == 7, hmk(kc) + ["WVv"], [kv])
            for kc in range(8):
                mm(pF[:, i * 8:(i + 1) * 8], HM[:, kc, i * 128:(i + 1) * 128], WVf[:, kc, :], kc == 0, kc == 7,
                   hmk(kc) + ["WVf"], [kF])
            evac_copy(V[:, tt, :, 0:64], pv[:, :].rearrange("p (h d) -> p h d", h=8), [kv], [("V", tt)])
        FZ, kFZ = ft.alloc()
        P.add("dve", lambda: V_.tensor_tensor(out=FZ[:, 0:32], in0=pF[:, 0:32], in1=bfb[:, l, :], op=ALU.add),
              [kF, "bfb"], [kFZ])
        P.add("act", lambda: S_.activation(out=FZ[:, 0:32], in_=FZ[:, 0:32], func=AF.Exp, scale=-1.0), [kFZ], [kFZ])
        P.add("act", lambda: S_.activation(out=FZ[:, 0:32], in_=FZ[:, 0:32], func=AF.Ln, bias=1.0), [kFZ], [kFZ])
        cdma_w, cdma_r = [], []

        def flush_cdma(lst):
            while lst:
                it = lst.pop(0)
                if callable(it):
                    it()
                else:
                    P.add("sp", it[0], it[1], it[2], dma=True)

        def fpath_tail():
            pC, kC = ps[6], ("ps", 6)
            for i in range(4):
                for j in range(i + 1):
                    rhs = negtri_f if j == i else nonesf[:, :]
                    mm(pC[0:8, i * 128:(i + 1) * 128], FZ[:, j * 8:(j + 1) * 8], rhs, j == 0, j == i,
                       [kFZ, "cstf", "nonesf"], [kC])
            CC, kCC = ft.alloc()
            P.add("dve", lambda: V_.tensor_scalar(out=CC[0:8, :], in0=pC[0:8, :], scalar1=CB[0:8, 0:1], scalar2=None,
                                                  op0=ALU.add), [kC, "CB"], [kCC])
            ft.release(kFZ)
            P.add("dve", lambda: V_.tensor_copy(out=CB[0:8, 0:1], in_=CC[0:8, 511:512]), [kCC], ["CB"])
            HI, kHI = bt.alloc()
            MID, kMID = bt.alloc()
            LO, kLO = bt.alloc()
            R1, kR1 = ft.alloc()
            R2, kR2 = ft.alloc()
            P.add("dve", lambda: V_.tensor_copy(out=HI[0:8, :], in_=CC[0:8, :]), [kCC], [kHI])
            P.add("dve", lambda: V_.tensor_tensor(out=R1[0:8, :], in0=CC[0:8, :], in1=HI[0:8, :], op=ALU.subtract),
                  [kCC, kHI], [kR1])
            P.add("dve", lambda: V_.tensor_copy(out=MID[0:8, :], in_=R1[0:8, :]), [kR1], [kMID])
            P.add("dve", lambda: V_.tensor_tensor(out=R2[0:8, :], in0=R1[0:8, :], in1=MID[0:8, :], op=ALU.subtract),
                  [kR1, kMID], [kR2])
            P.add("dve", lambda: V_.tensor_copy(out=LO[0:8, :], in_=R2[0:8, :]), [kR2], [kLO])
            ft.release(kCC)
            ft.release(kR1)
            ft.release(kR2)
            sc = scr_d[b % 2]
            for j, (tile_, key_) in enumerate(((HI, kHI), (MID, kMID), (LO, kLO))):
                cdma_w.append((lambda j=j, tile_=tile_: nc.sync.dma_start(out=sc[:, j, :], in_=tile_[0:8, :]),
                               [key_], [("scr", b % 2, j)]))
            cdma_w.append(lambda: (bt.release(kHI), bt.release(kMID), bt.release(kLO)))
            scr_keys = [("scr", b % 2, j) for j in range(3)]
            cdma_r.append((lambda: nc.sync.dma_start(out=QA[64:67, :, :], in_=sc.rearrange("h j t -> j h t")),
                           scr_keys + [("QAc",)], [("QAaug",)]))
            cdma_r.append((lambda: nc.sync.dma_start(out=KA[67:70, :, blk], in_=sc.rearrange("h j t -> j h t")),
                           scr_keys + [("KAc",)], [("KAaug", b)]))

        def qk_s1(which, j):
            pa, ka = proj_chunk()
            SQ, kSQ = bt.alloc()
            P.add("act", lambda: S_.activation(out=SQ[:, :], in_=pa[:, :], func=AF.Square), [ka], [kSQ])
            return (which, j, pa, ka, SQ, kSQ)

        def qk_s2(st_):
            which, j, QR, kQR, SQ, kSQ = st_
            pm, km = stat.get()
            mm(pm[:, :], bdiag_b, SQ[:, :], True, True, [kSQ, "cstb"], [km])
            RQ, kRQ = ft.alloc()
            P.add("act", lambda: S_.activation(out=RQ[:, :], in_=pm[:, :], func=AF.Ln, bias=EPS, scale=1.0 / 64),
                  [km], [kRQ])
            P.add("act", lambda: S_.activation(out=RQ[:, :], in_=RQ[:, :], func=AF.Exp, scale=-0.5), [kRQ], [kRQ])
            for par in (0, 1):
                h = 2 * j + par
                r0 = 64 * par
                if which == "q":
                    dst = QA[0:64, h, :]
                    wkey = ("QA", h)
                    gc = DC[r0:r0 + 64, l, D_GQ8 + j:D_GQ8 + j + 1]
                    gk = ("DC", D_GQ8)
                else:
                    dst = KA[0:64, h, blk]
                    wkey = ("KA", h, b)
                    gc = cols[r0:r0 + 64, l, C_GK + j:C_GK + j + 1]
                    gk = "cols"
                P.add("dve", lambda dst=dst, gc=gc, r0=r0: V_.scalar_tensor_tensor(
                    out=dst, in0=QR[r0:r0 + 64, :], scalar=gc, in1=RQ[r0:r0 + 64, :], op0=ALU.mult, op1=ALU.mult),
                    [kQR, kRQ, gk], [wkey])
            ft.release(kRQ)
            bt.release(kSQ)

        pend = None
        for which, j in [("k", j) for j in range(4)] + [("q", j) for j in range(4)]:
            cur = qk_s1(which, j)
            if pend is not None:
                qk_s2(pend)
            pend = cur
            if which == "k" and j == 1:
                fpath_tail()
            drain_conv(1)

        flush_deferred()
        flush_cdma(cdma_w)
        for i in (0, 1):
            pz, kz = proj_chunk()
            if pend is not None:
                qk_s2(pend)
                pend = None
            tanh_gate(pz, kz, ZC[:, i, :], [("ZC", i)])
            drain_conv(1)
        for i in (0, 1):
            pc, kc_ = proj_chunk()
            px, kx = proj_chunk()
            CX, kCX = ft.alloc()
            P.add("act", lambda pc=pc, CX=CX: S_.copy(out=CX[:, :], in_=pc[:, :]), [kc_], [kCX])
            P.add("dve", lambda i=i, px=px, CX=CX: V_.tensor_tensor(out=S0[:, i, 2:514], in0=CX[:, :], in1=px[:, :], op=ALU.mult),
                  [kCX, kx], [("S0", i)])
            ft.release(kCX)
            drain_conv(2)
        flush_cdma(cdma_r)
        for i in (0, 1):
            pb, kb_ = proj_chunk()
            pz, kz = proj_chunk()
            T2, kT2 = ft.alloc()
            tanh_gate(pz, kz, T2[:, :], [kT2])
            P.add("dve", lambda i=i, pb=pb, T2=T2: V_.tensor_tensor(out=GS[:, i, :], in0=T2[:, :], in1=pb[:, :], op=ALU.mult),
                  [kT2, kb_], [("GS", i)])
            ft.release(kT2)
            drain_conv(2)
        a_sq()
        for j in range(4):
            pz, kz = proj_chunk()
            tanh_gate(pz, kz, ZA[:, j, :], [("ZA", j)])
            a_sq_part(j)
            drain_conv(1)
        a_mm()

        ln = {}

        def ln1():
            pmu, kmu = stat.get()
            pm2, km2 = stat.get()
            for i in (0, 1):
                Tb, kTb = bt.alloc()
                Tq, kTq = bt.alloc()
                P.add("act", lambda i=i, Tb=Tb: S_.copy(out=Tb[:, :], in_=U1[i][0][:, :]), [U1[i][1]], [kTb])
                P.add("act", lambda i=i, Tq=Tq: S_.activation(out=Tq[:, :], in_=U1[i][0][:, :], func=AF.Square), [U1[i][1]], [kTq])
                mm(pmu[:, :], onesb[:, :], Tb[:, :], i == 0, i == 1, [kTb, "onesb"], [kmu])
                mm(pm2[:, :], onesb[:, :], Tq[:, :], i == 0, i == 1, [kTq, "onesb"], [km2])
                bt.release(kTb)
                bt.release(kTq)
            ln["st"] = (pmu, kmu, pm2, km2)

        def ln2a():
            pmu, kmu, pm2, km2 = ln["st"]
            MEAN, kME = ft.alloc()
            MSQ, kMS = ft.alloc()
            if b >= 2:
                P.add("dve", lambda: V_.tensor_scalar(out=MEAN[:, :], in0=pmu[:, :], scalar1=1.0 / 256, scalar2=None, op0=ALU.mult),
                      [kmu], [kME])
                P.add("dve", lambda: V_.tensor_tensor(out=MSQ[:, :], in0=MEAN[:, :], in1=MEAN[:, :], op=ALU.mult), [kME], [kMS])
            else:
                P.add("act", lambda: S_.activation(out=MEAN[:, :], in_=pmu[:, :], func=AF.Identity, scale=1.0 / 256), [kmu], [kME])
                P.add("act", lambda: S_.activation(out=MSQ[:, :], in_=pmu[:, :], func=AF.Square, scale=1.0 / 256), [kmu], [kMS])
            ln["m"] = (MEAN, kME, MSQ, kMS)

        def ln2b():
            pmu, kmu, pm2, km2 = ln["st"]
            MEAN, kME, MSQ, kMS = ln["m"]
            VAR, kVA = MSQ, kMS
            P.add("dve", lambda: V_.scalar_tensor_tensor(out=VAR[:, :], in0=pm2[:, :], scalar=1.0 / 256, in1=MSQ[:, :],
                                                         op0=ALU.mult, op1=ALU.subtract), [km2, kMS], [kVA])
            P.add("dve", lambda: V_.tensor_scalar(out=VAR[:, :], in0=VAR[:, :], scalar1=0.0, scalar2=EPS,
                                                  op0=ALU.max, op1=ALU.add), [kVA], [kVA])
            ln["v"] = (VAR, kVA)

        def ln2c():
            VAR, kVA = ln["v"]
            P.add("act", lambda: S_.activation(out=VAR[:, :], in_=VAR[:, :], func=AF.Ln), [kVA], [kVA])
            P.add("act", lambda: S_.activation(out=VAR[:, :], in_=VAR[:, :], func=AF.Exp, scale=-0.5), [kVA], [kVA])

        def ln2d():
            MEAN, kME, MSQ, kMS = ln["m"]
            VAR, kVA = ln["v"]
            for i in (0, 1):
                XN, kXN = U1[i]
                P.add("dve", lambda XN=XN: V_.tensor_tensor(out=XN[:, :], in0=XN[:, :], in1=MEAN[:, :], op=ALU.subtract),
                      [kXN, kME], [kXN])
                P.add("dve", lambda XN=XN: V_.tensor_tensor(out=XN[:, :], in0=XN[:, :], in1=VAR[:, :], op=ALU.mult),
                      [kXN, kVA], [kXN])
            ft.release(kME)
            ft.release(kVA)

        def ln2e():
            ys = []
            for i in (0, 1):
                XN, kXN = U1[i]
                Y, kY = ft.alloc()
                P.add("dve", lambda XN=XN, Y=Y, i=i: V_.tensor_scalar(
                    out=Y[:, :], in0=XN[:, :], scalar1=cols[:, l, C_LNG + i:C_LNG + i + 1],
                    scalar2=cols[:, l, C_LNB + i:C_LNB + i + 1], op0=ALU.mult, op1=ALU.add), [kXN, "cols"], [kY])
                ys.append((Y, kY))
            ln["y"] = ys

        def sw(i, st_):
            XN, kXN = U1[i]
            Y, kY = ln["y"][i]
            if st_ == 0:
                P.add("act", lambda: S_.activation(out=XN[:, :], in_=Y[:, :], func=AF.Exp, scale=-1.0), [kY, kXN], [kXN])
            elif st_ == 1:
                P.add("act", lambda: S_.activation(out=XN[:, :], in_=XN[:, :], func=AF.Ln, bias=1.0), [kXN], [kXN])
            else:
                P.add("act", lambda: S_.activation(out=XN[:, :], in_=XN[:, :], func=AF.Exp, scale=-1.0), [kXN], [kXN])

        def ln2f():
            pass

        def ln2f_silu():
            SL = []
            for i in (0, 1):
                XN, kXN = U1[i]
                SLt, kSL = bt.alloc()
                P.add("act", lambda XN=XN, SLt=SLt, i=i: S_.activation(
                    out=SLt[:, :], in_=XN[:, :], func=AF.Silu, bias=cols[:, l, C_LNB + i:C_LNB + i + 1],
                    scale=cols[:, l, C_LNG + i:C_LNG + i + 1]), [kXN, "cols"], [kSL])
                ft.release(kXN)
                SL.append((SLt, kSL))
            ln["SL"] = SL

        def pf_cf0_t():
            if nxt is not None:
                emit_cf_pair(0, 1 - p, stat)

        def pf_cf1_t():
            if nxt is not None:
                emit_cf_pair(1, 1 - p, stat)

        def ln2g():
            SL = []
            for i in (0, 1):
                XN, kXN = U1[i]
                Y, kY = ln["y"][i]
                SLt, kSL = bt.alloc()
                P.add("dve", lambda XN=XN, Y=Y, SLt=SLt: V_.tensor_tensor(out=SLt[:, :], in0=Y[:, :], in1=XN[:, :], op=ALU.mult),
                      [kXN, kY], [kSL])
                ft.release(kY)
                ft.release(kXN)
                SL.append((SLt, kSL))
            ln["SL"] = SL

        def ln3a():
            SL = ln["SL"]
            pps = []
            for co in (0, 1):
                ppo, kpo = stat.get()
                for ci in (0, 1):
                    mm(ppo[:, :], PW[:, ci, co * 128:(co + 1) * 128], SL[ci][0][:, :], ci == 0, ci == 1, [SL[ci][1], "PW"], [kpo])
                pps.append((ppo, kpo))
            for i in (0, 1):
                bt.release(SL[i][1])
            ln["pp"] = pps

        def ln3b():
            for co in (0, 1):
                ppo, kpo = ln["pp"][co]
                P.add("dve", lambda co=co, ppo=ppo: V_.scalar_tensor_tensor(
                    out=HM[:, co, :], in0=ppo[:, :], scalar=1.0, in1=ZC[:, co, :], op0=ALU.mult, op1=ALU.mult),
                    [kpo, ("ZC", co)], hmk(co))

        def sc_stage():
            for i in (0, 1):
                SA, kSA = ft.alloc()
                SB, kSB = ft.alloc()
                w = lambda k: DC[:, l, D_SCWH + i * 3 + k:D_SCWH + i * 3 + k + 1]
                rk = [("S0", i), ("S0h", i), ("DC", D_SCWH)]
                P.add("dve", lambda i=i, SA=SA, w0=w(0): V_.tensor_scalar(out=SA[:, :], in0=S0[:, i, 0:512], scalar1=w0, scalar2=None,
                                                                        op0=ALU.mult), rk, [kSA])
                P.add("dve", lambda i=i, SA=SA, SB=SB, w1=w(1): V_.scalar_tensor_tensor(
                    out=SB[:, :], in0=S0[:, i, 1:513], scalar=w1, in1=SA[:, :], op0=ALU.mult, op1=ALU.add), rk + [kSA], [kSB])
                P.add("dve", lambda i=i, SA=SA, SB=SB, w2=w(2): V_.scalar_tensor_tensor(
                    out=SA[:, :], in0=S0[:, i, 2:514], scalar=w2, in1=SB[:, :], op0=ALU.mult, op1=ALU.add), rk + [kSB], [kSA])
                P.add("dve", lambda i=i, SA=SA: V_.tensor_tensor(out=HM[:, 2 + i, :], in0=SA[:, :], in1=GS[:, i, :], op=ALU.mult),
                      [kSA, ("GS", i)], hmk(2 + i))
                P.add("dve", lambda i=i: V_.tensor_copy(out=S0[:, i, 0:2], in_=S0[:, i, 512:514]), [("S0", i)], [("S0h", i)])
                ft.release(kSA)
                ft.release(kSB)

        def ln1_act():
            tl = []
            for i in (0, 1):
                _, kT = ft.alloc()
                Tb = FTb16[:, kT[1], 0:512]
                Tq = FTb16[:, kT[1], 512:1024]
                if b >= 2:
                    P.add("dve", lambda i=i, Tb=Tb: V_.tensor_copy(out=Tb, in_=U1[i][0][:, :]), [U1[i][1]], [kT])
                    P.add("dve", lambda i=i, Tq=Tq: V_.tensor_tensor(out=Tq, in0=U1[i][0][:, :], in1=U1[i][0][:, :], op=ALU.mult),
                          [U1[i][1], kT], [kT])
                else:
                    P.add("act", lambda i=i, Tb=Tb: S_.copy(out=Tb, in_=U1[i][0][:, :]), [U1[i][1]], [kT])
                    P.add("act", lambda i=i, Tq=Tq: S_.activation(out=Tq, in_=U1[i][0][:, :], func=AF.Square), [U1[i][1], kT], [kT])
                tl.append((Tb, Tq, kT))
            ln["tl"] = tl

        def ln1_mm():
            pmu, kmu = stat.get()
            pm2, km2 = stat.get()
            for i, (Tb, Tq, kT) in enumerate(ln["tl"]):
                mm(pmu[:, :], onesb[:, :], Tb, i == 0, i == 1, [kT, "onesb"], [kmu])
                mm(pm2[:, :], onesb[:, :], Tq, i == 0, i == 1, [kT, "onesb"], [km2])
                ft.release(kT)
            ln["st"] = (pmu, kmu, pm2, km2)

        def taps6():
            drain_conv(6)

        def taps_all():
            drain_conv(1000)

        def pf_halo():
            if nxt is not None and nxt[1] == 0:
                P.add("pool", lambda: G_.memset(U0[:, :, 0:30], 0.0), [], [("U0h", 0), ("U0h", 1)])

        pfs = {}
        LN2 = math.log(2.0)

        def pf_cf(i, st_):
            if nxt is None:
                return
            if st_ == 0:
                pa, ka = proj_chunk(1 - p, stat)
                pg, kg = proj_chunk(1 - p, stat)
                SG, kSG = ft.alloc()
                pfs[i] = (SG, kSG)
                P.add("act", lambda: S_.activation(out=SG[:, :], in_=pg[:, :], func=AF.Exp, scale=-1.0), [kg], [kSG])
                P.add("act", lambda: S_.copy(out=U0[:, i, 30:542], in_=pa[:, :]), [ka], [("U0", i)])
            elif st_ == 1:
                SG, kSG = pfs[i]
                P.add("act", lambda: S_.activation(out=SG[:, :], in_=SG[:, :], func=AF.Ln, bias=1.0), [kSG], [kSG])
            elif st_ == 2:
                SG, kSG = pfs[i]
                P.add("act", lambda: S_.activation(out=SG[:, :], in_=SG[:, :], func=AF.Exp, scale=-1.0, bias=LN2), [kSG], [kSG])
            else:
                SG, kSG = pfs.pop(i)
                P.add("dve", lambda: V_.tensor_tensor(out=U0[:, i, 30:542], in0=U0[:, i, 30:542], in1=SG[:, :], op=ALU.mult),
                      [("U0", i), kSG], [("U0", i)])
                ft.release(kSG)

        def pf_conv():
            if nxt is not None:
                g_, u_ = make_conv(nxt[0])
                cv["key"] = (s, nxt[0], nxt[1])
                cv["gen"] = g_
                cv["U1"] = u_

        NSL = 32
        def L(f, *a):
            return lambda: f(*a)

        if b >= 2:
            slot_fns = {0: [sc_stage, taps6], 1: [a_rs, taps6], 2: [taps6], 3: [taps6], 4: [taps6], 5: [taps6],
                        6: [a_ht, taps6], 7: [taps_all], 8: [ln1_act], 10: [ln1_mm], 12: [ln2a], 13: [ln2b], 14: [ln2c],
                        15: [ln2d], 16: [ln2e, pf_halo], 17: [L(pf_cf, 0, 0)], 18: [L(sw, 0, 0), L(pf_cf, 0, 1)],
                        19: [L(sw, 1, 0), L(pf_cf, 0, 2)], 20: [L(sw, 0, 1), L(pf_cf, 0, 3)], 21: [L(sw, 1, 1), L(pf_cf, 1, 0)],
                        22: [L(sw, 0, 2), L(pf_cf, 1, 1)], 23: [L(sw, 1, 2), L(pf_cf, 1, 2)], 24: [ln2g, L(pf_cf, 1, 3)],
                        25: [ln3a], 26: [pf_conv], 27: [ln3b]}
        else:
            slot_fns = {0: [sc_stage, taps6], 1: [a_rs, taps6], 2: [taps6], 3: [taps6], 4: [taps6], 5: [taps6],
                        6: [a_ht, taps6], 7: [taps_all], 8: [ln1_act], 10: [ln1_mm], 12: [ln2a], 13: [ln2b], 15: [ln2c],
                        17: [ln2d], 19: [pf_halo], 20: [pf_cf0_t, ln2f_silu], 22: [pf_cf1_t], 25: [ln3a],
                        27: [ln3b], 28: [pf_conv]}

        steps = [(h, kb) for h in range(8) for kb in range(4 * b + 4)]
        nsteps = len(steps)
        stb = [(ps[i], ("ps", i)) for i in range(4)]
        pts = {}
        nrm = {}

        def emit_score(idx):
            h, kb = steps[idx]
            st, kst_ = stb[idx % 4]
            i = kb - 4 * b
            n0 = 128 * i if i > 0 else 0
            rd = [("KA", h, kb // 4), ("KAaug", kb // 4), ("KAc",), ("QA", h), ("QAaug",), ("QAc",)]
            mm(st[:, n0:512], KA[0:70, h, kb * 128:(kb + 1) * 128], QA[0:70, h, n0:512], True, i < 0, rd, [kst_])
            if i >= 0:
                mm(st[:, n0:n0 + 128], ident_b, maskneg_b, False, True, ["cstb"], [kst_])
            PTt, kPT = bt.alloc()
            P.add("act", lambda: S_.activation(out=PTt[:, n0:512], in_=st[:, n0:512], func=AF.Exp), [kst_], [kPT])
            pts[idx] = (PTt, kPT, n0)

        def n1(h):
            po, kpo = ps[4 + h % 2], ("ps", 4 + h % 2)
            if h % 2 == 0:
                NUM, kNU = ft.alloc()
                DEN, kDE = ft.alloc()
                nrm["pair"] = (NUM, kNU, DEN, kDE)
                if b >= 2:
                    P.add("dve", lambda: V_.tensor_copy(out=NUM[0:64, :], in_=po[0:64, :]), [kpo], [kNU])
                else:
                    P.add("act", lambda: S_.copy(out=NUM[0:64, :], in_=po[0:64, :]), [kpo], [kNU])
                P.add("act", lambda: S_.activation(out=DEN[0:64, :], in_=po[64:128, :], func=AF.Ln), [kpo], [kDE])
            else:
                NUM, kNU, DEN, kDE = nrm["pair"]
                P.add("dve", lambda: V_.tensor_copy(out=NUM[64:128, :], in_=po[0:64, :]), [kpo], [kNU])
                P.add("act", lambda: S_.activation(out=DEN[64:128, :], in_=po[64:128, :], func=AF.Ln), [kpo], [kDE])
                nrm[h // 2] = (NUM, kNU, DEN, kDE)

        def n2(j):
            NUM, kNU, DEN, kDE = nrm[j]
            P.add("act", lambda: S_.activation(out=DEN[:, :], in_=DEN[:, :], func=AF.Exp, scale=-1.0), [kDE], [kDE])
            P.add("dve", lambda: V_.tensor_tensor(out=NUM[:, :], in0=NUM[:, :], in1=DEN[:, :], op=ALU.mult), [kNU, kDE], [kNU])

        def n3(j):
            NUM, kNU, DEN, kDE = nrm.pop(j)
            P.add("dve", lambda: V_.tensor_tensor(out=HM[:, 4 + j, :], in0=NUM[:, :], in1=ZA[:, j, :], op=ALU.mult),
                  [kNU, ("ZA", j)], [("HM", p, 4 + j, 0), ("HM", p, 4 + j, 1)])
            ft.release(kNU)
            ft.release(kDE)

        def emit_pv(idx):
            h, kb = steps[idx]
            PTt, kPT, n0 = pts.pop(idx)
            po, kpo = ps[4 + h % 2], ("ps", 4 + h % 2)
            last = kb == 4 * b + 3
            mm(po[:, n0:512], V[:, kb, h, :], PTt[:, n0:512], kb == 0, last, [("V", kb), kPT], [kpo])
            bt.release(kPT)
            if last:
                if b >= 2:
                    if h % 2 == 0 and h >= 2:
                        n2(h // 2 - 1)
                        n3(h // 2 - 1)
                    n1(h)
                else:
                    n1(h)
                    if h % 2 == 0 and h >= 2:
                        n2(h // 2 - 1)
                    if h % 2 == 1 and h >= 3:
                        n3(h // 2 - 1)

        fired = set()
        for idx in range(nsteps + LA):
            if idx < nsteps:
                emit_score(idx)
            j = idx - LA
            if j >= 0:
                emit_pv(j)
                k = ((j + 1) * NSL) // nsteps - 1
                for kk in range(k + 1):
                    if kk not in fired and (kk + 1) * nsteps <= (j + 1) * NSL:
                        fired.add(kk)
                        late_step()
                        for fn_ in slot_fns.get(kk, ()):
                            fn_()
        n2(3)
        n3(3)
        for kk in range(NSL):
            if kk not in fired:
                fired.add(kk)
                for fn_ in slot_fns.get(kk, ()):
                    fn_()

        for m in range(8):
            pa, ka = proj_chunk()
            P.add("dve", lambda m=m, pa=pa: V_.tensor_tensor(out=xT[:, m, blk], in0=xT[:, m, blk], in1=pa[:, :], op=ALU.add),
                  xk(m) + [ka], xk(m))
            if cv["key"] is not None:
                drain(cv["gen"], 2)

        if last_layer:
            for i, tt in enumerate(tts):
                XO = XS[i % 2]
                if i >= 2:
                    flush_deferred(1)
                for half in (0, 1):
                    m0 = 4 * half
                    pt_, kpt = misc.get()
                    for q in range(4):
                        P.add("pe", lambda q=q, pt_=pt_, m0=m0, tt=tt: T_.transpose(
                            pt_[:, q * 128:(q + 1) * 128], xT[:, m0 + q, tt * 128:(tt + 1) * 128], ident_f),
                            [("xT", m0 + q, tt), "cstf"], [kpt])
                    evac_copy(XO[:, m0 * 128:(m0 + 4) * 128], pt_[:, :], [kpt], XSK[i % 2][4 * half * 2:(4 * half + 4) * 2] if False else [("XO", p, i % 2, half)] + XSK[i % 2])
                deferred.append((lambda XO=XO, tt=tt, s=s: nc.sync.dma_start(out=y_d[s, tt * 128:(tt + 1) * 128, :], in_=XO),
                                 [("XO", p, i % 2, 0), ("XO", p, i % 2, 1)] + XSK[i % 2], [("y", s, tt)] + XSK[i % 2]))

    for s in range(NSEQ):
        flush_deferred()
        for tt in range(NTT):
            XI = XSs[0][tt % 2]
            XIK = XSKs[0][tt % 2]
            P.add("sp", lambda XI=XI, tt=tt, s=s: nc.sync.dma_start(out=XI, in_=x_d[s, tt * 128:(tt + 1) * 128, :]),
                  [], XIK, dma=True)
            for half in (0, 1):
                m0 = 4 * half
                pt_, kpt = misc.get()
                for q in range(4):
                    P.add("pe", lambda q=q, pt_=pt_, XI=XI, m0=m0: T_.transpose(
                        pt_[:, q * 128:(q + 1) * 128], XI[:, (m0 + q) * 128:(m0 + q + 1) * 128], ident_f),
                        XIK + ["cstf"], [kpt])
                evac_copy(xT[:, m0:m0 + 4, tt * 128:(tt + 1) * 128], pt_[:, :].rearrange("p (a b) -> p a b", a=4),
                          [kpt], [("xT", m0 + q, tt) for q in range(4)])
        seqblocks = [(l, b) for l in range(NL) for b in range(NBLK)]
        emit_A(0, 0, 0)
        for n_, (l, b) in enumerate(seqblocks):
            if b == 0:
                P.add("sp", lambda l=l: nc.sync.dma_start(out=WVv.reshape([128, 4096])[:, :], in_=wv_b[l]),
                      [("wvsc", l, k) for k in range(4)], ["WVv"], dma=True)
                P.add("sp", lambda l=l: nc.sync.dma_start(out=PW.reshape([128, 512])[:, :], in_=pw_b[l]), [("pwsc", l)], ["PW"], dma=True)
                P.add("sp", lambda l=l: nc.sync.dma_start(out=WFs[:, :], in_=wf_d[l]), [], ["WFs"], dma=True)
                P.add("dve", lambda: V_.tensor_copy(out=WVf.reshape([128, 64])[:, :], in_=WFs[:, :]), ["WFs"], ["WVf"])
                P.add("pool", lambda: G_.memset(CB[:, :], 0.0), [], ["CB"])
                P.add("pool", lambda: G_.memset(S0[:, :, 0:2], 0.0), [], [("S0h", 0), ("S0h", 1)])
            nxt = seqblocks[n_ + 1] if n_ + 1 < len(seqblocks) else None
            if s == 0 and l == 0 and b <= NBLK - 2:
                lstate["active"] = True
            elif not lstate["done"]:
                lstate["active"] = True
                late_flush()
            block(s, l, b, l == NL - 1, n_ % 2, nxt)
            lstate["active"] = False

    flush_deferred()
    ykeys = [("y", s, tt) for s in range(NSEQ) for tt in range(NTT)]
    P.add("sp", lambda: None, ykeys, [])
    return nc, P


def prep_shared(inputs, NL):
    w_in = np.asarray(inputs["w_in"], np.float32)[:NL]
    w_out = np.asarray(inputs["w_out"], np.float32)[:NL]
    order = chunk_order()
    wfm = np.empty((NL, 26, 128, 8, 128), np.float32)
    for c, (kind, idx) in enumerate(order):
        c0 = _COL0[kind] + idx * 128
        blk = w_in[:, :, c0:c0 + 128].reshape(NL, 8, 128, 128)
        wfm[:, c] = blk.transpose(0, 2, 1, 3)
    wv = np.ascontiguousarray(w_in[:, :, 2816:3328].reshape(NL, 8, 128, 512).transpose(0, 2, 1, 3))
    wf = np.ascontiguousarray(w_in[:, :, 3840:3848].reshape(NL, 8, 128, 8).transpose(0, 2, 1, 3)).reshape(NL, 128, 64)
    wo = w_out.reshape(NL, 8, 128, 8, 128).transpose(0, 3, 2, 1, 4)
    pw = np.asarray(inputs["cf_pw"], np.float32)[:NL].reshape(NL, 2, 128, 256).transpose(0, 2, 1, 3)
    cols = np.zeros((128, NL, NCOL), np.float32)
    g = np.asarray(inputs["norm_g"], np.float32)[:NL]
    cols[:, :, C_G:C_G + 8] = g.reshape(NL, 8, 128).transpose(2, 0, 1)
    for name, c0 in (("cf_dw_b", C_CFB), ("cf_ln_g", C_LNG), ("cf_ln_b", C_LNB)):
        a = np.asarray(inputs[name], np.float32)[:NL]
        cols[:, :, c0:c0 + 2] = a.reshape(NL, 2, 128).transpose(2, 0, 1)
    cfw = np.asarray(inputs["cf_dw"], np.float32)[:NL]
    cols[:, :, C_CFW:C_CFW + 62] = cfw.reshape(NL, 31, 2, 128).transpose(3, 0, 2, 1).reshape(128, NL, 62)
    scw = np.asarray(inputs["sc_dw"], np.float32)[:NL]
    cols[:, :, C_SCW:C_SCW + 6] = scw.reshape(NL, 3, 2, 128).transpose(3, 0, 2, 1).reshape(128, NL, 6)
    for name, c0 in (("q_norm_g", C_GQ), ("k_norm_g", C_GK)):
        a = np.asarray(inputs[name], np.float32)[:NL]
        cols[:, :, c0:c0 + 4] = a.reshape(NL, 4, 128).transpose(2, 0, 1)
    bf = np.asarray(inputs["b_f"], np.float32)[:NL]
    bfb = np.broadcast_to(np.tile(bf, (1, 4))[None], (128, NL, 32)).copy()
    cst = np.zeros((128, 512), np.float32)
    ii = np.arange(128)
    cst[:, 0:128] = np.eye(128, dtype=np.float32)
    cst[:, 128:256] = -(ii[:, None] <= ii[None, :]).astype(np.float32)
    cst[:, 256:384] = (ii[:, None] // 64 == ii[None, :] // 64).astype(np.float32)
    cst[:, 384:512] = np.where(ii[:, None] <= ii[None, :], 0.0, -30000.0).astype(np.float32)
    return dict(wfm=np.ascontiguousarray(wfm), wv=wv, wf=wf, wo=np.ascontiguousarray(wo), pw=np.ascontiguousarray(pw),
                cols=cols, bfb=bfb, cst=cst)


_CACHE = {}


def run(inputs, NSEQ, NBLK, NL, n_cores, x_shards):
    key = (NSEQ, NBLK, NL)
    if key not in _CACHE:
        nc, P = build(NSEQ, NBLK, NL)
        with ExitStack() as stack:
            info = P.finalize(stack)
        _CACHE[key] = (nc, info)
    nc, info = _CACHE[key]
    shared = prep_shared(inputs, NL)
    in_maps = [dict(shared, x=np.ascontiguousarray(xs)) for xs in x_shards]
    res = run_bass_kernel_spmd(nc, in_maps, core_ids=list(range(n_cores)))
    return [r["y"] for r in res.results], info


def kernel(**inputs):
    x = np.asarray(inputs["x"], np.float32)
    n = 8
    shards = [x[2 * c:2 * c + 2] for c in range(n)]
    outs, _ = run(inputs, 2, 4, 2, n, shards)
    return np.concatenate(outs, axis=0).astype(np.float32)
```

```python
import numpy as np
from contextlib import ExitStack
import concourse.bass as bass
import concourse.mybir as mybir
from concourse.bass_utils import run_bass_kernel_spmd

F32 = mybir.dt.float32
BF16 = mybir.dt.bfloat16
AF = mybir.ActivationFunctionType
ALU = mybir.AluOpType

EPS = 1e-6
C_G, C_CFB, C_LNG, C_LNB, C_CFW, C_SCW, C_GQ, C_GK, NCOL = 0, 8, 10, 12, 14, 76, 82, 86, 90
D_CFWH, D_SCWH, D_GQ8, D_LNGH, D_LNBH, ND = 0, 62, 68, 72, 74, 76
NSLOT = 3
NFT = 8
NBT = 6
NDMASEM = 25
LA = 3


class Prog:
    ENGS = ["pe", "act", "dve", "pool", "sp"]

    def __init__(self, nc):
        self.nc = nc
        self.eng = {"pe": nc.tensor, "act": nc.scalar, "dve": nc.vector, "pool": nc.gpsimd, "sp": nc.sync}
        self.ops = []

    def add(self, eng, fn, reads=(), writes=(), dma=False):
        self.ops.append(dict(eng=eng, fn=fn, reads=list(reads), writes=list(writes), dma=dma))

    def finalize(self, stack):
        ops = self.ops
        last_w, readers = {}, {}
        for i, op in enumerate(ops):
            deps = set()
            for k in op["reads"]:
                if k in last_w:
                    deps.add(last_w[k])
            for k in op["writes"]:
                if k in last_w:
                    deps.add(last_w[k])
                for r in readers.get(k, ()):
                    deps.add(r)
            deps.discard(i)
            keep = set()
            for d in deps:
                dop = ops[d]
                if (not dop["dma"]) and dop["eng"] == "pe" and op["eng"] == "pe" and not op["dma"]:
                    continue
                keep.add(d)
            latest = {}
            kept2 = set()
            for d in keep:
                dop = ops[d]
                if dop["dma"]:
                    kept2.add(d)
                else:
                    latest[dop["eng"]] = max(latest.get(dop["eng"], -1), d)
            kept2 |= set(latest.values())
            op["deps"] = kept2
            for k in op["reads"]:
                readers.setdefault(k, []).append(i)
            for k in op["writes"]:
                last_w[k] = i
                readers[k] = []
        targets = set()
        for op in ops:
            targets |= op["deps"]
        cnt = {e: 0 for e in self.ENGS}
        dma_use = [0] * NDMASEM
        rr = 0
        for i, op in enumerate(ops):
            if op["dma"]:
                s = rr % NDMASEM
                rr += 1
                dma_use[s] += 1
                op["sem"] = ("dma", s)
                op["val"] = 16 * dma_use[s]
                op["prev"] = 16 * (dma_use[s] - 1)
            elif i in targets:
                cnt[op["eng"]] += 1
                op["sem"] = ("eng", op["eng"])
                op["val"] = cnt[op["eng"]]
            else:
                op["sem"] = None
        sems = {}

        def sem(key):
            if key not in sems:
                sems[key] = stack.enter_context(self.nc.semaphore("s_%s_%s" % key))
            return sems[key]

        waited = {e: {} for e in self.ENGS}
        nwait = 0
        for op in ops:
            e = op["eng"]
            E = self.eng[e]
            need = {}
            for d in op["deps"]:
                dop = ops[d]
                key = dop["sem"]
                need[key] = max(need.get(key, 0), dop["val"])
            if op["dma"] and op["prev"] > 0:
                need[op["sem"]] = max(need.get(op["sem"], 0), op["prev"])
            for key, val in need.items():
                if waited[e].get(key, 0) < val:
                    E.wait_ge(sem(key), val)
                    waited[e][key] = val
                    nwait += 1
            ins = op["fn"]()
            if op["sem"] is not None and ins is not None:
                ins.then_inc(sem(op["sem"]), 16 if op["dma"] else 1)
        return dict(n_ops=len(ops), n_wait=nwait, cnt=cnt)


class TPool:
    def __init__(self, tiles, name):
        self.tiles = tiles
        self.name = name
        self.free = list(range(len(tiles)))

    def alloc(self):
        assert self.free, "temp pool %s exhausted" % self.name
        i = self.free.pop(0)
        return self.tiles[i], (self.name, i)

    def release(self, key):
        assert key[0] == self.name and key[1] not in self.free
        self.free.append(key[1])


class Rot:
    def __init__(self, items):
        self.items = items
        self.i = 0

    def get(self):
        it = self.items[self.i % len(self.items)]
        self.i += 1
        return it


def chunk_order():
    o = []
    o += [("cfa", 0), ("cfg", 0), ("cfa", 1), ("cfg", 1)]
    o += [("k", j) for j in range(4)]
    o += [("q", j) for j in range(4)]
    o += [("cfz", 0), ("cfz", 1)]
    o += [("scC", 0), ("scx", 0), ("scC", 1), ("scx", 1)]
    o += [("scB", 0), ("scz", 0), ("scB", 1), ("scz", 1)]
    o += [("az", j) for j in range(4)]
    return o


_COL0 = {"cfa": 0, "cfg": 256, "cfz": 512, "scB": 768, "scC": 1024, "scx": 1280, "scz": 1536,
         "q": 1792, "k": 2304, "v": 2816, "az": 3328, "f": 3840}


def build(NSEQ, NBLK, NL):
    S = NBLK * 512
    NTT = NBLK * 4
    nc = bass.Bass("TRN2", target_bir_lowering=False)

    def dram(name, shape, dtype=F32, kind="ExternalInput"):
        return nc.dram_tensor(name, shape, dtype, kind=kind).ap()

    x_d = dram("x", [NSEQ, S, 1024])
    wfm_d = dram("wfm", [NL, 26, 128, 8, 128])
    wv_d = dram("wv", [NL, 128, 8, 512])
    wf_d = dram("wf", [NL, 128, 64])
    wo_d = dram("wo", [NL, 8, 128, 8, 128])
    pw_d = dram("pw", [NL, 128, 2, 256])
    cols_d = dram("cols", [128, NL, NCOL])
    bfb_d = dram("bfb", [128, NL, 32])
    cst_d = dram("cst", [128, 512])
    y_d = dram("y", [NSEQ, S, 1024], kind="ExternalOutput")
    scr_d = dram("scr", [2, 8, 3, 512], BF16, kind="Internal")
    wfm_b = dram("wfm_b", [NL, 26, 128, 1024], BF16, kind="Internal")
    wo_b = dram("wo_b", [NL, 8, 128, 1024], BF16, kind="Internal")
    wv_b = dram("wv_b", [NL, 128, 4096], BF16, kind="Internal")
    pw_b = dram("pw_b", [NL, 128, 512], BF16, kind="Internal")

    def A(name, shape, dtype):
        return nc.alloc_sbuf_tensor("sb_" + name, shape, dtype)

    xT = A("xT", [128, 8, S], F32)
    KA = A("KA", [128, 8, S], BF16)
    V = A("V", [128, NTT, 8, 128], BF16)
    QA = A("QA", [128, 8, 512], BF16)
    HMs = [A("HM0", [128, 8, 512], BF16), A("HM1", [128, 8, 512], BF16)]
    HMfs = [h_.bitcast(F32).reshape([128, 2, 1024]) for h_ in HMs]
    XSs = [[hf_[:, 0, :], hf_[:, 1, :]] for hf_ in HMfs]
    XSKs = [[[("HM", p_, kc, hf) for kc in range(4) for hf in (0, 1)], [("HM", p_, kc, hf) for kc in range(4, 8) for hf in (0, 1)]]
            for p_ in (0, 1)]
    WR = [A("WR%d" % i, [128, 8, 128], BF16) for i in range(NSLOT)]
    WVv = A("WVv", [128, 8, 512], BF16)
    WVf = A("WVf", [128, 8, 8], BF16)
    WFs = A("WFs", [128, 64], F32)
    PW = A("PW", [128, 2, 256], BF16)
    U0 = A("U0", [128, 2, 542], BF16)
    S0 = A("S0", [128, 2, 514], BF16)
    ZC = A("ZC", [128, 2, 512], BF16)
    GS = A("GS", [128, 2, 512], BF16)
    ZA = A("ZA", [128, 4, 512], BF16)
    FTbig = A("FT", [128, NFT, 512], F32)
    ft = TPool([FTbig[:, i, :] for i in range(NFT)], "FT")
    FTflat = FTbig.reshape([128, NFT // 2, 1024])
    FTb16 = FTbig.bitcast(BF16)
    bt = TPool([A("BT%d" % i, [128, 512], BF16) for i in range(NBT)], "BT")
    cols = A("cols", [128, NL, NCOL], F32)
    DC = A("DC", [128, NL, ND], F32)
    bfb = A("bfb", [128, NL, 32], F32)
    cstf = A("cstf", [128, 512], F32)
    cstb = A("cstb", [128, 512], BF16)
    onesb = A("onesb", [128, 128], BF16)
    nonesf = A("nonesf", [128, 128], F32)
    CB = A("CB", [128, 2], F32)
    ps = [nc.alloc_psum_tensor("ps%d" % i, [128, 512], F32) for i in range(8)]
    acc = Rot([(ps[i], ("ps", i)) for i in range(4)])
    misc = Rot([(ps[i], ("ps", i)) for i in (4, 5)])
    stat = Rot([(ps[i], ("ps", i)) for i in (6, 7)])

    ident_f = cstf[:, 0:128]
    negtri_f = cstf[:, 128:256]
    ident_b = cstb[:, 0:128]
    bdiag_b = cstb[:, 256:384]
    maskneg_b = cstb[:, 384:512]

    P = Prog(nc)
    V_ = nc.vector
    G_ = nc.gpsimd
    S_ = nc.scalar
    T_ = nc.tensor

    def mm(out, lhsT, rhs, start, stop, reads, writes):
        P.add("pe", lambda: T_.matmul(out, lhsT, rhs, start=start, stop=stop), reads, writes)

    hctx = dict(p=0)

    def hmk(kc):
        return [("HM", hctx["p"], kc, 0), ("HM", hctx["p"], kc, 1)]

    P.add("sp", lambda: nc.sync.dma_start(out=cols[:, :, :], in_=cols_d), [], ["cols"], dma=True)
    P.add("sp", lambda: nc.sync.dma_start(out=bfb[:, :, :], in_=bfb_d), [], ["bfb"], dma=True)
    P.add("sp", lambda: nc.sync.dma_start(out=cstf[:, :], in_=cst_d), [], ["cstf"], dma=True)
    P.add("dve", lambda: V_.tensor_copy(out=cstb[:, :], in_=cstf[:, :]), ["cstf"], ["cstb"])
    P.add("pool", lambda: G_.memset(onesb[:, :], 1.0), [], ["onesb"])
    P.add("pool", lambda: G_.memset(nonesf[:, :], -1.0), [], ["nonesf"])
    P.add("pool", lambda: G_.memset(V[:, :, :, :], 1.0), [], [("V", tt) for tt in range(NTT)])
    P.add("pool", lambda: G_.memset(QA[64:70, :, :], -1.0), [], [("QAc",)])
    P.add("pool", lambda: G_.memset(KA[64:70, :, :], 1.0), [], [("KAc",)])
    for (dst, src, n, sc) in ((D_CFWH, C_CFW, 62, 0.5), (D_SCWH, C_SCW, 6, 1.0), (D_GQ8, C_GQ, 4, 0.125),
                              (D_LNGH, C_LNG, 2, 0.5), (D_LNBH, C_LNB, 2, 0.5)):
        P.add("dve", lambda dst=dst, src=src, n=n, sc=sc: V_.tensor_scalar(
            out=DC[:, :, dst:dst + n], in0=cols[:, :, src:src + n], scalar1=sc, scalar2=None, op0=ALU.mult),
            ["cols"], [("DC", dst)])
    dc_all = [("DC", d) for d in (D_CFWH, D_SCWH, D_GQ8, D_LNGH, D_LNBH)]

    jobs = []
    for l in range(NL):
        for c in range(26):
            jobs.append((wfm_d[l, c].rearrange("p a b -> p (a b)"), wfm_b[l, c], 1024, ("wsc", l, c)))
        for m in range(8):
            jobs.append((wo_d[l, m].rearrange("p a b -> p (a b)"), wo_b[l, m], 1024, ("wsc", l, 26 + m)))
        for k in range(4):
            jobs.append((wv_d[l][:, 2 * k:2 * k + 2, :].rearrange("p a b -> p (a b)"), wv_b[l][:, 1024 * k:1024 * (k + 1)],
                         1024, ("wvsc", l, k)))
        jobs.append((pw_d[l].rearrange("p a b -> p (a b)"), pw_b[l], 512, ("pwsc", l)))
    WRflat = [w.reshape([128, 1024]) for w in WR]
    cast_rot = ["pool", "act", "dve"]
    NST = NFT // 2

    def emit_load(n):
        src, dst, ne, key = jobs[n]
        j = n % NST
        P.add("sp", lambda: nc.sync.dma_start(out=FTflat[:, j, 0:ne], in_=src), [], [("FT", 2 * j), ("FT", 2 * j + 1)], dma=True)

    def emit_cast_store(n):
        src, dst, ne, key = jobs[n]
        j = n % NST
        slot = n % NSLOT
        en = cast_rot[n % 3]
        o_ = WRflat[slot][:, 0:ne]
        i_ = FTflat[:, j, 0:ne]
        if en == "pool":
            fn = lambda: G_.tensor_copy(out=o_, in_=i_)
        elif en == "act":
            fn = lambda: S_.copy(out=o_, in_=i_)
        else:
            fn = lambda: V_.tensor_copy(out=o_, in_=i_)
        P.add(en, fn, [("FT", 2 * j), ("FT", 2 * j + 1)], [("WR", slot)])
        P.add("sp", lambda: nc.sync.dma_start(out=dst, in_=o_), [("WR", slot)], [key], dma=True)

    NJ0 = 39 if (NL > 1 and NBLK >= 2) else len(jobs)
    PRE = min(3, NST - 1)
    for n in range(NJ0 + PRE):
        if n < NJ0:
            emit_load(n)
        if n - PRE >= 0:
            emit_cast_store(n - PRE)

    Vf32 = V.bitcast(F32).reshape([128, NTT * 512])
    Vb16 = V.reshape([128, NTT * 1024])
    stg_f = Vf32[:, (NTT - 4) * 512:(NTT - 4) * 512 + 1024]
    stg_fk = [("V", NTT - 4), ("V", NTT - 3)]
    stg_b = [Vb16[:, (NTT - 2) * 1024:(NTT - 1) * 1024], Vb16[:, (NTT - 1) * 1024:NTT * 1024]]
    stg_bk = [[("V", NTT - 2)], [("V", NTT - 1)]]

    def late_gen():
        late = jobs[NJ0:]

        def A(n):
            src, dst, ne, key = late[n]
            P.add("sp", lambda: nc.sync.dma_start(out=stg_f[:, 0:ne], in_=src), [], stg_fk, dma=True)

        def B(n):
            src, dst, ne, key = late[n]
            P.add("act", lambda: S_.copy(out=stg_b[n % 2][:, 0:ne], in_=stg_f[:, 0:ne]), stg_fk, stg_bk[n % 2])

        def C(n):
            src, dst, ne, key = late[n]
            P.add("sp", lambda: nc.sync.dma_start(out=dst, in_=stg_b[n % 2][:, 0:ne]), stg_bk[n % 2], [key], dma=True)

        if not late:
            return
        A(0)
        yield
        for n in range(len(late)):
            B(n)
            yield
            if n + 1 < len(late):
                A(n + 1)
                yield
            C(n)
            yield
        P.add("pool", lambda: G_.memset(V[:, NTT - 4:NTT, :, 64:128], 1.0), [], [("V", tt) for tt in range(NTT - 4, NTT)])

    lstate = dict(gen=late_gen(), active=False, done=False)

    def late_step(n=1):
        if not lstate["active"] or lstate["done"]:
            return
        for _ in range(n):
            try:
                next(lstate["gen"])
            except StopIteration:
                lstate["done"] = True
                return

    def late_flush():
        if lstate["done"]:
            return
        for _ in lstate["gen"]:
            pass
        lstate["done"] = True

    chunks = []
    for s in range(NSEQ):
        sb_ = [(l, b) for l in range(NL) for b in range(NBLK)]
        for n_, (l, b) in enumerate(sb_):
            if n_ == 0:
                for c in range(4):
                    chunks.append((wfm_b[l, c], ("wsc", l, c)))
            for c in range(4, 26):
                chunks.append((wfm_b[l, c], ("wsc", l, c)))
            if n_ + 1 < len(sb_):
                l2 = sb_[n_ + 1][0]
                for c in range(4):
                    chunks.append((wfm_b[l2, c], ("wsc", l2, c)))
            for m in range(8):
                chunks.append((wo_b[l, m], ("wsc", l, 26 + m)))
    cstate = dict(cons=0, issue=0)

    def get_chunk():
        while cstate["issue"] < min(len(chunks), cstate["cons"] + NSLOT):
            i = cstate["issue"]
            slot = i % NSLOT
            P.add("sp", lambda slot=slot, i=i: nc.sync.dma_start(out=WRflat[slot][:, :], in_=chunks[i][0]),
                  [chunks[i][1]], [("WR", slot)], dma=True)
            cstate["issue"] += 1
        slot = cstate["cons"] % NSLOT
        cstate["cons"] += 1
        late_step()
        return WR[slot], ("WR", slot)

    def proj_chunk(pp=None, rot=None):
        W, kW = get_chunk()
        pa, ka = (rot or acc).get()
        pp = hctx["p"] if pp is None else pp
        for kc in range(8):
            mm(pa[:, :], W[:, kc, :], HMs[pp][:, kc, :], kc == 0, kc == 7,
               [kW, ("HM", pp, kc, 0), ("HM", pp, kc, 1)], [ka])
        return pa, ka

    cv = dict(key=None, gen=None, U1=None)

    def make_conv(lt):
        U1 = []

        def conv_gen():
            for i in (0, 1):
                AA, kAA = ft.alloc()
                AB, kAB = ft.alloc()
                accs = [(AA, kAA), (AB, kAB)]
                for k in range(31):
                    At, kAt = accs[k % 2]
                    wc = DC[:, lt, D_CFWH + i * 31 + k:D_CFWH + i * 31 + k + 1]
                    src = U0[:, i, k:k + 512]
                    rk = [("U0", i), ("U0h", i), ("DC", D_CFWH)]
                    if k < 2:
                        P.add("dve", lambda At=At, src=src, wc=wc: V_.tensor_scalar(
                            out=At[:, :], in0=src, scalar1=wc, scalar2=None, op0=ALU.mult), rk, [kAt])
                    else:
                        P.add("dve", lambda At=At, src=src, wc=wc: V_.scalar_tensor_tensor(
                            out=At[:, :], in0=src, scalar=wc, in1=At[:, :], op0=ALU.mult, op1=ALU.add), rk + [kAt], [kAt])
                    yield
                P.add("dve", lambda AA=AA, AB=AB, i=i: V_.scalar_tensor_tensor(
                    out=AA[:, :], in0=AA[:, :], scalar=cols[:, lt, C_CFB + i:C_CFB + i + 1], in1=AB[:, :],
                    op0=ALU.add, op1=ALU.add), [kAA, kAB, "cols"], [kAA])
                ft.release(kAB)
                U1.append((AA, kAA))
                P.add("dve", lambda i=i: V_.tensor_copy(out=U0[:, i, 0:30], in_=U0[:, i, 512:542]), [("U0", i)], [("U0h", i)])
                yield

        return conv_gen(), U1

    def drain(gen, n):
        if gen is None:
            return
        for _ in range(n):
            try:
                next(gen)
            except StopIteration:
                return

    def emit_cf_pair(i, pp, rot):
        pa, ka = proj_chunk(pp, rot)
        pg, kg = proj_chunk(pp, rot)
        P.add("act", lambda: S_.activation(out=U0[:, i, 30:542], in_=pg[:, :], func=AF.Tanh, scale=0.5), [kg], [("U0", i)])
        P.add("dve", lambda: V_.scalar_tensor_tensor(
            out=U0[:, i, 30:542], in0=U0[:, i, 30:542], scalar=1.0, in1=pa[:, :], op0=ALU.add, op1=ALU.mult),
            [("U0", i), ka], [("U0", i)])


    ev = dict(i=0)

    def evac_copy(out, in_, reads, writes):
        ev["i"] += 1
        if ev["i"] % 2 == 0:
            P.add("act", lambda: S_.copy(out=out, in_=in_), reads, writes)
        else:
            P.add("dve", lambda: V_.tensor_copy(out=out, in_=in_), reads, writes)

    deferred = []

    def flush_deferred(n=100):
        while deferred and n > 0:
            fn, rd, wr = deferred.pop(0)
            P.add("sp", fn, rd, wr, dma=True)
            n -= 1

    def emit_A(l, b, p):
        HM = HMs[p]
        t0 = b * 512
        tts = [4 * b + i for i in range(4)]
        blk = slice(t0, t0 + 512)
        xk = lambda kc: [("xT", kc, tt) for tt in tts]
        hk = lambda kc: [("HM", p, kc, 0), ("HM", p, kc, 1)]
        P.add("act", lambda: S_.activation(out=HM[:, :, :], in_=xT[:, :, blk], func=AF.Square),
              [k for kc in range(8) for k in xk(kc)], [k for kc in range(8) for k in hk(kc)])
        pst, kst = stat.get()
        for kc in range(8):
            mm(pst[:, :], onesb[:, :], HM[:, kc, :], kc == 0, kc == 7, hk(kc) + ["onesb"], [kst])
        RS, kRS = ft.alloc()
        P.add("act", lambda: S_.activation(out=RS[:, :], in_=pst[:, :], func=AF.Ln, bias=EPS, scale=1.0 / 1024),
              [kst], [kRS])
        P.add("act", lambda: S_.activation(out=RS[:, :], in_=RS[:, :], func=AF.Exp, scale=-0.5), [kRS], [kRS])
        for kc in range(8):
            P.add("dve", lambda kc=kc: V_.scalar_tensor_tensor(
                out=HM[:, kc, :], in0=xT[:, kc, blk], scalar=cols[:, l, C_G + kc:C_G + kc + 1], in1=RS[:, :],
                op0=ALU.mult, op1=ALU.mult), xk(kc) + [kRS, "cols"], hk(kc))
        ft.release(kRS)

    def block(s, l, b, last_layer, p, nxt):
        hctx["p"] = p
        HM = HMs[p]
        XS = XSs[p]
        XSK = XSKs[p]
        t0 = b * 512
        tts = [4 * b + i for i in range(4)]
        blk = slice(t0, t0 + 512)
        xk = lambda kc: [("xT", kc, tt) for tt in tts]

        if cv["key"] == (s, l, b):
            cg, U1 = cv["gen"], cv["U1"]
        else:
            if b == 0:
                P.add("pool", lambda: G_.memset(U0[:, :, 0:30], 0.0), [], [("U0h", 0), ("U0h", 1)])
            for i in (0, 1):
                emit_cf_pair(i, p, acc)
            cg, U1 = make_conv(l)
        cv["key"] = None

        def drain_conv(n):
            drain(cg, n)

        def tanh_gate(pz, kz, out_ap, wkeys):
            P.add("act", lambda: S_.activation(out=out_ap, in_=pz[:, :], func=AF.Silu), [kz], wkeys)

        an = {}

        def a_sq():
            if nxt is None:
                return
            l2, b2 = nxt
            HM2 = HMs[1 - p]
            blk2 = slice(b2 * 512, b2 * 512 + 512)
            tts2 = [4 * b2 + i for i in range(4)]
            xk2 = lambda kc: [("xT", kc, tt) for tt in tts2]
            hk2 = lambda kc: [("HM", 1 - p, kc, 0), ("HM", 1 - p, kc, 1)]
            an["c"] = (l2, HM2, blk2, xk2, hk2)

        def a_sq_part(q):
            if nxt is None:
                return
            l2, HM2, blk2, xk2, hk2 = an["c"]
            for kc in (2 * q, 2 * q + 1):
                P.add("act", lambda kc=kc: S_.activation(out=HM2[:, kc, :], in_=xT[:, kc, blk2], func=AF.Square),
                      xk2(kc), hk2(kc))

        def a_mm():
            if nxt is None:
                return
            l2, HM2, blk2, xk2, hk2 = an["c"]
            pst, kst = stat.get()
            for kc in range(8):
                mm(pst[:, :], onesb[:, :], HM2[:, kc, :], kc == 0, kc == 7, hk2(kc) + ["onesb"], [kst])
            an["pst"] = (pst, kst)

        def a_rs():
            if nxt is None:
                return
            pst, kst = an["pst"]
            RS, kRS = ft.alloc()
            P.add("act", lambda: S_.activation(out=RS[:, :], in_=pst[:, :], func=AF.Ln, bias=EPS, scale=1.0 / 1024),
                  [kst], [kRS])
            P.add("act", lambda: S_.activation(out=RS[:, :], in_=RS[:, :], func=AF.Exp, scale=-0.5), [kRS], [kRS])
            an["rs"] = (RS, kRS)

        def a_ht():
            if nxt is None:
                return
            l2, HM2, blk2, xk2, hk2 = an["c"]
            RS, kRS = an["rs"]
            for kc in range(8):
                P.add("dve", lambda kc=kc: V_.scalar_tensor_tensor(
                    out=HM2[:, kc, :], in0=xT[:, kc, blk2], scalar=cols[:, l2, C_G + kc:C_G + kc + 1], in1=RS[:, :],
                    op0=ALU.mult, op1=ALU.mult), xk2(kc) + [kRS, "cols"], hk2(kc))
            ft.release(kRS)

        pF, kF = ps[7], ("ps", 7)
        for i, tt in enumerate(tts):
            pv, kv = misc.get()
            for kc in range(8):
                mm(pv[:, :], HM[:, kc, i * 128:(i + 1) * 128], WVv[:, kc, :], kc == 0, kc == 7, hmk(kc) + ["WVv"], [kv])
            for kc in range(8):
                mm(pF[:, i * 8:(i + 1) * 8], HM[:, kc, i * 128:(i + 1) * 128], WVf[:, kc, :], kc == 0, kc == 7,
                   hmk(kc) + ["WVf"], [kF])
            evac_copy(V[:, tt, :, 0:64], pv[:, :].rearrange("p (h d) -> p h d", h=8), [kv], [("V", tt)])
        FZ, kFZ = ft.alloc()
        P.add("dve", lambda: V_.tensor_tensor(out=FZ[:, 0:32], in0=pF[:, 0:32], in1=bfb[:, l, :], op=ALU.add),
              [kF, "bfb"], [kFZ])
        P.add("act", lambda: S_.activation(out=FZ[:, 0:32], in_=FZ[:, 0:32], func=AF.Exp, scale=-1.0), [kFZ], [kFZ])
        P.add("act", lambda: S_.activation(out=FZ[:, 0:32], in_=FZ[:, 0:32], func=AF.Ln, bias=1.0), [kFZ], [kFZ])
        cdma_w, cdma_r = [], []

        def flush_cdma(lst):
            while lst:
                it = lst.pop(0)
                if callable(it):
                    it()
                else:
                    P.add("sp", it[0], it[1], it[2], dma=True)

        def fpath_tail():
            pC, kC = ps[6], ("ps", 6)
            for i in range(4):
                for j in range(i + 1):
                    rhs = negtri_f if j == i else nonesf[:, :]
                    mm(pC[0:8, i * 128:(i + 1) * 128], FZ[:, j * 8:(j + 1) * 8], rhs, j == 0, j == i,
                       [kFZ, "cstf", "nonesf"], [kC])
            CC, kCC = ft.alloc()
            P.add("dve", lambda: V_.tensor_scalar(out=CC[0:8, :], in0=pC[0:8, :], scalar1=CB[0:8, 0:1], scalar2=None,
                                                  op0=ALU.add), [kC, "CB"], [kCC])
            ft.release(kFZ)
            P.add("dve", lambda: V_.tensor_copy(out=CB[0:8, 0:1], in_=CC[0:8, 511:512]), [kCC], ["CB"])
            HI, kHI = bt.alloc()
            MID, kMID = bt.alloc()
            LO, kLO = bt.alloc()
            R1, kR1 = ft.alloc()
            R2, kR2 = ft.alloc()
            P.add("dve", lambda: V_.tensor_copy(out=HI[0:8, :], in_=CC[0:8, :]), [kCC], [kHI])
            P.add("dve", lambda: V_.tensor_tensor(out=R1[0:8, :], in0=CC[0:8, :], in1=HI[0:8, :], op=ALU.subtract),
                  [kCC, kHI], [kR1])
            P.add("dve", lambda: V_.tensor_copy(out=MID[0:8, :], in_=R1[0:8, :]), [kR1], [kMID])
            P.add("dve", lambda: V_.tensor_tensor(out=R2[0:8, :], in0=R1[0:8, :], in1=MID[0:8, :], op=ALU.subtract),
                  [kR1, kMID], [kR2])
            P.add("dve", lambda: V_.tensor_copy(out=LO[0:8, :], in_=R2[0:8, :]), [kR2], [kLO])
            ft.release(kCC)
            ft.release(kR1)
            ft.release(kR2)
            sc = scr_d[b % 2]
            for j, (tile_, key_) in enumerate(((HI, kHI), (MID, kMID), (LO, kLO))):
                cdma_w.append((lambda j=j, tile_=tile_: nc.sync.dma_start(out=sc[:, j, :], in_=tile_[0:8, :]),
                               [key_], [("scr", b % 2, j)]))
            cdma_w.append(lambda: (bt.release(kHI), bt.release(kMID), bt.release(kLO)))
            scr_keys = [("scr", b % 2, j) for j in range(3)]
            cdma_r.append((lambda: nc.sync.dma_start(out=QA[64:67, :, :], in_=sc.rearrange("h j t -> j h t")),
                           scr_keys + [("QAc",)], [("QAaug",)]))
            cdma_r.append((lambda: nc.sync.dma_start(out=KA[67:70, :, blk], in_=sc.rearrange("h j t -> j h t")),
                           scr_keys + [("KAc",)], [("KAaug", b)]))

        def qk_s1(which, j):
            pa, ka = proj_chunk()
            SQ, kSQ = bt.alloc()
            P.add("act", lambda: S_.activation(out=SQ[:, :], in_=pa[:, :], func=AF.Square), [ka], [kSQ])
            return (which, j, pa, ka, SQ, kSQ)

        def qk_s2(st_):
            which, j, QR, kQR, SQ, kSQ = st_
            pm, km = stat.get()
            mm(pm[:, :], bdiag_b, SQ[:, :], True, True, [kSQ, "cstb"], [km])
            RQ, kRQ = ft.alloc()
            P.add("act", lambda: S_.activation(out=RQ[:, :], in_=pm[:, :], func=AF.Ln, bias=EPS, scale=1.0 / 64),
                  [km], [kRQ])
            P.add("act", lambda: S_.activation(out=RQ[:, :], in_=RQ[:, :], func=AF.Exp, scale=-0.5), [kRQ], [kRQ])
            for par in (0, 1):
                h = 2 * j + par
                r0 = 64 * par
                if which == "q":
                    dst = QA[0:64, h, :]
                    wkey = ("QA", h)
                    gc = DC[r0:r0 + 64, l, D_GQ8 + j:D_GQ8 + j + 1]
                    gk = ("DC", D_GQ8)
                else:
                    dst = KA[0:64, h, blk]
                    wkey = ("KA", h, b)
                    gc = cols[r0:r0 + 64, l, C_GK + j:C_GK + j + 1]
                    gk = "cols"
                P.add("dve", lambda dst=dst, gc=gc, r0=r0: V_.scalar_tensor_tensor(
                    out=dst, in0=QR[r0:r0 + 64, :], scalar=gc, in1=RQ[r0:r0 + 64, :], op0=ALU.mult, op1=ALU.mult),
                    [kQR, kRQ, gk], [wkey])
            ft.release(kRQ)
            bt.release(kSQ)

        pend = None
        for which, j in [("k", j) for j in range(4)] + [("q", j) for j in range(4)]:
            cur = qk_s1(which, j)
            if pend is not None:
                qk_s2(pend)
            pend = cur
            if which == "k" and j == 1:
                fpath_tail()
            drain_conv(1)

        flush_deferred()
        flush_cdma(cdma_w)
        for i in (0, 1):
            pz, kz = proj_chunk()
            if pend is not None:
                qk_s2(pend)
                pend = None
            tanh_gate(pz, kz, ZC[:, i, :], [("ZC", i)])
            drain_conv(1)
        for i in (0, 1):
            pc, kc_ = proj_chunk()
            px, kx = proj_chunk()
            CX, kCX = ft.alloc()
            P.add("act", lambda pc=pc, CX=CX: S_.copy(out=CX[:, :], in_=pc[:, :]), [kc_], [kCX])
            P.add("dve", lambda i=i, px=px, CX=CX: V_.tensor_tensor(out=S0[:, i, 2:514], in0=CX[:, :], in1=px[:, :], op=ALU.mult),
                  [kCX, kx], [("S0", i)])
            ft.release(kCX)
            drain_conv(2)
        flush_cdma(cdma_r)
        for i in (0, 1):
            pb, kb_ = proj_chunk()
            pz, kz = proj_chunk()
            T2, kT2 = ft.alloc()
            tanh_gate(pz, kz, T2[:, :], [kT2])
            P.add("dve", lambda i=i, pb=pb, T2=T2: V_.tensor_tensor(out=GS[:, i, :], in0=T2[:, :], in1=pb[:, :], op=ALU.mult),
                  [kT2, kb_], [("GS", i)])
            ft.release(kT2)
            drain_conv(2)
        a_sq()
        for j in range(4):
            pz, kz = proj_chunk()
            tanh_gate(pz, kz, ZA[:, j, :], [("ZA", j)])
            a_sq_part(j)
            drain_conv(1)
        a_mm()
        a_rs()

        ln = {}

        def ln1():
            pmu, kmu = stat.get()
            pm2, km2 = stat.get()
            for i in (0, 1):
                Tb, kTb = bt.alloc()
                Tq, kTq = bt.alloc()
                P.add("act", lambda i=i, Tb=Tb: S_.copy(out=Tb[:, :], in_=U1[i][0][:, :]), [U1[i][1]], [kTb])
                P.add("act", lambda i=i, Tq=Tq: S_.activation(out=Tq[:, :], in_=U1[i][0][:, :], func=AF.Square), [U1[i][1]], [kTq])
                mm(pmu[:, :], onesb[:, :], Tb[:, :], i == 0, i == 1, [kTb, "onesb"], [kmu])
                mm(pm2[:, :], onesb[:, :], Tq[:, :], i == 0, i == 1, [kTq, "onesb"], [km2])
                bt.release(kTb)
                bt.release(kTq)
            ln["st"] = (pmu, kmu, pm2, km2)

        def ln2a():
            pmu, kmu, pm2, km2 = ln["st"]
            MEAN, kME = ft.alloc()
            MSQ, kMS = ft.alloc()
            if b >= 2:
                P.add("dve", lambda: V_.tensor_scalar(out=MEAN[:, :], in0=pmu[:, :], scalar1=1.0 / 256, scalar2=None, op0=ALU.mult),
                      [kmu], [kME])
                P.add("dve", lambda: V_.tensor_tensor(out=MSQ[:, :], in0=MEAN[:, :], in1=MEAN[:, :], op=ALU.mult), [kME], [kMS])
            else:
                P.add("act", lambda: S_.activation(out=MEAN[:, :], in_=pmu[:, :], func=AF.Identity, scale=1.0 / 256), [kmu], [kME])
                P.add("act", lambda: S_.activation(out=MSQ[:, :], in_=pmu[:, :], func=AF.Square, scale=1.0 / 256), [kmu], [kMS])
            ln["m"] = (MEAN, kME, MSQ, kMS)

        def ln2b():
            pmu, kmu, pm2, km2 = ln["st"]
            MEAN, kME, MSQ, kMS = ln["m"]
            VAR, kVA = MSQ, kMS
            P.add("dve", lambda: V_.scalar_tensor_tensor(out=VAR[:, :], in0=pm2[:, :], scalar=1.0 / 256, in1=MSQ[:, :],
                                                         op0=ALU.mult, op1=ALU.subtract), [km2, kMS], [kVA])
            P.add("dve", lambda: V_.tensor_scalar(out=VAR[:, :], in0=VAR[:, :], scalar1=0.0, scalar2=EPS,
                                                  op0=ALU.max, op1=ALU.add), [kVA], [kVA])
            ln["v"] = (VAR, kVA)

        def ln2c():
            VAR, kVA = ln["v"]
            P.add("act", lambda: S_.activation(out=VAR[:, :], in_=VAR[:, :], func=AF.Ln), [kVA], [kVA])

        def ln2c2():
            VAR, kVA = ln["v"]
            P.add("act", lambda: S_.activation(out=VAR[:, :], in_=VAR[:, :], func=AF.Exp, scale=-0.5), [kVA], [kVA])

        def ln2d():
            MEAN, kME, MSQ, kMS = ln["m"]
            VAR, kVA = ln["v"]
            for i in (0, 1):
                XN, kXN = U1[i]
                P.add("dve", lambda XN=XN: V_.tensor_tensor(out=XN[:, :], in0=XN[:, :], in1=MEAN[:, :], op=ALU.subtract),
                      [kXN, kME], [kXN])
                P.add("dve", lambda XN=XN: V_.tensor_tensor(out=XN[:, :], in0=XN[:, :], in1=VAR[:, :], op=ALU.mult),
                      [kXN, kVA], [kXN])
            ft.release(kME)
            ft.release(kVA)

        def ln2e():
            pass

        def ln2f():
            SL = []
            for i in (0, 1):
                XN, kXN = U1[i]
                SLt, kSL = bt.alloc()
                P.add("act", lambda XN=XN, SLt=SLt, i=i: S_.activation(
                    out=SLt[:, :], in_=XN[:, :], func=AF.Silu, bias=cols[:, l, C_LNB + i:C_LNB + i + 1],
                    scale=cols[:, l, C_LNG + i:C_LNG + i + 1]), [kXN, "cols"], [kSL])
                ft.release(kXN)
                SL.append((SLt, kSL))
            ln["SL"] = SL

        def ln2g():
            pass

        def ln3a():
            SL = ln["SL"]
            pps = []
            for co in (0, 1):
                ppo, kpo = stat.get()
                for ci in (0, 1):
                    mm(ppo[:, :], PW[:, ci, co * 128:(co + 1) * 128], SL[ci][0][:, :], ci == 0, ci == 1, [SL[ci][1], "PW"], [kpo])
                pps.append((ppo, kpo))
            for i in (0, 1):
                bt.release(SL[i][1])
            ln["pp"] = pps

        def ln3b():
            for co in (0, 1):
                ppo, kpo = ln["pp"][co]
                P.add("dve", lambda co=co, ppo=ppo: V_.scalar_tensor_tensor(
                    out=HM[:, co, :], in0=ppo[:, :], scalar=1.0, in1=ZC[:, co, :], op0=ALU.mult, op1=ALU.mult),
                    [kpo, ("ZC", co)], hmk(co))

        def sc_stage():
            for i in (0, 1):
                SA, kSA = ft.alloc()
                SB, kSB = ft.alloc()
                w = lambda k: DC[:, l, D_SCWH + i * 3 + k:D_SCWH + i * 3 + k + 1]
                rk = [("S0", i), ("S0h", i), ("DC", D_SCWH)]
                P.add("dve", lambda i=i, SA=SA, w0=w(0): V_.tensor_scalar(out=SA[:, :], in0=S0[:, i, 0:512], scalar1=w0, scalar2=None,
                                                                        op0=ALU.mult), rk, [kSA])
                P.add("dve", lambda i=i, SA=SA, SB=SB, w1=w(1): V_.scalar_tensor_tensor(
                    out=SB[:, :], in0=S0[:, i, 1:513], scalar=w1, in1=SA[:, :], op0=ALU.mult, op1=ALU.add), rk + [kSA], [kSB])
                P.add("dve", lambda i=i, SA=SA, SB=SB, w2=w(2): V_.scalar_tensor_tensor(
                    out=SA[:, :], in0=S0[:, i, 2:514], scalar=w2, in1=SB[:, :], op0=ALU.mult, op1=ALU.add), rk + [kSB], [kSA])
                P.add("dve", lambda i=i, SA=SA: V_.tensor_tensor(out=HM[:, 2 + i, :], in0=SA[:, :], in1=GS[:, i, :], op=ALU.mult),
                      [kSA, ("GS", i)], hmk(2 + i))
                P.add("dve", lambda i=i: V_.tensor_copy(out=S0[:, i, 0:2], in_=S0[:, i, 512:514]), [("S0", i)], [("S0h", i)])
                ft.release(kSA)
                ft.release(kSB)

        def ln1_act():
            tl = []
            for i in (0, 1):
                _, kT = ft.alloc()
                Tb = FTb16[:, kT[1], 0:512]
                Tq = FTb16[:, kT[1], 512:1024]
                if b >= 2:
                    P.add("dve", lambda i=i, Tb=Tb: V_.tensor_copy(out=Tb, in_=U1[i][0][:, :]), [U1[i][1]], [kT])
                    P.add("dve", lambda i=i, Tq=Tq: V_.tensor_tensor(out=Tq, in0=U1[i][0][:, :], in1=U1[i][0][:, :], op=ALU.mult),
                          [U1[i][1], kT], [kT])
                else:
                    P.add("act", lambda i=i, Tb=Tb: S_.copy(out=Tb, in_=U1[i][0][:, :]), [U1[i][1]], [kT])
                    P.add("act", lambda i=i, Tq=Tq: S_.activation(out=Tq, in_=U1[i][0][:, :], func=AF.Square), [U1[i][1], kT], [kT])
                tl.append((Tb, Tq, kT))
            ln["tl"] = tl

        def ln1_mm():
            pmu, kmu = stat.get()
            pm2, km2 = stat.get()
            for i, (Tb, Tq, kT) in enumerate(ln["tl"]):
                mm(pmu[:, :], onesb[:, :], Tb, i == 0, i == 1, [kT, "onesb"], [kmu])
                mm(pm2[:, :], onesb[:, :], Tq, i == 0, i == 1, [kT, "onesb"], [km2])
                ft.release(kT)
            ln["st"] = (pmu, kmu, pm2, km2)

        def taps6():
            drain_conv(6)

        def taps_all():
            drain_conv(1000)

        def pf_halo():
            if nxt is not None and nxt[1] == 0:
                P.add("pool", lambda: G_.memset(U0[:, :, 0:30], 0.0), [], [("U0h", 0), ("U0h", 1)])

        def pf_cf0():
            if nxt is not None:
                emit_cf_pair(0, 1 - p, stat)

        def pf_cf1():
            if nxt is not None:
                emit_cf_pair(1, 1 - p, stat)

        def pf_conv():
            if nxt is not None:
                g_, u_ = make_conv(nxt[0])
                cv["key"] = (s, nxt[0], nxt[1])
                cv["gen"] = g_
                cv["U1"] = u_

        NSL = 32
        slot_fns = {0: [sc_stage, taps6], 1: [taps6], 2: [taps6], 3: [taps6], 4: [taps6], 5: [taps6],
                    6: [a_ht, taps6], 7: [taps_all], 8: [ln1_act], 10: [ln1_mm], 12: [ln2a], 13: [ln2b], 15: [ln2c], 16: [ln2c2],
                    17: [ln2d], 19: [ln2e, pf_halo], 20: [pf_cf0, ln2f], 22: [pf_cf1], 23: [ln2g], 25: [ln3a],
                    27: [ln3b], 28: [pf_conv]}

        steps = [(h, kb) for h in range(8) for kb in range(4 * b + 4)]
        nsteps = len(steps)
        stb = [(ps[i], ("ps", i)) for i in range(4)]
        pts = {}
        nrm = {}

        def emit_score(idx):
            h, kb = steps[idx]
            st, kst_ = stb[idx % 4]
            i = kb - 4 * b
            n0 = 128 * i if i > 0 else 0
            rd = [("KA", h, kb // 4), ("KAaug", kb // 4), ("KAc",), ("QA", h), ("QAaug",), ("QAc",)]
            mm(st[:, n0:512], KA[0:70, h, kb * 128:(kb + 1) * 128], QA[0:70, h, n0:512], True, i < 0, rd, [kst_])
            if i >= 0:
                mm(st[:, n0:n0 + 128], ident_b, maskneg_b, False, True, ["cstb"], [kst_])
            PTt, kPT = bt.alloc()
            P.add("act", lambda: S_.activation(out=PTt[:, n0:512], in_=st[:, n0:512], func=AF.Exp), [kst_], [kPT])
            pts[idx] = (PTt, kPT, n0)

        def n1(h):
            po, kpo = ps[4 + h % 2], ("ps", 4 + h % 2)
            if h % 2 == 0:
                NUM, kNU = ft.alloc()
                DEN, kDE = ft.alloc()
                nrm["pair"] = (NUM, kNU, DEN, kDE)
                if b >= 2:
                    P.add("dve", lambda: V_.tensor_copy(out=NUM[0:64, :], in_=po[0:64, :]), [kpo], [kNU])
                else:
                    P.add("act", lambda: S_.copy(out=NUM[0:64, :], in_=po[0:64, :]), [kpo], [kNU])
                P.add("act", lambda: S_.activation(out=DEN[0:64, :], in_=po[64:128, :], func=AF.Ln), [kpo], [kDE])
            else:
                NUM, kNU, DEN, kDE = nrm["pair"]
                P.add("dve", lambda: V_.tensor_copy(out=NUM[64:128, :], in_=po[0:64, :]), [kpo], [kNU])
                P.add("act", lambda: S_.activation(out=DEN[64:128, :], in_=po[64:128, :], func=AF.Ln), [kpo], [kDE])
                nrm[h // 2] = (NUM, kNU, DEN, kDE)

        def n2(j):
            NUM, kNU, DEN, kDE = nrm[j]
            P.add("act", lambda: S_.activation(out=DEN[:, :], in_=DEN[:, :], func=AF.Exp, scale=-1.0), [kDE], [kDE])
            P.add("dve", lambda: V_.tensor_tensor(out=NUM[:, :], in0=NUM[:, :], in1=DEN[:, :], op=ALU.mult), [kNU, kDE], [kNU])

        def n3(j):
            NUM, kNU, DEN, kDE = nrm.pop(j)
            P.add("dve", lambda: V_.tensor_tensor(out=HM[:, 4 + j, :], in0=NUM[:, :], in1=ZA[:, j, :], op=ALU.mult),
                  [kNU, ("ZA", j)], [("HM", p, 4 + j, 0), ("HM", p, 4 + j, 1)])
            ft.release(kNU)
            ft.release(kDE)

        def emit_pv(idx):
            h, kb = steps[idx]
            PTt, kPT, n0 = pts.pop(idx)
            po, kpo = ps[4 + h % 2], ("ps", 4 + h % 2)
            last = kb == 4 * b + 3
            mm(po[:, n0:512], V[:, kb, h, :], PTt[:, n0:512], kb == 0, last, [("V", kb), kPT], [kpo])
            bt.release(kPT)
            if last:
                n1(h)
                if h % 2 == 0 and h >= 2:
                    n2(h // 2 - 1)
                if h % 2 == 1 and h >= 3:
                    n3(h // 2 - 1)

        fired = set()
        for idx in range(nsteps + LA):
            if idx < nsteps:
                emit_score(idx)
            j = idx - LA
            if j >= 0:
                emit_pv(j)
                k = ((j + 1) * NSL) // nsteps - 1
                for kk in range(k + 1):
                    if kk not in fired and (kk + 1) * nsteps <= (j + 1) * NSL:
                        fired.add(kk)
                        late_step()
                        for fn_ in slot_fns.get(kk, ()):
                            fn_()
        n2(3)
        n3(3)
        for kk in range(NSL):
            if kk not in fired:
                fired.add(kk)
                for fn_ in slot_fns.get(kk, ()):
                    fn_()

        for m in range(8):
            pa, ka = proj_chunk()
            P.add("dve", lambda m=m, pa=pa: V_.tensor_tensor(out=xT[:, m, blk], in0=xT[:, m, blk], in1=pa[:, :], op=ALU.add),
                  xk(m) + [ka], xk(m))
            if cv["key"] is not None:
                drain(cv["gen"], 2)

        if last_layer:
            for i, tt in enumerate(tts):
                XO = XS[i % 2]
                if i >= 2:
                    flush_deferred(1)
                for half in (0, 1):
                    m0 = 4 * half
                    pt_, kpt = misc.get()
                    for q in range(4):
                        P.add("pe", lambda q=q, pt_=pt_, m0=m0, tt=tt: T_.transpose(
                            pt_[:, q * 128:(q + 1) * 128], xT[:, m0 + q, tt * 128:(tt + 1) * 128], ident_f),
                            [("xT", m0 + q, tt), "cstf"], [kpt])
                    evac_copy(XO[:, m0 * 128:(m0 + 4) * 128], pt_[:, :], [kpt], XSK[i % 2][4 * half * 2:(4 * half + 4) * 2] if False else [("XO", p, i % 2, half)] + XSK[i % 2])
                deferred.append((lambda XO=XO, tt=tt, s=s: nc.sync.dma_start(out=y_d[s, tt * 128:(tt + 1) * 128, :], in_=XO),
                                 [("XO", p, i % 2, 0), ("XO", p, i % 2, 1)] + XSK[i % 2], [("y", s, tt)] + XSK[i % 2]))

    for s in range(NSEQ):
        flush_deferred()
        for tt in range(NTT):
            XI = XSs[0][tt % 2]
            XIK = XSKs[0][tt % 2]
            P.add("sp", lambda XI=XI, tt=tt, s=s: nc.sync.dma_start(out=XI, in_=x_d[s, tt * 128:(tt + 1) * 128, :]),
                  [], XIK, dma=True)
            for half in (0, 1):
                m0 = 4 * half
                pt_, kpt = misc.get()
                for q in range(4):
                    P.add("pe", lambda q=q, pt_=pt_, XI=XI, m0=m0: T_.transpose(
                        pt_[:, q * 128:(q + 1) * 128], XI[:, (m0 + q) * 128:(m0 + q + 1) * 128], ident_f),
                        XIK + ["cstf"], [kpt])
                evac_copy(xT[:, m0:m0 + 4, tt * 128:(tt + 1) * 128], pt_[:, :].rearrange("p (a b) -> p a b", a=4),
                          [kpt], [("xT", m0 + q, tt) for q in range(4)])
        seqblocks = [(l, b) for l in range(NL) for b in range(NBLK)]
        emit_A(0, 0, 0)
        for n_, (l, b) in enumerate(seqblocks):
            if b == 0:
                P.add("sp", lambda l=l: nc.sync.dma_start(out=WVv.reshape([128, 4096])[:, :], in_=wv_b[l]),
                      [("wvsc", l, k) for k in range(4)], ["WVv"], dma=True)
                P.add("sp", lambda l=l: nc.sync.dma_start(out=PW.reshape([128, 512])[:, :], in_=pw_b[l]), [("pwsc", l)], ["PW"], dma=True)
                P.add("sp", lambda l=l: nc.sync.dma_start(out=WFs[:, :], in_=wf_d[l]), [], ["WFs"], dma=True)
                P.add("dve", lambda: V_.tensor_copy(out=WVf.reshape([128, 64])[:, :], in_=WFs[:, :]), ["WFs"], ["WVf"])
                P.add("pool", lambda: G_.memset(CB[:, :], 0.0), [], ["CB"])
                P.add("pool", lambda: G_.memset(S0[:, :, 0:2], 0.0), [], [("S0h", 0), ("S0h", 1)])
            nxt = seqblocks[n_ + 1] if n_ + 1 < len(seqblocks) else None
            if s == 0 and l == 0 and b <= NBLK - 2:
                lstate["active"] = True
            elif not lstate["done"]:
                lstate["active"] = True
                late_flush()
            block(s, l, b, l == NL - 1, n_ % 2, nxt)
            lstate["active"] = False

    flush_deferred()
    ykeys = [("y", s, tt) for s in range(NSEQ) for tt in range(NTT)]
    P.add("sp", lambda: None, ykeys, [])
    return nc, P


def prep_shared(inputs, NL):
    w_in = np.asarray(inputs["w_in"], np.float32)[:NL]
    w_out = np.asarray(inputs["w_out"], np.float32)[:NL]
    order = chunk_order()
    wfm = np.empty((NL, 26, 128, 8, 128), np.float32)
    for c, (kind, idx) in enumerate(order):
        c0 = _COL0[kind] + idx * 128
        blk = w_in[:, :, c0:c0 + 128].reshape(NL, 8, 128, 128)
        wfm[:, c] = blk.transpose(0, 2, 1, 3)
    wv = np.ascontiguousarray(w_in[:, :, 2816:3328].reshape(NL, 8, 128, 512).transpose(0, 2, 1, 3))
    wf = np.ascontiguousarray(w_in[:, :, 3840:3848].reshape(NL, 8, 128, 8).transpose(0, 2, 1, 3)).reshape(NL, 128, 64)
    wo = w_out.reshape(NL, 8, 128, 8, 128).transpose(0, 3, 2, 1, 4)
    pw = np.asarray(inputs["cf_pw"], np.float32)[:NL].reshape(NL, 2, 128, 256).transpose(0, 2, 1, 3)
    cols = np.zeros((128, NL, NCOL), np.float32)
    g = np.asarray(inputs["norm_g"], np.float32)[:NL]
    cols[:, :, C_G:C_G + 8] = g.reshape(NL, 8, 128).transpose(2, 0, 1)
    for name, c0 in (("cf_dw_b", C_CFB), ("cf_ln_g", C_LNG), ("cf_ln_b", C_LNB)):
        a = np.asarray(inputs[name], np.float32)[:NL]
        cols[:, :, c0:c0 + 2] = a.reshape(NL, 2, 128).transpose(2, 0, 1)
    cfw = np.asarray(inputs["cf_dw"], np.float32)[:NL]
    cols[:, :, C_CFW:C_CFW + 62] = cfw.reshape(NL, 31, 2, 128).transpose(3, 0, 2, 1).reshape(128, NL, 62)
    scw = np.asarray(inputs["sc_dw"], np.float32)[:NL]
    cols[:, :, C_SCW:C_SCW + 6] = scw.reshape(NL, 3, 2, 128).transpose(3, 0, 2, 1).reshape(128, NL, 6)
    for name, c0 in (("q_norm_g", C_GQ), ("k_norm_g", C_GK)):
        a = np.asarray(inputs[name], np.float32)[:NL]
        cols[:, :, c0:c0 + 4] = a.reshape(NL, 4, 128).transpose(2, 0, 1)
    bf = np.asarray(inputs["b_f"], np.float32)[:NL]
    bfb = np.broadcast_to(np.tile(bf, (1, 4))[None], (128, NL, 32)).copy()
    cst = np.zeros((128, 512), np.float32)
    ii = np.arange(128)
    cst[:, 0:128] = np.eye(128, dtype=np.float32)
    cst[:, 128:256] = -(ii[:, None] <= ii[None, :]).astype(np.float32)
    cst[:, 256:384] = (ii[:, None] // 64 == ii[None, :] // 64).astype(np.float32)
    cst[:, 384:512] = np.where(ii[:, None] <= ii[None, :], 0.0, -30000.0).astype(np.float32)
    return dict(wfm=np.ascontiguousarray(wfm), wv=wv, wf=wf, wo=np.ascontiguousarray(wo), pw=np.ascontiguousarray(pw),
                cols=cols, bfb=bfb, cst=cst)


_CACHE = {}


def run(inputs, NSEQ, NBLK, NL, n_cores, x_shards):
    key = (NSEQ, NBLK, NL)
    if key not in _CACHE:
        nc, P = build(NSEQ, NBLK, NL)
        with ExitStack() as stack:
            info = P.finalize(stack)
        _CACHE[key] = (nc, info)
    nc, info = _CACHE[key]
    shared = prep_shared(inputs, NL)
    in_maps = [dict(shared, x=np.ascontiguousarray(xs)) for xs in x_shards]
    res = run_bass_kernel_spmd(nc, in_maps, core_ids=list(range(n_cores)))
    return [r["y"] for r in res.results], info


def kernel(**inputs):
    x = np.asarray(inputs["x"], np.float32)
    n = 8
    shards = [x[2 * c:2 * c + 2] for c in range(n)]
    outs, _ = run(inputs, 2, 4, 2, n, shards)
    return np.concatenate(outs, axis=0).astype(np.float32)
```

```python
import numpy as np
from contextlib import ExitStack
import concourse.bass as bass
import concourse.mybir as mybir
from concourse.bass_utils import run_bass_kernel_spmd

F32 = mybir.dt.float32
BF16 = mybir.dt.bfloat16
AF = mybir.ActivationFunctionType
ALU = mybir.AluOpType

EPS = 1e-6
C_G, C_CFB, C_LNG, C_LNB, C_CFW, C_SCW, C_GQ, C_GK, NCOL = 0, 8, 10, 12, 14, 76, 82, 86, 90
D_CFWH, D_SCWH, D_GQ8, D_LNGH, D_LNBH, ND = 0, 62, 68, 72, 74, 76
NSLOT = 3
NFT = 8
NBT = 6
NDMASEM = 25
LA = 3


class Prog:
    ENGS = ["pe", "act", "dve", "pool", "sp"]

    def __init__(self, nc):
        self.nc = nc
        self.eng = {"pe": nc.tensor, "act": nc.scalar, "dve": nc.vector, "pool": nc.gpsimd, "sp": nc.sync}
        self.ops = []

    def add(self, eng, fn, reads=(), writes=(), dma=False):
        self.ops.append(dict(eng=eng, fn=fn, reads=list(reads), writes=list(writes), dma=dma))

    def finalize(self, stack):
        ops = self.ops
        last_w, readers = {}, {}
        for i, op in enumerate(ops):
            deps = set()
            for k in op["reads"]:
                if k in last_w:
                    deps.add(last_w[k])
            for k in op["writes"]:
                if k in last_w:
                    deps.add(last_w[k])
                for r in readers.get(k, ()):
                    deps.add(r)
            deps.discard(i)
            keep = set()
            for d in deps:
                dop = ops[d]
                if (not dop["dma"]) and dop["eng"] == "pe" and op["eng"] == "pe" and not op["dma"]:
                    continue
                keep.add(d)
            latest = {}
            kept2 = set()
            for d in keep:
                dop = ops[d]
                if dop["dma"]:
                    kept2.add(d)
                else:
                    latest[dop["eng"]] = max(latest.get(dop["eng"], -1), d)
            kept2 |= set(latest.values())
            op["deps"] = kept2
            for k in op["reads"]:
                readers.setdefault(k, []).append(i)
            for k in op["writes"]:
                last_w[k] = i
                readers[k] = []
        targets = set()
        for op in ops:
            targets |= op["deps"]
        cnt = {e: 0 for e in self.ENGS}
        dma_use = [0] * NDMASEM
        rr = 0
        for i, op in enumerate(ops):
            if op["dma"]:
                s = rr % NDMASEM
                rr += 1
                dma_use[s] += 1
                op["sem"] = ("dma", s)
                op["val"] = 16 * dma_use[s]
                op["prev"] = 16 * (dma_use[s] - 1)
            elif i in targets:
                cnt[op["eng"]] += 1
                op["sem"] = ("eng", op["eng"])
                op["val"] = cnt[op["eng"]]
            else:
                op["sem"] = None
        sems = {}

        def sem(key):
            if key not in sems:
                sems[key] = stack.enter_context(self.nc.semaphore("s_%s_%s" % key))
            return sems[key]

        waited = {e: {} for e in self.ENGS}
        nwait = 0
        for op in ops:
            e = op["eng"]
            E = self.eng[e]
            need = {}
            for d in op["deps"]:
                dop = ops[d]
                key = dop["sem"]
                need[key] = max(need.get(key, 0), dop["val"])
            if op["dma"] and op["prev"] > 0:
                need[op["sem"]] = max(need.get(op["sem"], 0), op["prev"])
            for key, val in need.items():
                if waited[e].get(key, 0) < val:
                    E.wait_ge(sem(key), val)
                    waited[e][key] = val
                    nwait += 1
            ins = op["fn"]()
            if op["sem"] is not None and ins is not None:
                ins.then_inc(sem(op["sem"]), 16 if op["dma"] else 1)
        return dict(n_ops=len(ops), n_wait=nwait, cnt=cnt)


class TPool:
    def __init__(self, tiles, name):
        self.tiles = tiles
        self.name = name
        self.free = list(range(len(tiles)))

    def alloc(self):
        assert self.free, "temp pool %s exhausted" % self.name
        i = self.free.pop(0)
        return self.tiles[i], (self.name, i)

    def release(self, key):
        assert key[0] == self.name and key[1] not in self.free
        self.free.append(key[1])


class Rot:
    def __init__(self, items):
        self.items = items
        self.i = 0

    def get(self):
        it = self.items[self.i % len(self.items)]
        self.i += 1
        return it


def chunk_order():
    o = []
    o += [("cfa", 0), ("cfg", 0), ("cfa", 1), ("cfg", 1)]
    o += [("k", j) for j in range(4)]
    o += [("q", j) for j in range(4)]
    o += [("cfz", 0), ("cfz", 1)]
    o += [("scC", 0), ("scx", 0), ("scC", 1), ("scx", 1)]
    o += [("scB", 0), ("scz", 0), ("scB", 1), ("scz", 1)]
    o += [("az", j) for j in range(4)]
    return o


_COL0 = {"cfa": 0, "cfg": 256, "cfz": 512, "scB": 768, "scC": 1024, "scx": 1280, "scz": 1536,
         "q": 1792, "k": 2304, "v": 2816, "az": 3328, "f": 3840}


def build(NSEQ, NBLK, NL):
    S = NBLK * 512
    NTT = NBLK * 4
    nc = bass.Bass("TRN2", target_bir_lowering=False)

    def dram(name, shape, dtype=F32, kind="ExternalInput"):
        return nc.dram_tensor(name, shape, dtype, kind=kind).ap()

    x_d = dram("x", [NSEQ, S, 1024])
    wfm_d = dram("wfm", [NL, 26, 128, 8, 128])
    wv_d = dram("wv", [NL, 128, 8, 512])
    wf_d = dram("wf", [NL, 128, 64])
    wo_d = dram("wo", [NL, 8, 128, 8, 128])
    pw_d = dram("pw", [NL, 128, 2, 256])
    cols_d = dram("cols", [128, NL, NCOL])
    bfb_d = dram("bfb", [128, NL, 32])
    cst_d = dram("cst", [128, 512])
    y_d = dram("y", [NSEQ, S, 1024], kind="ExternalOutput")
    scr_d = dram("scr", [2, 8, 3, 512], BF16, kind="Internal")
    wfm_b = dram("wfm_b", [NL, 26, 128, 1024], BF16, kind="Internal")
    wo_b = dram("wo_b", [NL, 8, 128, 1024], BF16, kind="Internal")
    wv_b = dram("wv_b", [NL, 128, 4096], BF16, kind="Internal")
    pw_b = dram("pw_b", [NL, 128, 512], BF16, kind="Internal")

    def A(name, shape, dtype):
        return nc.alloc_sbuf_tensor("sb_" + name, shape, dtype)

    xT = A("xT", [128, 8, S], F32)
    KA = A("KA", [128, 8, S], BF16)
    V = A("V", [128, NTT, 8, 128], BF16)
    QA = A("QA", [128, 8, 512], BF16)
    HMs = [A("HM0", [128, 8, 512], BF16), A("HM1", [128, 8, 512], BF16)]
    HMfs = [h_.bitcast(F32).reshape([128, 2, 1024]) for h_ in HMs]
    XSs = [[hf_[:, 0, :], hf_[:, 1, :]] for hf_ in HMfs]
    XSKs = [[[("HM", p_, kc, hf) for kc in range(4) for hf in (0, 1)], [("HM", p_, kc, hf) for kc in range(4, 8) for hf in (0, 1)]]
            for p_ in (0, 1)]
    WR = [A("WR%d" % i, [128, 8, 128], BF16) for i in range(NSLOT)]
    WVv = A("WVv", [128, 8, 512], BF16)
    WVf = A("WVf", [128, 8, 8], BF16)
    WFs = A("WFs", [128, 64], F32)
    PW = A("PW", [128, 2, 256], BF16)
    U0 = A("U0", [128, 2, 542], BF16)
    S0 = A("S0", [128, 2, 514], BF16)
    ZC = A("ZC", [128, 2, 512], BF16)
    GS = A("GS", [128, 2, 512], BF16)
    ZA = A("ZA", [128, 4, 512], BF16)
    FTbig = A("FT", [128, NFT, 512], F32)
    ft = TPool([FTbig[:, i, :] for i in range(NFT)], "FT")
    FTflat = FTbig.reshape([128, NFT // 2, 1024])
    FTb16 = FTbig.bitcast(BF16)
    bt = TPool([A("BT%d" % i, [128, 512], BF16) for i in range(NBT)], "BT")
    cols = A("cols", [128, NL, NCOL], F32)
    DC = A("DC", [128, NL, ND], F32)
    bfb = A("bfb", [128, NL, 32], F32)
    cstf = A("cstf", [128, 512], F32)
    cstb = A("cstb", [128, 512], BF16)
    onesb = A("onesb", [128, 128], BF16)
    nonesf = A("nonesf", [128, 128], F32)
    CB = A("CB", [128, 2], F32)
    ps = [nc.alloc_psum_tensor("ps%d" % i, [128, 512], F32) for i in range(8)]
    acc = Rot([(ps[i], ("ps", i)) for i in range(4)])
    misc = Rot([(ps[i], ("ps", i)) for i in (4, 5)])
    stat = Rot([(ps[i], ("ps", i)) for i in (6, 7)])

    ident_f = cstf[:, 0:128]
    negtri_f = cstf[:, 128:256]
    ident_b = cstb[:, 0:128]
    bdiag_b = cstb[:, 256:384]
    maskneg_b = cstb[:, 384:512]

    P = Prog(nc)
    V_ = nc.vector
    G_ = nc.gpsimd
    S_ = nc.scalar
    T_ = nc.tensor

    def mm(out, lhsT, rhs, start, stop, reads, writes):
        P.add("pe", lambda: T_.matmul(out, lhsT, rhs, start=start, stop=stop), reads, writes)

    hctx = dict(p=0)

    def hmk(kc):
        return [("HM", hctx["p"], kc, 0), ("HM", hctx["p"], kc, 1)]

    P.add("sp", lambda: nc.sync.dma_start(out=cols[:, :, :], in_=cols_d), [], ["cols"], dma=True)
    P.add("sp", lambda: nc.sync.dma_start(out=bfb[:, :, :], in_=bfb_d), [], ["bfb"], dma=True)
    P.add("sp", lambda: nc.sync.dma_start(out=cstf[:, :], in_=cst_d), [], ["cstf"], dma=True)
    P.add("dve", lambda: V_.tensor_copy(out=cstb[:, :], in_=cstf[:, :]), ["cstf"], ["cstb"])
    P.add("pool", lambda: G_.memset(onesb[:, :], 1.0), [], ["onesb"])
    P.add("pool", lambda: G_.memset(nonesf[:, :], -1.0), [], ["nonesf"])
    P.add("pool", lambda: G_.memset(V[:, :, :, :], 1.0), [], [("V", tt) for tt in range(NTT)])
    P.add("pool", lambda: G_.memset(QA[64:70, :, :], -1.0), [], [("QAc",)])
    P.add("pool", lambda: G_.memset(KA[64:70, :, :], 1.0), [], [("KAc",)])
    for (dst, src, n, sc) in ((D_CFWH, C_CFW, 62, 0.5), (D_SCWH, C_SCW, 6, 1.0), (D_GQ8, C_GQ, 4, 0.125),
                              (D_LNGH, C_LNG, 2, 0.5), (D_LNBH, C_LNB, 2, 0.5)):
        P.add("dve", lambda dst=dst, src=src, n=n, sc=sc: V_.tensor_scalar(
            out=DC[:, :, dst:dst + n], in0=cols[:, :, src:src + n], scalar1=sc, scalar2=None, op0=ALU.mult),
            ["cols"], [("DC", dst)])
    dc_all = [("DC", d) for d in (D_CFWH, D_SCWH, D_GQ8, D_LNGH, D_LNBH)]

    jobs = []
    for l in range(NL):
        for c in range(26):
            jobs.append((wfm_d[l, c].rearrange("p a b -> p (a b)"), wfm_b[l, c], 1024, ("wsc", l, c)))
        for m in range(8):
            jobs.append((wo_d[l, m].rearrange("p a b -> p (a b)"), wo_b[l, m], 1024, ("wsc", l, 26 + m)))
        for k in range(4):
            jobs.append((wv_d[l][:, 2 * k:2 * k + 2, :].rearrange("p a b -> p (a b)"), wv_b[l][:, 1024 * k:1024 * (k + 1)],
                         1024, ("wvsc", l, k)))
        jobs.append((pw_d[l].rearrange("p a b -> p (a b)"), pw_b[l], 512, ("pwsc", l)))
    WRflat = [w.reshape([128, 1024]) for w in WR]
    cast_rot = ["pool", "act", "dve"]
    NST = NFT // 2

    def emit_load(n):
        src, dst, ne, key = jobs[n]
        j = n % NST
        P.add("sp", lambda: nc.sync.dma_start(out=FTflat[:, j, 0:ne], in_=src), [], [("FT", 2 * j), ("FT", 2 * j + 1)], dma=True)

    def emit_cast_store(n):
        src, dst, ne, key = jobs[n]
        j = n % NST
        slot = n % NSLOT
        en = cast_rot[n % 3]
        o_ = WRflat[slot][:, 0:ne]
        i_ = FTflat[:, j, 0:ne]
        if en == "pool":
            fn = lambda: G_.tensor_copy(out=o_, in_=i_)
        elif en == "act":
            fn = lambda: S_.copy(out=o_, in_=i_)
        else:
            fn = lambda: V_.tensor_copy(out=o_, in_=i_)
        P.add(en, fn, [("FT", 2 * j), ("FT", 2 * j + 1)], [("WR", slot)])
        P.add("sp", lambda: nc.sync.dma_start(out=dst, in_=o_), [("WR", slot)], [key], dma=True)

    NJ0 = 39 if (NL > 1 and NBLK >= 2) else len(jobs)
    PRE = min(3, NST - 1)
    for n in range(NJ0 + PRE):
        if n < NJ0:
            emit_load(n)
        if n - PRE >= 0:
            emit_cast_store(n - PRE)

    Vf32 = V.bitcast(F32).reshape([128, NTT * 512])
    Vb16 = V.reshape([128, NTT * 1024])
    stg_f = Vf32[:, (NTT - 4) * 512:(NTT - 4) * 512 + 1024]
    stg_fk = [("V", NTT - 4), ("V", NTT - 3)]
    stg_b = [Vb16[:, (NTT - 2) * 1024:(NTT - 1) * 1024], Vb16[:, (NTT - 1) * 1024:NTT * 1024]]
    stg_bk = [[("V", NTT - 2)], [("V", NTT - 1)]]

    def late_gen():
        late = jobs[NJ0:]

        def A(n):
            src, dst, ne, key = late[n]
            P.add("sp", lambda: nc.sync.dma_start(out=stg_f[:, 0:ne], in_=src), [], stg_fk, dma=True)

        def B(n):
            src, dst, ne, key = late[n]
            P.add("act", lambda: S_.copy(out=stg_b[n % 2][:, 0:ne], in_=stg_f[:, 0:ne]), stg_fk, stg_bk[n % 2])

        def C(n):
            src, dst, ne, key = late[n]
            P.add("sp", lambda: nc.sync.dma_start(out=dst, in_=stg_b[n % 2][:, 0:ne]), stg_bk[n % 2], [key], dma=True)

        if not late:
            return
        A(0)
        yield
        for n in range(len(late)):
            B(n)
            yield
            if n + 1 < len(late):
                A(n + 1)
                yield
            C(n)
            yield
        P.add("pool", lambda: G_.memset(V[:, NTT - 4:NTT, :, 64:128], 1.0), [], [("V", tt) for tt in range(NTT - 4, NTT)])

    lstate = dict(gen=late_gen(), active=False, done=False)

    def late_step(n=1):
        if not lstate["active"] or lstate["done"]:
            return
        for _ in range(n):
            try:
                next(lstate["gen"])
            except StopIteration:
                lstate["done"] = True
                return

    def late_flush():
        if lstate["done"]:
            return
        for _ in lstate["gen"]:
            pass
        lstate["done"] = True

    chunks = []
    for s in range(NSEQ):
        sb_ = [(l, b) for l in range(NL) for b in range(NBLK)]
        for n_, (l, b) in enumerate(sb_):
            if n_ == 0:
                for c in range(4):
                    chunks.append((wfm_b[l, c], ("wsc", l, c)))
            for c in range(4, 26):
                chunks.append((wfm_b[l, c], ("wsc", l, c)))
            if n_ + 1 < len(sb_):
                l2 = sb_[n_ + 1][0]
                for c in range(4):
                    chunks.append((wfm_b[l2, c], ("wsc", l2, c)))
            for m in range(8):
                chunks.append((wo_b[l, m], ("wsc", l, 26 + m)))
    cstate = dict(cons=0, issue=0)

    def get_chunk():
        while cstate["issue"] < min(len(chunks), cstate["cons"] + NSLOT):
            i = cstate["issue"]
            slot = i % NSLOT
            P.add("sp", lambda slot=slot, i=i: nc.sync.dma_start(out=WRflat[slot][:, :], in_=chunks[i][0]),
                  [chunks[i][1]], [("WR", slot)], dma=True)
            cstate["issue"] += 1
        slot = cstate["cons"] % NSLOT
        cstate["cons"] += 1
        late_step()
        return WR[slot], ("WR", slot)

    def proj_chunk(pp=None, rot=None):
        W, kW = get_chunk()
        pa, ka = (rot or acc).get()
        pp = hctx["p"] if pp is None else pp
        for kc in range(8):
            mm(pa[:, :], W[:, kc, :], HMs[pp][:, kc, :], kc == 0, kc == 7,
               [kW, ("HM", pp, kc, 0), ("HM", pp, kc, 1)], [ka])
        return pa, ka

    cv = dict(key=None, gen=None, U1=None)

    def make_conv(lt):
        U1 = []

        def conv_gen():
            for i in (0, 1):
                AA, kAA = ft.alloc()
                AB, kAB = ft.alloc()
                accs = [(AA, kAA), (AB, kAB)]
                for k in range(31):
                    At, kAt = accs[k % 2]
                    wc = DC[:, lt, D_CFWH + i * 31 + k:D_CFWH + i * 31 + k + 1]
                    src = U0[:, i, k:k + 512]
                    rk = [("U0", i), ("U0h", i), ("DC", D_CFWH)]
                    if k < 2:
                        P.add("dve", lambda At=At, src=src, wc=wc: V_.tensor_scalar(
                            out=At[:, :], in0=src, scalar1=wc, scalar2=None, op0=ALU.mult), rk, [kAt])
                    else:
                        P.add("dve", lambda At=At, src=src, wc=wc: V_.scalar_tensor_tensor(
                            out=At[:, :], in0=src, scalar=wc, in1=At[:, :], op0=ALU.mult, op1=ALU.add), rk + [kAt], [kAt])
                    yield
                P.add("dve", lambda AA=AA, AB=AB, i=i: V_.scalar_tensor_tensor(
                    out=AA[:, :], in0=AA[:, :], scalar=cols[:, lt, C_CFB + i:C_CFB + i + 1], in1=AB[:, :],
                    op0=ALU.add, op1=ALU.add), [kAA, kAB, "cols"], [kAA])
                ft.release(kAB)
                U1.append((AA, kAA))
                P.add("dve", lambda i=i: V_.tensor_copy(out=U0[:, i, 0:30], in_=U0[:, i, 512:542]), [("U0", i)], [("U0h", i)])
                yield

        return conv_gen(), U1

    def drain(gen, n):
        if gen is None:
            return
        for _ in range(n):
            try:
                next(gen)
            except StopIteration:
                return

    def emit_cf_pair(i, pp, rot):
        pa, ka = proj_chunk(pp, rot)
        pg, kg = proj_chunk(pp, rot)
        P.add("act", lambda: S_.activation(out=U0[:, i, 30:542], in_=pg[:, :], func=AF.Tanh, scale=0.5), [kg], [("U0", i)])
        P.add("dve", lambda: V_.scalar_tensor_tensor(
            out=U0[:, i, 30:542], in0=U0[:, i, 30:542], scalar=1.0, in1=pa[:, :], op0=ALU.add, op1=ALU.mult),
            [("U0", i), ka], [("U0", i)])


    ev = dict(i=0)

    def evac_copy(out, in_, reads, writes):
        ev["i"] += 1
        if ev["i"] % 2 == 0:
            P.add("act", lambda: S_.copy(out=out, in_=in_), reads, writes)
        else:
            P.add("dve", lambda: V_.tensor_copy(out=out, in_=in_), reads, writes)

    deferred = []

    def flush_deferred(n=100):
        while deferred and n > 0:
            fn, rd, wr = deferred.pop(0)
            P.add("sp", fn, rd, wr, dma=True)
            n -= 1

    def emit_A(l, b, p):
        HM = HMs[p]
        t0 = b * 512
        tts = [4 * b + i for i in range(4)]
        blk = slice(t0, t0 + 512)
        xk = lambda kc: [("xT", kc, tt) for tt in tts]
        hk = lambda kc: [("HM", p, kc, 0), ("HM", p, kc, 1)]
        P.add("act", lambda: S_.activation(out=HM[:, :, :], in_=xT[:, :, blk], func=AF.Square),
              [k for kc in range(8) for k in xk(kc)], [k for kc in range(8) for k in hk(kc)])
        pst, kst = stat.get()
        for kc in range(8):
            mm(pst[:, :], onesb[:, :], HM[:, kc, :], kc == 0, kc == 7, hk(kc) + ["onesb"], [kst])
        RS, kRS = ft.alloc()
        P.add("act", lambda: S_.activation(out=RS[:, :], in_=pst[:, :], func=AF.Ln, bias=EPS, scale=1.0 / 1024),
              [kst], [kRS])
        P.add("act", lambda: S_.activation(out=RS[:, :], in_=RS[:, :], func=AF.Exp, scale=-0.5), [kRS], [kRS])
        for kc in range(8):
            P.add("dve", lambda kc=kc: V_.scalar_tensor_tensor(
                out=HM[:, kc, :], in0=xT[:, kc, blk], scalar=cols[:, l, C_G + kc:C_G + kc + 1], in1=RS[:, :],
                op0=ALU.mult, op1=ALU.mult), xk(kc) + [kRS, "cols"], hk(kc))
        ft.release(kRS)

    def block(s, l, b, last_layer, p, nxt):
        hctx["p"] = p
        HM = HMs[p]
        XS = XSs[p]
        XSK = XSKs[p]
        t0 = b * 512
        tts = [4 * b + i for i in range(4)]
        blk = slice(t0, t0 + 512)
        xk = lambda kc: [("xT", kc, tt) for tt in tts]

        if cv["key"] == (s, l, b):
            cg, U1 = cv["gen"], cv["U1"]
        else:
            if b == 0:
                P.add("pool", lambda: G_.memset(U0[:, :, 0:30], 0.0), [], [("U0h", 0), ("U0h", 1)])
            for i in (0, 1):
                emit_cf_pair(i, p, acc)
            cg, U1 = make_conv(l)
        cv["key"] = None

        def drain_conv(n):
            drain(cg, n)

        def tanh_gate(pz, kz, out_ap, wkeys):
            P.add("act", lambda: S_.activation(out=out_ap, in_=pz[:, :], func=AF.Silu), [kz], wkeys)

        an = {}

        def a_sq():
            if nxt is None:
                return
            l2, b2 = nxt
            HM2 = HMs[1 - p]
            blk2 = slice(b2 * 512, b2 * 512 + 512)
            tts2 = [4 * b2 + i for i in range(4)]
            xk2 = lambda kc: [("xT", kc, tt) for tt in tts2]
            hk2 = lambda kc: [("HM", 1 - p, kc, 0), ("HM", 1 - p, kc, 1)]
            an["c"] = (l2, HM2, blk2, xk2, hk2)

        def a_sq_part(q):
            if nxt is None:
                return
            l2, HM2, blk2, xk2, hk2 = an["c"]
            for kc in (2 * q, 2 * q + 1):
                P.add("act", lambda kc=kc: S_.activation(out=HM2[:, kc, :], in_=xT[:, kc, blk2], func=AF.Square),
                      xk2(kc), hk2(kc))

        def a_mm():
            if nxt is None:
                return
            l2, HM2, blk2, xk2, hk2 = an["c"]
            pst, kst = stat.get()
            for kc in range(8):
                mm(pst[:, :], onesb[:, :], HM2[:, kc, :], kc == 0, kc == 7, hk2(kc) + ["onesb"], [kst])
            an["pst"] = (pst, kst)

        def a_rs():
            if nxt is None:
                return
            pst, kst = an["pst"]
            RS, kRS = ft.alloc()
            P.add("act", lambda: S_.activation(out=RS[:, :], in_=pst[:, :], func=AF.Ln, bias=EPS, scale=1.0 / 1024),
                  [kst], [kRS])
            P.add("act", lambda: S_.activation(out=RS[:, :], in_=RS[:, :], func=AF.Exp, scale=-0.5), [kRS], [kRS])
            an["rs"] = (RS, kRS)

        def a_ht():
            if nxt is None:
                return
            l2, HM2, blk2, xk2, hk2 = an["c"]
            RS, kRS = an["rs"]
            for kc in range(8):
                P.add("dve", lambda kc=kc: V_.scalar_tensor_tensor(
                    out=HM2[:, kc, :], in0=xT[:, kc, blk2], scalar=cols[:, l2, C_G + kc:C_G + kc + 1], in1=RS[:, :],
                    op0=ALU.mult, op1=ALU.mult), xk2(kc) + [kRS, "cols"], hk2(kc))
            ft.release(kRS)

        pF, kF = ps[7], ("ps", 7)
        for i, tt in enumerate(tts):
            pv, kv = misc.get()
            for kc in range(8):
                mm(pv[:, :], HM[:, kc, i * 128:(i + 1) * 128], WVv[:, kc, :], kc == 0, kc == 7, hmk(kc) + ["WVv"], [kv])
            for kc in range(8):
                mm(pF[:, i * 8:(i + 1) * 8], HM[:, kc, i * 128:(i + 1) * 128], WVf[:, kc, :], kc == 0, kc == 7,
                   hmk(kc) + ["WVf"], [kF])
            evac_copy(V[:, tt, :, 0:64], pv[:, :].rearrange("p (h d) -> p h d", h=8), [kv], [("V", tt)])
        FZ, kFZ = ft.alloc()
        P.add("dve", lambda: V_.tensor_tensor(out=FZ[:, 0:32], in0=pF[:, 0:32], in1=bfb[:, l, :], op=ALU.add),
              [kF, "bfb"], [kFZ])
        P.add("act", lambda: S_.activation(out=FZ[:, 0:32], in_=FZ[:, 0:32], func=AF.Exp, scale=-1.0), [kFZ], [kFZ])
        P.add("act", lambda: S_.activation(out=FZ[:, 0:32], in_=FZ[:, 0:32], func=AF.Ln, bias=1.0), [kFZ], [kFZ])
        cdma_w, cdma_r = [], []

        def flush_cdma(lst):
            while lst:
                it = lst.pop(0)
                if callable(it):
                    it()
                else:
                    P.add("sp", it[0], it[1], it[2], dma=True)

        def fpath_tail():
            pC, kC = ps[6], ("ps", 6)
            for i in range(4):
                for j in range(i + 1):
                    rhs = negtri_f if j == i else nonesf[:, :]
                    mm(pC[0:8, i * 128:(i + 1) * 128], FZ[:, j * 8:(j + 1) * 8], rhs, j == 0, j == i,
                       [kFZ, "cstf", "nonesf"], [kC])
            CC, kCC = ft.alloc()
            P.add("dve", lambda: V_.tensor_scalar(out=CC[0:8, :], in0=pC[0:8, :], scalar1=CB[0:8, 0:1], scalar2=None,
                                                  op0=ALU.add), [kC, "CB"], [kCC])
            ft.release(kFZ)
            P.add("dve", lambda: V_.tensor_copy(out=CB[0:8, 0:1], in_=CC[0:8, 511:512]), [kCC], ["CB"])
            HI, kHI = bt.alloc()
            MID, kMID = bt.alloc()
            LO, kLO = bt.alloc()
            R1, kR1 = ft.alloc()
            R2, kR2 = ft.alloc()
            P.add("dve", lambda: V_.tensor_copy(out=HI[0:8, :], in_=CC[0:8, :]), [kCC], [kHI])
            P.add("dve", lambda: V_.tensor_tensor(out=R1[0:8, :], in0=CC[0:8, :], in1=HI[0:8, :], op=ALU.subtract),
                  [kCC, kHI], [kR1])
            P.add("dve", lambda: V_.tensor_copy(out=MID[0:8, :], in_=R1[0:8, :]), [kR1], [kMID])
            P.add("dve", lambda: V_.tensor_tensor(out=R2[0:8, :], in0=R1[0:8, :], in1=MID[0:8, :], op=ALU.subtract),
                  [kR1, kMID], [kR2])
            P.add("dve", lambda: V_.tensor_copy(out=LO[0:8, :], in_=R2[0:8, :]), [kR2], [kLO])
            ft.release(kCC)
            ft.release(kR1)
            ft.release(kR2)
            sc = scr_d[b % 2]
            for j, (tile_, key_) in enumerate(((HI, kHI), (MID, kMID), (LO, kLO))):
                cdma_w.append((lambda j=j, tile_=tile_: nc.sync.dma_start(out=sc[:, j, :], in_=tile_[0:8, :]),
                               [key_], [("scr", b % 2, j)]))
            cdma_w.append(lambda: (bt.release(kHI), bt.release(kMID), bt.release(kLO)))
            scr_keys = [("scr", b % 2, j) for j in range(3)]
            cdma_r.append((lambda: nc.sync.dma_start(out=QA[64:67, :, :], in_=sc.rearrange("h j t -> j h t")),
                           scr_keys + [("QAc",)], [("QAaug",)]))
            cdma_r.append((lambda: nc.sync.dma_start(out=KA[67:70, :, blk], in_=sc.rearrange("h j t -> j h t")),
                           scr_keys + [("KAc",)], [("KAaug", b)]))

        def qk_s1(which, j):
            pa, ka = proj_chunk()
            SQ, kSQ = bt.alloc()
            P.add("act", lambda: S_.activation(out=SQ[:, :], in_=pa[:, :], func=AF.Square), [ka], [kSQ])
            return (which, j, pa, ka, SQ, kSQ)

        def qk_s2(st_):
            which, j, QR, kQR, SQ, kSQ = st_
            pm, km = stat.get()
            mm(pm[:, :], bdiag_b, SQ[:, :], True, True, [kSQ, "cstb"], [km])
            RQ, kRQ = ft.alloc()
            P.add("act", lambda: S_.activation(out=RQ[:, :], in_=pm[:, :], func=AF.Ln, bias=EPS, scale=1.0 / 64),
                  [km], [kRQ])
            P.add("act", lambda: S_.activation(out=RQ[:, :], in_=RQ[:, :], func=AF.Exp, scale=-0.5), [kRQ], [kRQ])
            for par in (0, 1):
                h = 2 * j + par
                r0 = 64 * par
                if which == "q":
                    dst = QA[0:64, h, :]
                    wkey = ("QA", h)
                    gc = DC[r0:r0 + 64, l, D_GQ8 + j:D_GQ8 + j + 1]
                    gk = ("DC", D_GQ8)
                else:
                    dst = KA[0:64, h, blk]
                    wkey = ("KA", h, b)
                    gc = cols[r0:r0 + 64, l, C_GK + j:C_GK + j + 1]
                    gk = "cols"
                P.add("dve", lambda dst=dst, gc=gc, r0=r0: V_.scalar_tensor_tensor(
                    out=dst, in0=QR[r0:r0 + 64, :], scalar=gc, in1=RQ[r0:r0 + 64, :], op0=ALU.mult, op1=ALU.mult),
                    [kQR, kRQ, gk], [wkey])
            ft.release(kRQ)
            bt.release(kSQ)

        pend = None
        for which, j in [("k", j) for j in range(4)] + [("q", j) for j in range(4)]:
            cur = qk_s1(which, j)
            if pend is not None:
                qk_s2(pend)
            pend = cur
            if which == "k" and j == 1:
                fpath_tail()
            drain_conv(1)

        flush_deferred()
        flush_cdma(cdma_w)
        for i in (0, 1):
            pz, kz = proj_chunk()
            if pend is not None:
                qk_s2(pend)
                pend = None
            tanh_gate(pz, kz, ZC[:, i, :], [("ZC", i)])
            drain_conv(1)
        for i in (0, 1):
            pc, kc_ = proj_chunk()
            px, kx = proj_chunk()
            CX, kCX = ft.alloc()
            P.add("act", lambda pc=pc, CX=CX: S_.copy(out=CX[:, :], in_=pc[:, :]), [kc_], [kCX])
            P.add("dve", lambda i=i, px=px, CX=CX: V_.tensor_tensor(out=S0[:, i, 2:514], in0=CX[:, :], in1=px[:, :], op=ALU.mult),
                  [kCX, kx], [("S0", i)])
            ft.release(kCX)
            drain_conv(2)
        flush_cdma(cdma_r)
        for i in (0, 1):
            pb, kb_ = proj_chunk()
            pz, kz = proj_chunk()
            T2, kT2 = ft.alloc()
            tanh_gate(pz, kz, T2[:, :], [kT2])
            P.add("dve", lambda i=i, pb=pb, T2=T2: V_.tensor_tensor(out=GS[:, i, :], in0=T2[:, :], in1=pb[:, :], op=ALU.mult),
                  [kT2, kb_], [("GS", i)])
            ft.release(kT2)
            drain_conv(2)
        a_sq()
        for j in range(4):
            pz, kz = proj_chunk()
            tanh_gate(pz, kz, ZA[:, j, :], [("ZA", j)])
            a_sq_part(j)
            drain_conv(1)
        a_mm()
        a_rs()

        ln = {}

        def ln1():
            pmu, kmu = stat.get()
            pm2, km2 = stat.get()
            for i in (0, 1):
                Tb, kTb = bt.alloc()
                Tq, kTq = bt.alloc()
                P.add("act", lambda i=i, Tb=Tb: S_.copy(out=Tb[:, :], in_=U1[i][0][:, :]), [U1[i][1]], [kTb])
                P.add("act", lambda i=i, Tq=Tq: S_.activation(out=Tq[:, :], in_=U1[i][0][:, :], func=AF.Square), [U1[i][1]], [kTq])
                mm(pmu[:, :], onesb[:, :], Tb[:, :], i == 0, i == 1, [kTb, "onesb"], [kmu])
                mm(pm2[:, :], onesb[:, :], Tq[:, :], i == 0, i == 1, [kTq, "onesb"], [km2])
                bt.release(kTb)
                bt.release(kTq)
            ln["st"] = (pmu, kmu, pm2, km2)

        def ln2a():
            pmu, kmu, pm2, km2 = ln["st"]
            MEAN, kME = ft.alloc()
            MSQ, kMS = ft.alloc()
            if b >= 2:
                P.add("dve", lambda: V_.tensor_scalar(out=MEAN[:, :], in0=pmu[:, :], scalar1=1.0 / 256, scalar2=None, op0=ALU.mult),
                      [kmu], [kME])
                P.add("dve", lambda: V_.tensor_tensor(out=MSQ[:, :], in0=MEAN[:, :], in1=MEAN[:, :], op=ALU.mult), [kME], [kMS])
            else:
                P.add("act", lambda: S_.activation(out=MEAN[:, :], in_=pmu[:, :], func=AF.Identity, scale=1.0 / 256), [kmu], [kME])
                P.add("act", lambda: S_.activation(out=MSQ[:, :], in_=pmu[:, :], func=AF.Square, scale=1.0 / 256), [kmu], [kMS])
            ln["m"] = (MEAN, kME, MSQ, kMS)

        def ln2b():
            pmu, kmu, pm2, km2 = ln["st"]
            MEAN, kME, MSQ, kMS = ln["m"]
            VAR, kVA = MSQ, kMS
            P.add("dve", lambda: V_.scalar_tensor_tensor(out=VAR[:, :], in0=pm2[:, :], scalar=1.0 / 256, in1=MSQ[:, :],
                                                         op0=ALU.mult, op1=ALU.subtract), [km2, kMS], [kVA])
            P.add("dve", lambda: V_.tensor_scalar(out=VAR[:, :], in0=VAR[:, :], scalar1=0.0, scalar2=EPS,
                                                  op0=ALU.max, op1=ALU.add), [kVA], [kVA])
            ln["v"] = (VAR, kVA)

        def ln2c():
            VAR, kVA = ln["v"]
            P.add("act", lambda: S_.activation(out=VAR[:, :], in_=VAR[:, :], func=AF.Ln), [kVA], [kVA])

        def ln2c2():
            VAR, kVA = ln["v"]
            P.add("act", lambda: S_.activation(out=VAR[:, :], in_=VAR[:, :], func=AF.Exp, scale=-0.5), [kVA], [kVA])

        def ln2d():
            MEAN, kME, MSQ, kMS = ln["m"]
            VAR, kVA = ln["v"]
            for i in (0, 1):
                XN, kXN = U1[i]
                P.add("dve", lambda XN=XN: V_.tensor_tensor(out=XN[:, :], in0=XN[:, :], in1=MEAN[:, :], op=ALU.subtract),
                      [kXN, kME], [kXN])
                P.add("dve", lambda XN=XN: V_.tensor_tensor(out=XN[:, :], in0=XN[:, :], in1=VAR[:, :], op=ALU.mult),
                      [kXN, kVA], [kXN])
            ft.release(kME)
            ft.release(kVA)

        def ln2e():
            pass

        def ln2f():
            SL = []
            for i in (0, 1):
                XN, kXN = U1[i]
                SLt, kSL = bt.alloc()
                P.add("act", lambda XN=XN, SLt=SLt, i=i: S_.activation(
                    out=SLt[:, :], in_=XN[:, :], func=AF.Silu, bias=cols[:, l, C_LNB + i:C_LNB + i + 1],
                    scale=cols[:, l, C_LNG + i:C_LNG + i + 1]), [kXN, "cols"], [kSL])
                ft.release(kXN)
                SL.append((SLt, kSL))
            ln["SL"] = SL

        def ln2g():
            pass

        def ln3a():
            SL = ln["SL"]
            pps = []
            for co in (0, 1):
                ppo, kpo = stat.get()
                for ci in (0, 1):
                    mm(ppo[:, :], PW[:, ci, co * 128:(co + 1) * 128], SL[ci][0][:, :], ci == 0, ci == 1, [SL[ci][1], "PW"], [kpo])
                pps.append((ppo, kpo))
            for i in (0, 1):
                bt.release(SL[i][1])
            ln["pp"] = pps

        def ln3b():
            for co in (0, 1):
                ppo, kpo = ln["pp"][co]
                P.add("dve", lambda co=co, ppo=ppo: V_.scalar_tensor_tensor(
                    out=HM[:, co, :], in0=ppo[:, :], scalar=1.0, in1=ZC[:, co, :], op0=ALU.mult, op1=ALU.mult),
                    [kpo, ("ZC", co)], hmk(co))

        def sc_stage(which=(0, 1)):
            for i in which:
                SA, kSA = ft.alloc()
                SB, kSB = ft.alloc()
                w = lambda k: DC[:, l, D_SCWH + i * 3 + k:D_SCWH + i * 3 + k + 1]
                rk = [("S0", i), ("S0h", i), ("DC", D_SCWH)]
                P.add("dve", lambda i=i, SA=SA, w0=w(0): V_.tensor_scalar(out=SA[:, :], in0=S0[:, i, 0:512], scalar1=w0, scalar2=None,
                                                                        op0=ALU.mult), rk, [kSA])
                P.add("dve", lambda i=i, SA=SA, SB=SB, w1=w(1): V_.scalar_tensor_tensor(
                    out=SB[:, :], in0=S0[:, i, 1:513], scalar=w1, in1=SA[:, :], op0=ALU.mult, op1=ALU.add), rk + [kSA], [kSB])
                P.add("dve", lambda i=i, SA=SA, SB=SB, w2=w(2): V_.scalar_tensor_tensor(
                    out=SA[:, :], in0=S0[:, i, 2:514], scalar=w2, in1=SB[:, :], op0=ALU.mult, op1=ALU.add), rk + [kSB], [kSA])
                P.add("dve", lambda i=i, SA=SA: V_.tensor_tensor(out=HM[:, 2 + i, :], in0=SA[:, :], in1=GS[:, i, :], op=ALU.mult),
                      [kSA, ("GS", i)], hmk(2 + i))
                P.add("dve", lambda i=i: V_.tensor_copy(out=S0[:, i, 0:2], in_=S0[:, i, 512:514]), [("S0", i)], [("S0h", i)])
                ft.release(kSA)
                ft.release(kSB)

        def ln1_act(which=(0, 1)):
            tl = ln.setdefault("tl", [])
            for i in which:
                _, kT = ft.alloc()
                Tb = FTb16[:, kT[1], 0:512]
                Tq = FTb16[:, kT[1], 512:1024]
                if b >= 2:
                    P.add("dve", lambda i=i, Tb=Tb: V_.tensor_copy(out=Tb, in_=U1[i][0][:, :]), [U1[i][1]], [kT])
                    P.add("dve", lambda i=i, Tq=Tq: V_.tensor_tensor(out=Tq, in0=U1[i][0][:, :], in1=U1[i][0][:, :], op=ALU.mult),
                          [U1[i][1], kT], [kT])
                else:
                    P.add("act", lambda i=i, Tb=Tb: S_.copy(out=Tb, in_=U1[i][0][:, :]), [U1[i][1]], [kT])
                    P.add("act", lambda i=i, Tq=Tq: S_.activation(out=Tq, in_=U1[i][0][:, :], func=AF.Square), [U1[i][1], kT], [kT])
                tl.append((Tb, Tq, kT))

        def ln1_mm():
            pmu, kmu = stat.get()
            pm2, km2 = stat.get()
            for i, (Tb, Tq, kT) in enumerate(ln["tl"]):
                mm(pmu[:, :], onesb[:, :], Tb, i == 0, i == 1, [kT, "onesb"], [kmu])
                mm(pm2[:, :], onesb[:, :], Tq, i == 0, i == 1, [kT, "onesb"], [km2])
                ft.release(kT)
            ln["st"] = (pmu, kmu, pm2, km2)

        def taps6():
            drain_conv(6)

        def taps_all():
            drain_conv(1000)

        def pf_halo():
            if nxt is not None and nxt[1] == 0:
                P.add("pool", lambda: G_.memset(U0[:, :, 0:30], 0.0), [], [("U0h", 0), ("U0h", 1)])

        def pf_cf0():
            if nxt is not None:
                emit_cf_pair(0, 1 - p, stat)

        def pf_cf1():
            if nxt is not None:
                emit_cf_pair(1, 1 - p, stat)

        def pf_conv():
            if nxt is not None:
                g_, u_ = make_conv(nxt[0])
                cv["key"] = (s, nxt[0], nxt[1])
                cv["gen"] = g_
                cv["U1"] = u_

        NSL = 32
        slot_fns = {0: [lambda: sc_stage((0,)), taps6], 1: [taps6], 2: [lambda: sc_stage((1,)), taps6], 3: [taps6], 4: [taps6], 5: [taps6],
                    6: [a_ht, taps6], 7: [taps_all], 8: [lambda: ln1_act((0,))], 9: [lambda: ln1_act((1,))], 10: [ln1_mm], 12: [ln2a], 13: [ln2b], 15: [ln2c], 16: [ln2c2],
                    17: [ln2d], 19: [ln2e, pf_halo], 20: [pf_cf0, ln2f], 22: [pf_cf1], 23: [ln2g], 25: [ln3a],
                    27: [ln3b], 28: [pf_conv]}

        steps = [(h, kb) for h in range(8) for kb in range(4 * b + 4)]
        nsteps = len(steps)
        stb = [(ps[i], ("ps", i)) for i in range(4)]
        pts = {}
        nrm = {}

        def emit_score(idx):
            h, kb = steps[idx]
            st, kst_ = stb[idx % 4]
            i = kb - 4 * b
            n0 = 128 * i if i > 0 else 0
            rd = [("KA", h, kb // 4), ("KAaug", kb // 4), ("KAc",), ("QA", h), ("QAaug",), ("QAc",)]
            mm(st[:, n0:512], KA[0:70, h, kb * 128:(kb + 1) * 128], QA[0:70, h, n0:512], True, i < 0, rd, [kst_])
            if i >= 0:
                mm(st[:, n0:n0 + 128], ident_b, maskneg_b, False, True, ["cstb"], [kst_])
            PTt, kPT = bt.alloc()
            P.add("act", lambda: S_.activation(out=PTt[:, n0:512], in_=st[:, n0:512], func=AF.Exp), [kst_], [kPT])
            pts[idx] = (PTt, kPT, n0)

        def n1(h):
            po, kpo = ps[4 + h % 2], ("ps", 4 + h % 2)
            if h % 2 == 0:
                NUM, kNU = ft.alloc()
                DEN, kDE = ft.alloc()
                nrm["pair"] = (NUM, kNU, DEN, kDE)
                if b >= 2:
                    P.add("dve", lambda: V_.tensor_copy(out=NUM[0:64, :], in_=po[0:64, :]), [kpo], [kNU])
                else:
                    P.add("act", lambda: S_.copy(out=NUM[0:64, :], in_=po[0:64, :]), [kpo], [kNU])
                P.add("act", lambda: S_.activation(out=DEN[0:64, :], in_=po[64:128, :], func=AF.Ln), [kpo], [kDE])
            else:
                NUM, kNU, DEN, kDE = nrm["pair"]
                P.add("dve", lambda: V_.tensor_copy(out=NUM[64:128, :], in_=po[0:64, :]), [kpo], [kNU])
                P.add("act", lambda: S_.activation(out=DEN[64:128, :], in_=po[64:128, :], func=AF.Ln), [kpo], [kDE])
                nrm[h // 2] = (NUM, kNU, DEN, kDE)

        def n2(j):
            NUM, kNU, DEN, kDE = nrm[j]
            P.add("act", lambda: S_.activation(out=DEN[:, :], in_=DEN[:, :], func=AF.Exp, scale=-1.0), [kDE], [kDE])
            P.add("dve", lambda: V_.tensor_tensor(out=NUM[:, :], in0=NUM[:, :], in1=DEN[:, :], op=ALU.mult), [kNU, kDE], [kNU])

        def n3(j):
            NUM, kNU, DEN, kDE = nrm.pop(j)
            P.add("dve", lambda: V_.tensor_tensor(out=HM[:, 4 + j, :], in0=NUM[:, :], in1=ZA[:, j, :], op=ALU.mult),
                  [kNU, ("ZA", j)], [("HM", p, 4 + j, 0), ("HM", p, 4 + j, 1)])
            ft.release(kNU)
            ft.release(kDE)

        def emit_pv(idx):
            h, kb = steps[idx]
            PTt, kPT, n0 = pts.pop(idx)
            po, kpo = ps[4 + h % 2], ("ps", 4 + h % 2)
            last = kb == 4 * b + 3
            mm(po[:, n0:512], V[:, kb, h, :], PTt[:, n0:512], kb == 0, last, [("V", kb), kPT], [kpo])
            bt.release(kPT)
            if last:
                n1(h)
                if h % 2 == 0 and h >= 2:
                    n2(h // 2 - 1)
                if h % 2 == 1 and h >= 3:
                    n3(h // 2 - 1)

        fired = set()
        for idx in range(nsteps + LA):
            if idx < nsteps:
                emit_score(idx)
            j = idx - LA
            if j >= 0:
                emit_pv(j)
                k = ((j + 1) * NSL) // nsteps - 1
                for kk in range(k + 1):
                    if kk not in fired and (kk + 1) * nsteps <= (j + 1) * NSL:
                        fired.add(kk)
                        late_step()
                        for fn_ in slot_fns.get(kk, ()):
                            fn_()
        n2(3)
        n3(3)
        for kk in range(NSL):
            if kk not in fired:
                fired.add(kk)
                for fn_ in slot_fns.get(kk, ()):
                    fn_()

        for m in range(8):
            pa, ka = proj_chunk()
            P.add("dve", lambda m=m, pa=pa: V_.tensor_tensor(out=xT[:, m, blk], in0=xT[:, m, blk], in1=pa[:, :], op=ALU.add),
                  xk(m) + [ka], xk(m))
            if cv["key"] is not None:
                drain(cv["gen"], 2)

        if last_layer:
            for i, tt in enumerate(tts):
                XO = XS[i % 2]
                if i >= 2:
                    flush_deferred(1)
                for half in (0, 1):
                    m0 = 4 * half
                    pt_, kpt = misc.get()
                    for q in range(4):
                        P.add("pe", lambda q=q, pt_=pt_, m0=m0, tt=tt: T_.transpose(
                            pt_[:, q * 128:(q + 1) * 128], xT[:, m0 + q, tt * 128:(tt + 1) * 128], ident_f),
                            [("xT", m0 + q, tt), "cstf"], [kpt])
                    evac_copy(XO[:, m0 * 128:(m0 + 4) * 128], pt_[:, :], [kpt], XSK[i % 2][4 * half * 2:(4 * half + 4) * 2] if False else [("XO", p, i % 2, half)] + XSK[i % 2])
                deferred.append((lambda XO=XO, tt=tt, s=s: nc.sync.dma_start(out=y_d[s, tt * 128:(tt + 1) * 128, :], in_=XO),
                                 [("XO", p, i % 2, 0), ("XO", p, i % 2, 1)] + XSK[i % 2], [("y", s, tt)] + XSK[i % 2]))

    for s in range(NSEQ):
        flush_deferred()
        for tt in range(NTT):
            XI = XSs[0][tt % 2]
            XIK = XSKs[0][tt % 2]
            P.add("sp", lambda XI=XI, tt=tt, s=s: nc.sync.dma_start(out=XI, in_=x_d[s, tt * 128:(tt + 1) * 128, :]),
                  [], XIK, dma=True)
            for half in (0, 1):
                m0 = 4 * half
                pt_, kpt = misc.get()
                for q in range(4):
                    P.add("pe", lambda q=q, pt_=pt_, XI=XI, m0=m0: T_.transpose(
                        pt_[:, q * 128:(q + 1) * 128], XI[:, (m0 + q) * 128:(m0 + q + 1) * 128], ident_f),
                        XIK + ["cstf"], [kpt])
                evac_copy(xT[:, m0:m0 + 4, tt * 128:(tt + 1) * 128], pt_[:, :].rearrange("p (a b) -> p a b", a=4),
                          [kpt], [("xT", m0 + q, tt) for q in range(4)])
        seqblocks = [(l, b) for l in range(NL) for b in range(NBLK)]
        emit_A(0, 0, 0)
        for n_, (l, b) in enumerate(seqblocks):
            if b == 0:
                P.add("sp", lambda l=l: nc.sync.dma_start(out=WVv.reshape([128, 4096])[:, :], in_=wv_b[l]),
                      [("wvsc", l, k) for k in range(4)], ["WVv"], dma=True)
                P.add("sp", lambda l=l: nc.sync.dma_start(out=PW.reshape([128, 512])[:, :], in_=pw_b[l]), [("pwsc", l)], ["PW"], dma=True)
                P.add("sp", lambda l=l: nc.sync.dma_start(out=WFs[:, :], in_=wf_d[l]), [], ["WFs"], dma=True)
                P.add("dve", lambda: V_.tensor_copy(out=WVf.reshape([128, 64])[:, :], in_=WFs[:, :]), ["WFs"], ["WVf"])
                P.add("pool", lambda: G_.memset(CB[:, :], 0.0), [], ["CB"])
                P.add("pool", lambda: G_.memset(S0[:, :, 0:2], 0.0), [], [("S0h", 0), ("S0h", 1)])
            nxt = seqblocks[n_ + 1] if n_ + 1 < len(seqblocks) else None
            if s == 0 and l == 0 and b <= NBLK - 2:
                lstate["active"] = True
            elif not lstate["done"]:
                lstate["active"] = True
                late_flush()
            block(s, l, b, l == NL - 1, n_ % 2, nxt)
            lstate["active"] = False

    flush_deferred()
    ykeys = [("y", s, tt) for s in range(NSEQ) for tt in range(NTT)]
    P.add("sp", lambda: None, ykeys, [])
    return nc, P


def prep_shared(inputs, NL):
    w_in = np.asarray(inputs["w_in"], np.float32)[:NL]
    w_out = np.asarray(inputs["w_out"], np.float32)[:NL]
    order = chunk_order()
    wfm = np.empty((NL, 26, 128, 8, 128), np.float32)
    for c, (kind, idx) in enumerate(order):
        c0 = _COL0[kind] + idx * 128
        blk = w_in[:, :, c0:c0 + 128].reshape(NL, 8, 128, 128)
        wfm[:, c] = blk.transpose(0, 2, 1, 3)
    wv = np.ascontiguousarray(w_in[:, :, 2816:3328].reshape(NL, 8, 128, 512).transpose(0, 2, 1, 3))
    wf = np.ascontiguousarray(w_in[:, :, 3840:3848].reshape(NL, 8, 128, 8).transpose(0, 2, 1, 3)).reshape(NL, 128, 64)
    wo = w_out.reshape(NL, 8, 128, 8, 128).transpose(0, 3, 2, 1, 4)
    pw = np.asarray(inputs["cf_pw"], np.float32)[:NL].reshape(NL, 2, 128, 256).transpose(0, 2, 1, 3)
    cols = np.zeros((128, NL, NCOL), np.float32)
    g = np.asarray(inputs["norm_g"], np.float32)[:NL]
    cols[:, :, C_G:C_G + 8] = g.reshape(NL, 8, 128).transpose(2, 0, 1)
    for name, c0 in (("cf_dw_b", C_CFB), ("cf_ln_g", C_LNG), ("cf_ln_b", C_LNB)):
        a = np.asarray(inputs[name], np.float32)[:NL]
        cols[:, :, c0:c0 + 2] = a.reshape(NL, 2, 128).transpose(2, 0, 1)
    cfw = np.asarray(inputs["cf_dw"], np.float32)[:NL]
    cols[:, :, C_CFW:C_CFW + 62] = cfw.reshape(NL, 31, 2, 128).transpose(3, 0, 2, 1).reshape(128, NL, 62)
    scw = np.asarray(inputs["sc_dw"], np.float32)[:NL]
    cols[:, :, C_SCW:C_SCW + 6] = scw.reshape(NL, 3, 2, 128).transpose(3, 0, 2, 1).reshape(128, NL, 6)
    for name, c0 in (("q_norm_g", C_GQ), ("k_norm_g", C_GK)):
        a = np.asarray(inputs[name], np.float32)[:NL]
        cols[:, :, c0:c0 + 4] = a.reshape(NL, 4, 128).transpose(2, 0, 1)
    bf = np.asarray(inputs["b_f"], np.float32)[:NL]
    bfb = np.broadcast_to(np.tile(bf, (1, 4))[None], (128, NL, 32)).copy()
    cst = np.zeros((128, 512), np.float32)
    ii = np.arange(128)
    cst[:, 0:128] = np.eye(128, dtype=np.float32)
    cst[:, 128:256] = -(ii[:, None] <= ii[None, :]).astype(np.float32)
    cst[:, 256:384] = (ii[:, None] // 64 == ii[None, :] // 64).astype(np.float32)
    cst[:, 384:512] = np.where(ii[:, None] <= ii[None, :], 0.0, -30000.0).astype(np.float32)
    return dict(wfm=np.ascontiguousarray(wfm), wv=wv, wf=wf, wo=np.ascontiguousarray(wo), pw=np.ascontiguousarray(pw),
                cols=cols, bfb=bfb, cst=cst)


_CACHE = {}


def run(inputs, NSEQ, NBLK, NL, n_cores, x_shards):
    key = (NSEQ, NBLK, NL)
    if key not in _CACHE:
        nc, P = build(NSEQ, NBLK, NL)
        with ExitStack() as stack:
            info = P.finalize(stack)
        _CACHE[key] = (nc, info)
    nc, info = _CACHE[key]
    shared = prep_shared(inputs, NL)
    in_maps = [dict(shared, x=np.ascontiguousarray(xs)) for xs in x_shards]
    res = run_bass_kernel_spmd(nc, in_maps, core_ids=list(range(n_cores)))
    return [r["y"] for r in res.results], info


def kernel(**inputs):
    x = np.asarray(inputs["x"], np.float32)
    n = 8
    shards = [x[2 * c:2 * c + 2] for c in range(n)]
    outs, _ = run(inputs, 2, 4, 2, n, shards)
    return np.concatenate(outs, axis=0).astype(np.float32)
```

```python
import numpy as np
from contextlib import ExitStack
import concourse.bass as bass
import concourse.mybir as mybir
from concourse.bass_utils import run_bass_kernel_spmd

F32 = mybir.dt.float32
BF16 = mybir.dt.bfloat16
AF = mybir.ActivationFunctionType
ALU = mybir.AluOpType

EPS = 1e-6
C_G, C_CFB, C_LNG, C_LNB, C_CFW, C_SCW, C_GQ, C_GK, NCOL = 0, 8, 10, 12, 14, 76, 82, 86, 90
D_CFWH, D_SCWH, D_GQ8, D_LNGH, D_LNBH, ND = 0, 62, 68, 72, 74, 76
NSLOT = 3
NFT = 8
NBT = 6
NDMASEM = 25
LA = 3


class Prog:
    ENGS = ["pe", "act", "dve", "pool", "sp"]

    def __init__(self, nc):
        self.nc = nc
        self.eng = {"pe": nc.tensor, "act": nc.scalar, "dve": nc.vector, "pool": nc.gpsimd, "sp": nc.sync}
        self.ops = []

    def add(self, eng, fn, reads=(), writes=(), dma=False):
        self.ops.append(dict(eng=eng, fn=fn, reads=list(reads), writes=list(writes), dma=dma))

    def finalize(self, stack):
        ops = self.ops
        last_w, readers = {}, {}
        for i, op in enumerate(ops):
            deps = set()
            for k in op["reads"]:
                if k in last_w:
                    deps.add(last_w[k])
            for k in op["writes"]:
                if k in last_w:
                    deps.add(last_w[k])
                for r in readers.get(k, ()):
                    deps.add(r)
            deps.discard(i)
            keep = set()
            for d in deps:
                dop = ops[d]
                if (not dop["dma"]) and dop["eng"] == "pe" and op["eng"] == "pe" and not op["dma"]:
                    continue
                keep.add(d)
            latest = {}
            kept2 = set()
            for d in keep:
                dop = ops[d]
                if dop["dma"]:
                    kept2.add(d)
                else:
                    latest[dop["eng"]] = max(latest.get(dop["eng"], -1), d)
            kept2 |= set(latest.values())
            op["deps"] = kept2
            for k in op["reads"]:
                readers.setdefault(k, []).append(i)
            for k in op["writes"]:
                last_w[k] = i
                readers[k] = []
        targets = set()
        for op in ops:
            targets |= op["deps"]
        cnt = {e: 0 for e in self.ENGS}
        dma_use = [0] * NDMASEM
        rr = 0
        for i, op in enumerate(ops):
            if op["dma"]:
                s = rr % NDMASEM
                rr += 1
                dma_use[s] += 1
                op["sem"] = ("dma", s)
                op["val"] = 16 * dma_use[s]
                op["prev"] = 16 * (dma_use[s] - 1)
            elif i in targets:
                cnt[op["eng"]] += 1
                op["sem"] = ("eng", op["eng"])
                op["val"] = cnt[op["eng"]]
            else:
                op["sem"] = None
        sems = {}

        def sem(key):
            if key not in sems:
                sems[key] = stack.enter_context(self.nc.semaphore("s_%s_%s" % key))
            return sems[key]

        waited = {e: {} for e in self.ENGS}
        nwait = 0
        for op in ops:
            e = op["eng"]
            E = self.eng[e]
            need = {}
            for d in op["deps"]:
                dop = ops[d]
                key = dop["sem"]
                need[key] = max(need.get(key, 0), dop["val"])
            if op["dma"] and op["prev"] > 0:
                need[op["sem"]] = max(need.get(op["sem"], 0), op["prev"])
            for key, val in need.items():
                if waited[e].get(key, 0) < val:
                    E.wait_ge(sem(key), val)
                    waited[e][key] = val
                    nwait += 1
            ins = op["fn"]()
            if op["sem"] is not None and ins is not None:
                ins.then_inc(sem(op["sem"]), 16 if op["dma"] else 1)
        return dict(n_ops=len(ops), n_wait=nwait, cnt=cnt)


class TPool:
    def __init__(self, tiles, name):
        self.tiles = tiles
        self.name = name
        self.free = list(range(len(tiles)))

    def alloc(self):
        assert self.free, "temp pool %s exhausted" % self.name
        i = self.free.pop(0)
        return self.tiles[i], (self.name, i)

    def release(self, key):
        assert key[0] == self.name and key[1] not in self.free
        self.free.append(key[1])


class Rot:
    def __init__(self, items):
        self.items = items
        self.i = 0

    def get(self):
        it = self.items[self.i % len(self.items)]
        self.i += 1
        return it


def chunk_order():
    o = []
    o += [("cfa", 0), ("cfg", 0), ("cfa", 1), ("cfg", 1)]
    o += [("k", j) for j in range(4)]
    o += [("q", j) for j in range(4)]
    o += [("cfz", 0), ("cfz", 1)]
    o += [("scC", 0), ("scx", 0), ("scC", 1), ("scx", 1)]
    o += [("scB", 0), ("scz", 0), ("scB", 1), ("scz", 1)]
    o += [("az", j) for j in range(4)]
    return o


_COL0 = {"cfa": 0, "cfg": 256, "cfz": 512, "scB": 768, "scC": 1024, "scx": 1280, "scz": 1536,
         "q": 1792, "k": 2304, "v": 2816, "az": 3328, "f": 3840}


def build(NSEQ, NBLK, NL):
    S = NBLK * 512
    NTT = NBLK * 4
    nc = bass.Bass("TRN2", target_bir_lowering=False)

    def dram(name, shape, dtype=F32, kind="ExternalInput"):
        return nc.dram_tensor(name, shape, dtype, kind=kind).ap()

    x_d = dram("x", [NSEQ, S, 1024])
    wfm_d = dram("wfm", [NL, 26, 128, 8, 128])
    wv_d = dram("wv", [NL, 128, 8, 512])
    wf_d = dram("wf", [NL, 128, 64])
    wo_d = dram("wo", [NL, 8, 128, 8, 128])
    pw_d = dram("pw", [NL, 128, 2, 256])
    cols_d = dram("cols", [128, NL, NCOL])
    bfb_d = dram("bfb", [128, NL, 32])
    cst_d = dram("cst", [128, 512])
    y_d = dram("y", [NSEQ, S, 1024], kind="ExternalOutput")
    scr_d = dram("scr", [2, 8, 3, 512], BF16, kind="Internal")
    wfm_b = dram("wfm_b", [NL, 26, 128, 1024], BF16, kind="Internal")
    wo_b = dram("wo_b", [NL, 8, 128, 1024], BF16, kind="Internal")
    wv_b = dram("wv_b", [NL, 128, 4096], BF16, kind="Internal")
    pw_b = dram("pw_b", [NL, 128, 512], BF16, kind="Internal")

    def A(name, shape, dtype):
        return nc.alloc_sbuf_tensor("sb_" + name, shape, dtype)

    xT = A("xT", [128, 8, S], F32)
    KA = A("KA", [128, 8, S], BF16)
    V = A("V", [128, NTT, 8, 128], BF16)
    QA = A("QA", [128, 8, 512], BF16)
    HMs = [A("HM0", [128, 8, 512], BF16), A("HM1", [128, 8, 512], BF16)]
    HMfs = [h_.bitcast(F32).reshape([128, 2, 1024]) for h_ in HMs]
    XSs = [[hf_[:, 0, :], hf_[:, 1, :]] for hf_ in HMfs]
    XSKs = [[[("HM", p_, kc, hf) for kc in range(4) for hf in (0, 1)], [("HM", p_, kc, hf) for kc in range(4, 8) for hf in (0, 1)]]
            for p_ in (0, 1)]
    WR = [A("WR%d" % i, [128, 8, 128], BF16) for i in range(NSLOT)]
    WVv = A("WVv", [128, 8, 512], BF16)
    WVf = A("WVf", [128, 8, 8], BF16)
    WFs = A("WFs", [128, 64], F32)
    PW = A("PW", [128, 2, 256], BF16)
    U0 = A("U0", [128, 2, 542], BF16)
    S0 = A("S0", [128, 2, 514], BF16)
    ZC = A("ZC", [128, 2, 512], BF16)
    GS = A("GS", [128, 2, 512], BF16)
    ZA = A("ZA", [128, 4, 512], BF16)
    FTbig = A("FT", [128, NFT, 512], F32)
    ft = TPool([FTbig[:, i, :] for i in range(NFT)], "FT")
    FTflat = FTbig.reshape([128, NFT // 2, 1024])
    FTb16 = FTbig.bitcast(BF16)
    bt = TPool([A("BT%d" % i, [128, 512], BF16) for i in range(NBT)], "BT")
    cols = A("cols", [128, NL, NCOL], F32)
    DC = A("DC", [128, NL, ND], F32)
    bfb = A("bfb", [128, NL, 32], F32)
    cstf = A("cstf", [128, 512], F32)
    cstb = A("cstb", [128, 512], BF16)
    onesb = A("onesb", [128, 128], BF16)
    nonesf = A("nonesf", [128, 128], F32)
    CB = A("CB", [128, 2], F32)
    ps = [nc.alloc_psum_tensor("ps%d" % i, [128, 512], F32) for i in range(8)]
    acc = Rot([(ps[i], ("ps", i)) for i in range(4)])
    misc = Rot([(ps[i], ("ps", i)) for i in (4, 5)])
    stat = Rot([(ps[i], ("ps", i)) for i in (6, 7)])

    ident_f = cstf[:, 0:128]
    negtri_f = cstf[:, 128:256]
    ident_b = cstb[:, 0:128]
    bdiag_b = cstb[:, 256:384]
    maskneg_b = cstb[:, 384:512]

    P = Prog(nc)
    V_ = nc.vector
    G_ = nc.gpsimd
    S_ = nc.scalar
    T_ = nc.tensor

    def mm(out, lhsT, rhs, start, stop, reads, writes):
        P.add("pe", lambda: T_.matmul(out, lhsT, rhs, start=start, stop=stop), reads, writes)

    hctx = dict(p=0)

    def hmk(kc):
        return [("HM", hctx["p"], kc, 0), ("HM", hctx["p"], kc, 1)]

    P.add("sp", lambda: nc.sync.dma_start(out=cols[:, :, :], in_=cols_d), [], ["cols"], dma=True)
    P.add("sp", lambda: nc.sync.dma_start(out=bfb[:, :, :], in_=bfb_d), [], ["bfb"], dma=True)
    P.add("sp", lambda: nc.sync.dma_start(out=cstf[:, :], in_=cst_d), [], ["cstf"], dma=True)
    P.add("dve", lambda: V_.tensor_copy(out=cstb[:, :], in_=cstf[:, :]), ["cstf"], ["cstb"])
    P.add("pool", lambda: G_.memset(onesb[:, :], 1.0), [], ["onesb"])
    P.add("pool", lambda: G_.memset(nonesf[:, :], -1.0), [], ["nonesf"])
    P.add("pool", lambda: G_.memset(V[:, :, :, :], 1.0), [], [("V", tt) for tt in range(NTT)])
    P.add("pool", lambda: G_.memset(QA[64:70, :, :], -1.0), [], [("QAc",)])
    P.add("pool", lambda: G_.memset(KA[64:70, :, :], 1.0), [], [("KAc",)])
    for (dst, src, n, sc) in ((D_CFWH, C_CFW, 62, 0.5), (D_SCWH, C_SCW, 6, 1.0), (D_GQ8, C_GQ, 4, 0.125),
                              (D_LNGH, C_LNG, 2, 0.5), (D_LNBH, C_LNB, 2, 0.5)):
        P.add("dve", lambda dst=dst, src=src, n=n, sc=sc: V_.tensor_scalar(
            out=DC[:, :, dst:dst + n], in0=cols[:, :, src:src + n], scalar1=sc, scalar2=None, op0=ALU.mult),
            ["cols"], [("DC", dst)])
    dc_all = [("DC", d) for d in (D_CFWH, D_SCWH, D_GQ8, D_LNGH, D_LNBH)]

    jobs = []
    for l in range(NL):
        for c in range(26):
            jobs.append((wfm_d[l, c].rearrange("p a b -> p (a b)"), wfm_b[l, c], 1024, ("wsc", l, c)))
        for m in range(8):
            jobs.append((wo_d[l, m].rearrange("p a b -> p (a b)"), wo_b[l, m], 1024, ("wsc", l, 26 + m)))
        for k in range(4):
            jobs.append((wv_d[l][:, 2 * k:2 * k + 2, :].rearrange("p a b -> p (a b)"), wv_b[l][:, 1024 * k:1024 * (k + 1)],
                         1024, ("wvsc", l, k)))
        jobs.append((pw_d[l].rearrange("p a b -> p (a b)"), pw_b[l], 512, ("pwsc", l)))
    WRflat = [w.reshape([128, 1024]) for w in WR]
    cast_rot = ["pool", "act", "dve"]
    NST = NFT // 2

    def emit_load(n):
        src, dst, ne, key = jobs[n]
        j = n % NST
        P.add("sp", lambda: nc.sync.dma_start(out=FTflat[:, j, 0:ne], in_=src), [], [("FT", 2 * j), ("FT", 2 * j + 1)], dma=True)

    def emit_cast_store(n):
        src, dst, ne, key = jobs[n]
        j = n % NST
        slot = n % NSLOT
        en = cast_rot[n % 3]
        o_ = WRflat[slot][:, 0:ne]
        i_ = FTflat[:, j, 0:ne]
        if en == "pool":
            fn = lambda: G_.tensor_copy(out=o_, in_=i_)
        elif en == "act":
            fn = lambda: S_.copy(out=o_, in_=i_)
        else:
            fn = lambda: V_.tensor_copy(out=o_, in_=i_)
        P.add(en, fn, [("FT", 2 * j), ("FT", 2 * j + 1)], [("WR", slot)])
        P.add("sp", lambda: nc.sync.dma_start(out=dst, in_=o_), [("WR", slot)], [key], dma=True)

    NJ0 = 39 if (NL > 1 and NBLK >= 2) else len(jobs)
    PRE = min(3, NST - 1)
    for n in range(NJ0 + PRE):
        if n < NJ0:
            emit_load(n)
        if n - PRE >= 0:
            emit_cast_store(n - PRE)

    Vf32 = V.bitcast(F32).reshape([128, NTT * 512])
    Vb16 = V.reshape([128, NTT * 1024])
    stg_f = Vf32[:, (NTT - 4) * 512:(NTT - 4) * 512 + 1024]
    stg_fk = [("V", NTT - 4), ("V", NTT - 3)]
    stg_b = [Vb16[:, (NTT - 2) * 1024:(NTT - 1) * 1024], Vb16[:, (NTT - 1) * 1024:NTT * 1024]]
    stg_bk = [[("V", NTT - 2)], [("V", NTT - 1)]]

    def late_gen():
        late = jobs[NJ0:]

        def A(n):
            src, dst, ne, key = late[n]
            P.add("sp", lambda: nc.sync.dma_start(out=stg_f[:, 0:ne], in_=src), [], stg_fk, dma=True)

        def B(n):
            src, dst, ne, key = late[n]
            P.add("act", lambda: S_.copy(out=stg_b[n % 2][:, 0:ne], in_=stg_f[:, 0:ne]), stg_fk, stg_bk[n % 2])

        def C(n):
            src, dst, ne, key = late[n]
            P.add("sp", lambda: nc.sync.dma_start(out=dst, in_=stg_b[n % 2][:, 0:ne]), stg_bk[n % 2], [key], dma=True)

        if not late:
            return
        A(0)
        yield
        for n in range(len(late)):
            B(n)
            yield
            if n + 1 < len(late):
                A(n + 1)
                yield
            C(n)
            yield
        P.add("pool", lambda: G_.memset(V[:, NTT - 4:NTT, :, 64:128], 1.0), [], [("V", tt) for tt in range(NTT - 4, NTT)])

    lstate = dict(gen=late_gen(), active=False, done=False)

    def late_step(n=1):
        if not lstate["active"] or lstate["done"]:
            return
        for _ in range(n):
            try:
                next(lstate["gen"])
            except StopIteration:
                lstate["done"] = True
                return

    def late_flush():
        if lstate["done"]:
            return
        for _ in lstate["gen"]:
            pass
        lstate["done"] = True

    chunks = []
    for s in range(NSEQ):
        sb_ = [(l, b) for l in range(NL) for b in range(NBLK)]
        for n_, (l, b) in enumerate(sb_):
            if n_ == 0:
                for c in range(4):
                    chunks.append((wfm_b[l, c], ("wsc", l, c)))
            for c in range(4, 26):
                chunks.append((wfm_b[l, c], ("wsc", l, c)))
            if n_ + 1 < len(sb_):
                l2 = sb_[n_ + 1][0]
                for c in range(4):
                    chunks.append((wfm_b[l2, c], ("wsc", l2, c)))
            for m in range(8):
                chunks.append((wo_b[l, m], ("wsc", l, 26 + m)))
    cstate = dict(cons=0, issue=0)

    def get_chunk():
        while cstate["issue"] < min(len(chunks), cstate["cons"] + NSLOT):
            i = cstate["issue"]
            slot = i % NSLOT
            P.add("sp", lambda slot=slot, i=i: nc.sync.dma_start(out=WRflat[slot][:, :], in_=chunks[i][0]),
                  [chunks[i][1]], [("WR", slot)], dma=True)
            cstate["issue"] += 1
        slot = cstate["cons"] % NSLOT
        cstate["cons"] += 1
        late_step()
        return WR[slot], ("WR", slot)

    def proj_chunk(pp=None, rot=None):
        W, kW = get_chunk()
        pa, ka = (rot or acc).get()
        pp = hctx["p"] if pp is None else pp
        for kc in range(8):
            mm(pa[:, :], W[:, kc, :], HMs[pp][:, kc, :], kc == 0, kc == 7,
               [kW, ("HM", pp, kc, 0), ("HM", pp, kc, 1)], [ka])
        return pa, ka

    cv = dict(key=None, gen=None, U1=None)

    def make_conv(lt):
        U1 = []

        def conv_gen():
            for i in (0, 1):
                AA, kAA = ft.alloc()
                AB, kAB = ft.alloc()
                accs = [(AA, kAA), (AB, kAB)]
                for k in range(31):
                    At, kAt = accs[k % 2]
                    wc = DC[:, lt, D_CFWH + i * 31 + k:D_CFWH + i * 31 + k + 1]
                    src = U0[:, i, k:k + 512]
                    rk = [("U0", i), ("U0h", i), ("DC", D_CFWH)]
                    if k < 2:
                        P.add("dve", lambda At=At, src=src, wc=wc: V_.tensor_scalar(
                            out=At[:, :], in0=src, scalar1=wc, scalar2=None, op0=ALU.mult), rk, [kAt])
                    else:
                        P.add("dve", lambda At=At, src=src, wc=wc: V_.scalar_tensor_tensor(
                            out=At[:, :], in0=src, scalar=wc, in1=At[:, :], op0=ALU.mult, op1=ALU.add), rk + [kAt], [kAt])
                    yield
                P.add("dve", lambda AA=AA, AB=AB, i=i: V_.scalar_tensor_tensor(
                    out=AA[:, :], in0=AA[:, :], scalar=cols[:, lt, C_CFB + i:C_CFB + i + 1], in1=AB[:, :],
                    op0=ALU.add, op1=ALU.add), [kAA, kAB, "cols"], [kAA])
                ft.release(kAB)
                U1.append((AA, kAA))
                P.add("dve", lambda i=i: V_.tensor_copy(out=U0[:, i, 0:30], in_=U0[:, i, 512:542]), [("U0", i)], [("U0h", i)])
                yield

        return conv_gen(), U1

    def drain(gen, n):
        if gen is None:
            return
        for _ in range(n):
            try:
                next(gen)
            except StopIteration:
                return

    def emit_cf_pair(i, pp, rot):
        pa, ka = proj_chunk(pp, rot)
        pg, kg = proj_chunk(pp, rot)
        P.add("act", lambda: S_.activation(out=U0[:, i, 30:542], in_=pg[:, :], func=AF.Tanh, scale=0.5), [kg], [("U0", i)])
        P.add("dve", lambda: V_.scalar_tensor_tensor(
            out=U0[:, i, 30:542], in0=U0[:, i, 30:542], scalar=1.0, in1=pa[:, :], op0=ALU.add, op1=ALU.mult),
            [("U0", i), ka], [("U0", i)])


    ev = dict(i=0)

    def evac_copy(out, in_, reads, writes):
        ev["i"] += 1
        if ev["i"] % 2 == 0:
            P.add("act", lambda: S_.copy(out=out, in_=in_), reads, writes)
        else:
            P.add("dve", lambda: V_.tensor_copy(out=out, in_=in_), reads, writes)

    deferred = []

    def flush_deferred(n=100):
        while deferred and n > 0:
            fn, rd, wr = deferred.pop(0)
            P.add("sp", fn, rd, wr, dma=True)
            n -= 1

    def emit_A(l, b, p):
        HM = HMs[p]
        t0 = b * 512
        tts = [4 * b + i for i in range(4)]
        blk = slice(t0, t0 + 512)
        xk = lambda kc: [("xT", kc, tt) for tt in tts]
        hk = lambda kc: [("HM", p, kc, 0), ("HM", p, kc, 1)]
        P.add("act", lambda: S_.activation(out=HM[:, :, :], in_=xT[:, :, blk], func=AF.Square),
              [k for kc in range(8) for k in xk(kc)], [k for kc in range(8) for k in hk(kc)])
        pst, kst = stat.get()
        for kc in range(8):
            mm(pst[:, :], onesb[:, :], HM[:, kc, :], kc == 0, kc == 7, hk(kc) + ["onesb"], [kst])
        RS, kRS = ft.alloc()
        P.add("act", lambda: S_.activation(out=RS[:, :], in_=pst[:, :], func=AF.Ln, bias=EPS, scale=1.0 / 1024),
              [kst], [kRS])
        P.add("act", lambda: S_.activation(out=RS[:, :], in_=RS[:, :], func=AF.Exp, scale=-0.5), [kRS], [kRS])
        for kc in range(8):
            P.add("dve", lambda kc=kc: V_.scalar_tensor_tensor(
                out=HM[:, kc, :], in0=xT[:, kc, blk], scalar=cols[:, l, C_G + kc:C_G + kc + 1], in1=RS[:, :],
                op0=ALU.mult, op1=ALU.mult), xk(kc) + [kRS, "cols"], hk(kc))
        ft.release(kRS)

    def block(s, l, b, last_layer, p, nxt):
        hctx["p"] = p
        HM = HMs[p]
        XS = XSs[p]
        XSK = XSKs[p]
        t0 = b * 512
        tts = [4 * b + i for i in range(4)]
        blk = slice(t0, t0 + 512)
        xk = lambda kc: [("xT", kc, tt) for tt in tts]

        if cv["key"] == (s, l, b):
            cg, U1 = cv["gen"], cv["U1"]
        else:
            if b == 0:
                P.add("pool", lambda: G_.memset(U0[:, :, 0:30], 0.0), [], [("U0h", 0), ("U0h", 1)])
            for i in (0, 1):
                emit_cf_pair(i, p, acc)
            cg, U1 = make_conv(l)
        cv["key"] = None

        def drain_conv(n):
            drain(cg, n)

        def tanh_gate(pz, kz, out_ap, wkeys):
            P.add("act", lambda: S_.activation(out=out_ap, in_=pz[:, :], func=AF.Silu), [kz], wkeys)

        an = {}

        def a_sq():
            if nxt is None:
                return
            l2, b2 = nxt
            HM2 = HMs[1 - p]
            blk2 = slice(b2 * 512, b2 * 512 + 512)
            tts2 = [4 * b2 + i for i in range(4)]
            xk2 = lambda kc: [("xT", kc, tt) for tt in tts2]
            hk2 = lambda kc: [("HM", 1 - p, kc, 0), ("HM", 1 - p, kc, 1)]
            an["c"] = (l2, HM2, blk2, xk2, hk2)

        def a_sq_part(q):
            if nxt is None:
                return
            l2, HM2, blk2, xk2, hk2 = an["c"]
            for kc in (2 * q, 2 * q + 1):
                P.add("act", lambda kc=kc: S_.activation(out=HM2[:, kc, :], in_=xT[:, kc, blk2], func=AF.Square),
                      xk2(kc), hk2(kc))

        def a_mm():
            if nxt is None:
                return
            l2, HM2, blk2, xk2, hk2 = an["c"]
            pst, kst = stat.get()
            for kc in range(8):
                mm(pst[:, :], onesb[:, :], HM2[:, kc, :], kc == 0, kc == 7, hk2(kc) + ["onesb"], [kst])
            an["pst"] = (pst, kst)

        def a_rs():
            if nxt is None:
                return
            pst, kst = an["pst"]
            RS, kRS = ft.alloc()
            P.add("act", lambda: S_.activation(out=RS[:, :], in_=pst[:, :], func=AF.Ln, bias=EPS, scale=1.0 / 1024),
                  [kst], [kRS])
            P.add("act", lambda: S_.activation(out=RS[:, :], in_=RS[:, :], func=AF.Exp, scale=-0.5), [kRS], [kRS])
            an["rs"] = (RS, kRS)

        def a_ht():
            if nxt is None:
                return
            l2, HM2, blk2, xk2, hk2 = an["c"]
            RS, kRS = an["rs"]
            for kc in range(8):
                P.add("dve", lambda kc=kc: V_.scalar_tensor_tensor(
                    out=HM2[:, kc, :], in0=xT[:, kc, blk2], scalar=cols[:, l2, C_G + kc:C_G + kc + 1], in1=RS[:, :],
                    op0=ALU.mult, op1=ALU.mult), xk2(kc) + [kRS, "cols"], hk2(kc))
            ft.release(kRS)

        pF, kF = ps[7], ("ps", 7)
        for i, tt in enumerate(tts):
            pv, kv = misc.get()
            for kc in range(8):
                mm(pv[:, :], HM[:, kc, i * 128:(i + 1) * 128], WVv[:, kc, :], kc == 0, kc == 7, hmk(kc) + ["WVv"], [kv])
            for kc in range(8):
                mm(pF[:, i * 8:(i + 1) * 8], HM[:, kc, i * 128:(i + 1) * 128], WVf[:, kc, :], kc == 0, kc == 7,
                   hmk(kc) + ["WVf"], [kF])
            evac_copy(V[:, tt, :, 0:64], pv[:, :].rearrange("p (h d) -> p h d", h=8), [kv], [("V", tt)])
        FZ, kFZ = ft.alloc()
        P.add("dve", lambda: V_.tensor_tensor(out=FZ[:, 0:32], in0=pF[:, 0:32], in1=bfb[:, l, :], op=ALU.add),
              [kF, "bfb"], [kFZ])
        P.add("act", lambda: S_.activation(out=FZ[:, 0:32], in_=FZ[:, 0:32], func=AF.Exp, scale=-1.0), [kFZ], [kFZ])
        P.add("act", lambda: S_.activation(out=FZ[:, 0:32], in_=FZ[:, 0:32], func=AF.Ln, bias=1.0), [kFZ], [kFZ])
        cdma_w, cdma_r = [], []

        def flush_cdma(lst):
            while lst:
                it = lst.pop(0)
                if callable(it):
                    it()
                else:
                    P.add("sp", it[0], it[1], it[2], dma=True)

        def fpath_tail():
            pC, kC = ps[6], ("ps", 6)
            for i in range(4):
                for j in range(i + 1):
                    rhs = negtri_f if j == i else nonesf[:, :]
                    mm(pC[0:8, i * 128:(i + 1) * 128], FZ[:, j * 8:(j + 1) * 8], rhs, j == 0, j == i,
                       [kFZ, "cstf", "nonesf"], [kC])
            CC, kCC = ft.alloc()
            P.add("dve", lambda: V_.tensor_scalar(out=CC[0:8, :], in0=pC[0:8, :], scalar1=CB[0:8, 0:1], scalar2=None,
                                                  op0=ALU.add), [kC, "CB"], [kCC])
            ft.release(kFZ)
            P.add("dve", lambda: V_.tensor_copy(out=CB[0:8, 0:1], in_=CC[0:8, 511:512]), [kCC], ["CB"])
            HI, kHI = bt.alloc()
            MID, kMID = bt.alloc()
            LO, kLO = bt.alloc()
            R1, kR1 = ft.alloc()
            R2, kR2 = ft.alloc()
            P.add("dve", lambda: V_.tensor_copy(out=HI[0:8, :], in_=CC[0:8, :]), [kCC], [kHI])
            P.add("dve", lambda: V_.tensor_tensor(out=R1[0:8, :], in0=CC[0:8, :], in1=HI[0:8, :], op=ALU.subtract),
                  [kCC, kHI], [kR1])
            P.add("dve", lambda: V_.tensor_copy(out=MID[0:8, :], in_=R1[0:8, :]), [kR1], [kMID])
            P.add("dve", lambda: V_.tensor_tensor(out=R2[0:8, :], in0=R1[0:8, :], in1=MID[0:8, :], op=ALU.subtract),
                  [kR1, kMID], [kR2])
            P.add("dve", lambda: V_.tensor_copy(out=LO[0:8, :], in_=R2[0:8, :]), [kR2], [kLO])
            ft.release(kCC)
            ft.release(kR1)
            ft.release(kR2)
            sc = scr_d[b % 2]
            for j, (tile_, key_) in enumerate(((HI, kHI), (MID, kMID), (LO, kLO))):
                cdma_w.append((lambda j=j, tile_=tile_: nc.sync.dma_start(out=sc[:, j, :], in_=tile_[0:8, :]),
                               [key_], [("scr", b % 2, j)]))
            cdma_w.append(lambda: (bt.release(kHI), bt.release(kMID), bt.release(kLO)))
            scr_keys = [("scr", b % 2, j) for j in range(3)]
            cdma_r.append((lambda: nc.sync.dma_start(out=QA[64:67, :, :], in_=sc.rearrange("h j t -> j h t")),
                           scr_keys + [("QAc",)], [("QAaug",)]))
            cdma_r.append((lambda: nc.sync.dma_start(out=KA[67:70, :, blk], in_=sc.rearrange("h j t -> j h t")),
                           scr_keys + [("KAc",)], [("KAaug", b)]))

        def qk_s1(which, j):
            pa, ka = proj_chunk()
            SQ, kSQ = bt.alloc()
            P.add("act", lambda: S_.activation(out=SQ[:, :], in_=pa[:, :], func=AF.Square), [ka], [kSQ])
            return (which, j, pa, ka, SQ, kSQ)

        def qk_s2(st_):
            which, j, QR, kQR, SQ, kSQ = st_
            pm, km = stat.get()
            mm(pm[:, :], bdiag_b, SQ[:, :], True, True, [kSQ, "cstb"], [km])
            RQ, kRQ = ft.alloc()
            P.add("act", lambda: S_.activation(out=RQ[:, :], in_=pm[:, :], func=AF.Ln, bias=EPS, scale=1.0 / 64),
                  [km], [kRQ])
            P.add("act", lambda: S_.activation(out=RQ[:, :], in_=RQ[:, :], func=AF.Exp, scale=-0.5), [kRQ], [kRQ])
            for par in (0, 1):
                h = 2 * j + par
                r0 = 64 * par
                if which == "q":
                    dst = QA[0:64, h, :]
                    wkey = ("QA", h)
                    gc = DC[r0:r0 + 64, l, D_GQ8 + j:D_GQ8 + j + 1]
                    gk = ("DC", D_GQ8)
                else:
                    dst = KA[0:64, h, blk]
                    wkey = ("KA", h, b)
                    gc = cols[r0:r0 + 64, l, C_GK + j:C_GK + j + 1]
                    gk = "cols"
                P.add("dve", lambda dst=dst, gc=gc, r0=r0: V_.scalar_tensor_tensor(
                    out=dst, in0=QR[r0:r0 + 64, :], scalar=gc, in1=RQ[r0:r0 + 64, :], op0=ALU.mult, op1=ALU.mult),
                    [kQR, kRQ, gk], [wkey])
            ft.release(kRQ)
            bt.release(kSQ)

        pend = None
        for which, j in [("k", j) for j in range(4)] + [("q", j) for j in range(4)]:
            cur = qk_s1(which, j)
            if pend is not None:
                qk_s2(pend)
            pend = cur
            if which == "k" and j == 1:
                fpath_tail()
            drain_conv(1)

        flush_deferred()
        flush_cdma(cdma_w)
        for i in (0, 1):
            pz, kz = proj_chunk()
            if pend is not None:
                qk_s2(pend)
                pend = None
            tanh_gate(pz, kz, ZC[:, i, :], [("ZC", i)])
            drain_conv(1)
        for i in (0, 1):
            pc, kc_ = proj_chunk()
            px, kx = proj_chunk()
            CX, kCX = ft.alloc()
            P.add("act", lambda pc=pc, CX=CX: S_.copy(out=CX[:, :], in_=pc[:, :]), [kc_], [kCX])
            P.add("dve", lambda i=i, px=px, CX=CX: V_.tensor_tensor(out=S0[:, i, 2:514], in0=CX[:, :], in1=px[:, :], op=ALU.mult),
                  [kCX, kx], [("S0", i)])
            ft.release(kCX)
            drain_conv(2)
        flush_cdma(cdma_r)
        for i in (0, 1):
            pb, kb_ = proj_chunk()
            pz, kz = proj_chunk()
            T2, kT2 = ft.alloc()
            tanh_gate(pz, kz, T2[:, :], [kT2])
            P.add("dve", lambda i=i, pb=pb, T2=T2: V_.tensor_tensor(out=GS[:, i, :], in0=T2[:, :], in1=pb[:, :], op=ALU.mult),
                  [kT2, kb_], [("GS", i)])
            ft.release(kT2)
            drain_conv(2)
        a_sq()
        for j in range(4):
            pz, kz = proj_chunk()
            tanh_gate(pz, kz, ZA[:, j, :], [("ZA", j)])
            a_sq_part(j)
            drain_conv(1)
        a_mm()
        a_rs()

        ln = {}

        def ln1():
            pmu, kmu = stat.get()
            pm2, km2 = stat.get()
            for i in (0, 1):
                Tb, kTb = bt.alloc()
                Tq, kTq = bt.alloc()
                P.add("act", lambda i=i, Tb=Tb: S_.copy(out=Tb[:, :], in_=U1[i][0][:, :]), [U1[i][1]], [kTb])
                P.add("act", lambda i=i, Tq=Tq: S_.activation(out=Tq[:, :], in_=U1[i][0][:, :], func=AF.Square), [U1[i][1]], [kTq])
                mm(pmu[:, :], onesb[:, :], Tb[:, :], i == 0, i == 1, [kTb, "onesb"], [kmu])
                mm(pm2[:, :], onesb[:, :], Tq[:, :], i == 0, i == 1, [kTq, "onesb"], [km2])
                bt.release(kTb)
                bt.release(kTq)
            ln["st"] = (pmu, kmu, pm2, km2)

        def ln2a():
            pmu, kmu, pm2, km2 = ln["st"]
            MEAN, kME = ft.alloc()
            MSQ, kMS = ft.alloc()
            if b >= 2:
                P.add("dve", lambda: V_.tensor_scalar(out=MEAN[:, :], in0=pmu[:, :], scalar1=1.0 / 256, scalar2=None, op0=ALU.mult),
                      [kmu], [kME])
                P.add("dve", lambda: V_.tensor_tensor(out=MSQ[:, :], in0=MEAN[:, :], in1=MEAN[:, :], op=ALU.mult), [kME], [kMS])
            else:
                P.add("act", lambda: S_.activation(out=MEAN[:, :], in_=pmu[:, :], func=AF.Identity, scale=1.0 / 256), [kmu], [kME])
                P.add("act", lambda: S_.activation(out=MSQ[:, :], in_=pmu[:, :], func=AF.Square, scale=1.0 / 256), [kmu], [kMS])
            ln["m"] = (MEAN, kME, MSQ, kMS)

        def ln2b():
            pmu, kmu, pm2, km2 = ln["st"]
            MEAN, kME, MSQ, kMS = ln["m"]
            VAR, kVA = MSQ, kMS
            P.add("dve", lambda: V_.scalar_tensor_tensor(out=VAR[:, :], in0=pm2[:, :], scalar=1.0 / 256, in1=MSQ[:, :],
                                                         op0=ALU.mult, op1=ALU.subtract), [km2, kMS], [kVA])
            P.add("dve", lambda: V_.tensor_scalar(out=VAR[:, :], in0=VAR[:, :], scalar1=0.0, scalar2=EPS,
                                                  op0=ALU.max, op1=ALU.add), [kVA], [kVA])
            ln["v"] = (VAR, kVA)

        def ln2c():
            VAR, kVA = ln["v"]
            P.add("act", lambda: S_.activation(out=VAR[:, :], in_=VAR[:, :], func=AF.Ln), [kVA], [kVA])

        def ln2c2():
            VAR, kVA = ln["v"]
            P.add("act", lambda: S_.activation(out=VAR[:, :], in_=VAR[:, :], func=AF.Exp, scale=-0.5), [kVA], [kVA])

        def ln2d():
            MEAN, kME, MSQ, kMS = ln["m"]
            VAR, kVA = ln["v"]
            for i in (0, 1):
                XN, kXN = U1[i]
                P.add("dve", lambda XN=XN: V_.tensor_tensor(out=XN[:, :], in0=XN[:, :], in1=MEAN[:, :], op=ALU.subtract),
                      [kXN, kME], [kXN])
                P.add("dve", lambda XN=XN: V_.tensor_tensor(out=XN[:, :], in0=XN[:, :], in1=VAR[:, :], op=ALU.mult),
                      [kXN, kVA], [kXN])
            ft.release(kME)
            ft.release(kVA)

        def ln2e():
            pass

        def ln2f():
            SL = []
            for i in (0, 1):
                XN, kXN = U1[i]
                SLt, kSL = bt.alloc()
                P.add("act", lambda XN=XN, SLt=SLt, i=i: S_.activation(
                    out=SLt[:, :], in_=XN[:, :], func=AF.Silu, bias=cols[:, l, C_LNB + i:C_LNB + i + 1],
                    scale=cols[:, l, C_LNG + i:C_LNG + i + 1]), [kXN, "cols"], [kSL])
                ft.release(kXN)
                SL.append((SLt, kSL))
            ln["SL"] = SL

        def ln2g():
            pass

        def ln3a():
            SL = ln["SL"]
            pps = []
            for co in (0, 1):
                ppo, kpo = stat.get()
                for ci in (0, 1):
                    mm(ppo[:, :], PW[:, ci, co * 128:(co + 1) * 128], SL[ci][0][:, :], ci == 0, ci == 1, [SL[ci][1], "PW"], [kpo])
                pps.append((ppo, kpo))
            for i in (0, 1):
                bt.release(SL[i][1])
            ln["pp"] = pps

        def ln3b():
            for co in (0, 1):
                ppo, kpo = ln["pp"][co]
                P.add("dve", lambda co=co, ppo=ppo: V_.scalar_tensor_tensor(
                    out=HM[:, co, :], in0=ppo[:, :], scalar=1.0, in1=ZC[:, co, :], op0=ALU.mult, op1=ALU.mult),
                    [kpo, ("ZC", co)], hmk(co))

        def sc_stage():
            for i in (0, 1):
                SA, kSA = ft.alloc()
                SB, kSB = ft.alloc()
                w = lambda k: DC[:, l, D_SCWH + i * 3 + k:D_SCWH + i * 3 + k + 1]
                rk = [("S0", i), ("S0h", i), ("DC", D_SCWH)]
                P.add("dve", lambda i=i, SA=SA, w0=w(0): V_.tensor_scalar(out=SA[:, :], in0=S0[:, i, 0:512], scalar1=w0, scalar2=None,
                                                                        op0=ALU.mult), rk, [kSA])
                P.add("dve", lambda i=i, SA=SA, SB=SB, w1=w(1): V_.scalar_tensor_tensor(
                    out=SB[:, :], in0=S0[:, i, 1:513], scalar=w1, in1=SA[:, :], op0=ALU.mult, op1=ALU.add), rk + [kSA], [kSB])
                P.add("dve", lambda i=i, SA=SA, SB=SB, w2=w(2): V_.scalar_tensor_tensor(
                    out=SA[:, :], in0=S0[:, i, 2:514], scalar=w2, in1=SB[:, :], op0=ALU.mult, op1=ALU.add), rk + [kSB], [kSA])
                P.add("dve", lambda i=i, SA=SA: V_.tensor_tensor(out=HM[:, 2 + i, :], in0=SA[:, :], in1=GS[:, i, :], op=ALU.mult),
                      [kSA, ("GS", i)], hmk(2 + i))
                P.add("dve", lambda i=i: V_.tensor_copy(out=S0[:, i, 0:2], in_=S0[:, i, 512:514]), [("S0", i)], [("S0h", i)])
                ft.release(kSA)
                ft.release(kSB)

        def ln1_act():
            tl = []
            for i in (0, 1):
                _, kT = ft.alloc()
                Tb = FTb16[:, kT[1], 0:512]
                Tq = FTb16[:, kT[1], 512:1024]
                if b >= 2:
                    P.add("dve", lambda i=i, Tb=Tb: V_.tensor_copy(out=Tb, in_=U1[i][0][:, :]), [U1[i][1]], [kT])
                    P.add("dve", lambda i=i, Tq=Tq: V_.tensor_tensor(out=Tq, in0=U1[i][0][:, :], in1=U1[i][0][:, :], op=ALU.mult),
                          [U1[i][1], kT], [kT])
                else:
                    P.add("act", lambda i=i, Tb=Tb: S_.copy(out=Tb, in_=U1[i][0][:, :]), [U1[i][1]], [kT])
                    P.add("act", lambda i=i, Tq=Tq: S_.activation(out=Tq, in_=U1[i][0][:, :], func=AF.Square), [U1[i][1], kT], [kT])
                tl.append((Tb, Tq, kT))
            ln["tl"] = tl

        def ln1_mm():
            pmu, kmu = stat.get()
            pm2, km2 = stat.get()
            for i, (Tb, Tq, kT) in enumerate(ln["tl"]):
                mm(pmu[:, :], onesb[:, :], Tb, i == 0, i == 1, [kT, "onesb"], [kmu])
                mm(pm2[:, :], onesb[:, :], Tq, i == 0, i == 1, [kT, "onesb"], [km2])
                ft.release(kT)
            ln["st"] = (pmu, kmu, pm2, km2)

        def taps6():
            drain_conv(6)

        def taps_all():
            drain_conv(1000)

        def pf_halo():
            if nxt is not None and nxt[1] == 0:
                P.add("pool", lambda: G_.memset(U0[:, :, 0:30], 0.0), [], [("U0h", 0), ("U0h", 1)])

        def pf_cf0():
            if nxt is not None:
                emit_cf_pair(0, 1 - p, stat)

        def pf_cf1():
            if nxt is not None:
                emit_cf_pair(1, 1 - p, stat)

        def pf_conv():
            if nxt is not None:
                g_, u_ = make_conv(nxt[0])
                cv["key"] = (s, nxt[0], nxt[1])
                cv["gen"] = g_
                cv["U1"] = u_

        NSL = 32
        slot_fns = {0: [sc_stage, taps6], 1: [taps6], 2: [taps6], 3: [taps6], 4: [taps6], 5: [taps6],
                    6: [a_ht, taps6], 7: [taps_all], 8: [ln1_act], 10: [ln1_mm], 12: [ln2a], 13: [ln2b], 15: [ln2c], 16: [ln2c2],
                    17: [ln2d], 19: [ln2e, pf_halo], 20: [pf_cf0, ln2f, pf_cf1], 23: [ln2g], 25: [ln3a],
                    27: [ln3b], 28: [pf_conv]}

        steps = [(h, kb) for h in range(8) for kb in range(4 * b + 4)]
        nsteps = len(steps)
        stb = [(ps[i], ("ps", i)) for i in range(4)]
        pts = {}
        nrm = {}

        def emit_score(idx):
            h, kb = steps[idx]
            st, kst_ = stb[idx % 4]
            i = kb - 4 * b
            n0 = 128 * i if i > 0 else 0
            rd = [("KA", h, kb // 4), ("KAaug", kb // 4), ("KAc",), ("QA", h), ("QAaug",), ("QAc",)]
            mm(st[:, n0:512], KA[0:70, h, kb * 128:(kb + 1) * 128], QA[0:70, h, n0:512], True, i < 0, rd, [kst_])
            if i >= 0:
                mm(st[:, n0:n0 + 128], ident_b, maskneg_b, False, True, ["cstb"], [kst_])
            PTt, kPT = bt.alloc()
            P.add("act", lambda: S_.activation(out=PTt[:, n0:512], in_=st[:, n0:512], func=AF.Exp), [kst_], [kPT])
            pts[idx] = (PTt, kPT, n0)

        def n1(h):
            po, kpo = ps[4 + h % 2], ("ps", 4 + h % 2)
            if h % 2 == 0:
                NUM, kNU = ft.alloc()
                DEN, kDE = ft.alloc()
                nrm["pair"] = (NUM, kNU, DEN, kDE)
                if b >= 2:
                    P.add("dve", lambda: V_.tensor_copy(out=NUM[0:64, :], in_=po[0:64, :]), [kpo], [kNU])
                else:
                    P.add("act", lambda: S_.copy(out=NUM[0:64, :], in_=po[0:64, :]), [kpo], [kNU])
                P.add("act", lambda: S_.activation(out=DEN[0:64, :], in_=po[64:128, :], func=AF.Ln), [kpo], [kDE])
            else:
                NUM, kNU, DEN, kDE = nrm["pair"]
                P.add("dve", lambda: V_.tensor_copy(out=NUM[64:128, :], in_=po[0:64, :]), [kpo], [kNU])
                P.add("act", lambda: S_.activation(out=DEN[64:128, :], in_=po[64:128, :], func=AF.Ln), [kpo], [kDE])
                nrm[h // 2] = (NUM, kNU, DEN, kDE)

        def n2(j):
            NUM, kNU, DEN, kDE = nrm[j]
            P.add("act", lambda: S_.activation(out=DEN[:, :], in_=DEN[:, :], func=AF.Exp, scale=-1.0), [kDE], [kDE])
            P.add("dve", lambda: V_.tensor_tensor(out=NUM[:, :], in0=NUM[:, :], in1=DEN[:, :], op=ALU.mult), [kNU, kDE], [kNU])

        def n3(j):
            NUM, kNU, DEN, kDE = nrm.pop(j)
            P.add("dve", lambda: V_.tensor_tensor(out=HM[:, 4 + j, :], in0=NUM[:, :], in1=ZA[:, j, :], op=ALU.mult),
                  [kNU, ("ZA", j)], [("HM", p, 4 + j, 0), ("HM", p, 4 + j, 1)])
            ft.release(kNU)
            ft.release(kDE)

        def emit_pv(idx):
            h, kb = steps[idx]
            PTt, kPT, n0 = pts.pop(idx)
            po, kpo = ps[4 + h % 2], ("ps", 4 + h % 2)
            last = kb == 4 * b + 3
            mm(po[:, n0:512], V[:, kb, h, :], PTt[:, n0:512], kb == 0, last, [("V", kb), kPT], [kpo])
            bt.release(kPT)
            if last:
                n1(h)
                if h % 2 == 0 and h >= 2:
                    n2(h // 2 - 1)
                if h % 2 == 1 and h >= 3:
                    n3(h // 2 - 1)

        fired = set()
        for idx in range(nsteps + LA):
            if idx < nsteps:
                emit_score(idx)
            j = idx - LA
            if j >= 0:
                emit_pv(j)
                k = ((j + 1) * NSL) // nsteps - 1
                for kk in range(k + 1):
                    if kk not in fired and (kk + 1) * nsteps <= (j + 1) * NSL:
                        fired.add(kk)
                        late_step()
                        for fn_ in slot_fns.get(kk, ()):
                            fn_()
        n2(3)
        n3(3)
        for kk in range(NSL):
            if kk not in fired:
                fired.add(kk)
                for fn_ in slot_fns.get(kk, ()):
                    fn_()

        for m in range(8):
            pa, ka = proj_chunk()
            P.add("dve", lambda m=m, pa=pa: V_.tensor_tensor(out=xT[:, m, blk], in0=xT[:, m, blk], in1=pa[:, :], op=ALU.add),
                  xk(m) + [ka], xk(m))
            if cv["key"] is not None:
                drain(cv["gen"], 2)

        if last_layer:
            for i, tt in enumerate(tts):
                XO = XS[i % 2]
                if i >= 2:
                    flush_deferred(1)
                for half in (0, 1):
                    m0 = 4 * half
                    pt_, kpt = misc.get()
                    for q in range(4):
                        P.add("pe", lambda q=q, pt_=pt_, m0=m0, tt=tt: T_.transpose(
                            pt_[:, q * 128:(q + 1) * 128], xT[:, m0 + q, tt * 128:(tt + 1) * 128], ident_f),
                            [("xT", m0 + q, tt), "cstf"], [kpt])
                    evac_copy(XO[:, m0 * 128:(m0 + 4) * 128], pt_[:, :], [kpt], XSK[i % 2][4 * half * 2:(4 * half + 4) * 2] if False else [("XO", p, i % 2, half)] + XSK[i % 2])
                deferred.append((lambda XO=XO, tt=tt, s=s: nc.sync.dma_start(out=y_d[s, tt * 128:(tt + 1) * 128, :], in_=XO),
                                 [("XO", p, i % 2, 0), ("XO", p, i % 2, 1)] + XSK[i % 2], [("y", s, tt)] + XSK[i % 2]))

    for s in range(NSEQ):
        flush_deferred()
        for tt in range(NTT):
            XI = XSs[0][tt % 2]
            XIK = XSKs[0][tt % 2]
            P.add("sp", lambda XI=XI, tt=tt, s=s: nc.sync.dma_start(out=XI, in_=x_d[s, tt * 128:(tt + 1) * 128, :]),
                  [], XIK, dma=True)
            for half in (0, 1):
                m0 = 4 * half
                pt_, kpt = misc.get()
                for q in range(4):
                    P.add("pe", lambda q=q, pt_=pt_, XI=XI, m0=m0: T_.transpose(
                        pt_[:, q * 128:(q + 1) * 128], XI[:, (m0 + q) * 128:(m0 + q + 1) * 128], ident_f),
                        XIK + ["cstf"], [kpt])
                evac_copy(xT[:, m0:m0 + 4, tt * 128:(tt + 1) * 128], pt_[:, :].rearrange("p (a b) -> p a b", a=4),
                          [kpt], [("xT", m0 + q, tt) for q in range(4)])
        seqblocks = [(l, b) for l in range(NL) for b in range(NBLK)]
        emit_A(0, 0, 0)
        for n_, (l, b) in enumerate(seqblocks):
            if b == 0:
                P.add("sp", lambda l=l: nc.sync.dma_start(out=WVv.reshape([128, 4096])[:, :], in_=wv_b[l]),
                      [("wvsc", l, k) for k in range(4)], ["WVv"], dma=True)
                P.add("sp", lambda l=l: nc.sync.dma_start(out=PW.reshape([128, 512])[:, :], in_=pw_b[l]), [("pwsc", l)], ["PW"], dma=True)
                P.add("sp", lambda l=l: nc.sync.dma_start(out=WFs[:, :], in_=wf_d[l]), [], ["WFs"], dma=True)
                P.add("dve", lambda: V_.tensor_copy(out=WVf.reshape([128, 64])[:, :], in_=WFs[:, :]), ["WFs"], ["WVf"])
                P.add("pool", lambda: G_.memset(CB[:, :], 0.0), [], ["CB"])
                P.add("pool", lambda: G_.memset(S0[:, :, 0:2], 0.0), [], [("S0h", 0), ("S0h", 1)])
            nxt = seqblocks[n_ + 1] if n_ + 1 < len(seqblocks) else None
            if s == 0 and l == 0 and b <= NBLK - 2:
                lstate["active"] = True
            elif not lstate["done"]:
                lstate["active"] = True
                late_flush()
            block(s, l, b, l == NL - 1, n_ % 2, nxt)
            lstate["active"] = False

    flush_deferred()
    ykeys = [("y", s, tt) for s in range(NSEQ) for tt in range(NTT)]
    P.add("sp", lambda: None, ykeys, [])
    return nc, P


def prep_shared(inputs, NL):
    w_in = np.asarray(inputs["w_in"], np.float32)[:NL]
    w_out = np.asarray(inputs["w_out"], np.float32)[:NL]
    order = chunk_order()
    wfm = np.empty((NL, 26, 128, 8, 128), np.float32)
    for c, (kind, idx) in enumerate(order):
        c0 = _COL0[kind] + idx * 128
        blk = w_in[:, :, c0:c0 + 128].reshape(NL, 8, 128, 128)
        wfm[:, c] = blk.transpose(0, 2, 1, 3)
    wv = np.ascontiguousarray(w_in[:, :, 2816:3328].reshape(NL, 8, 128, 512).transpose(0, 2, 1, 3))
    wf = np.ascontiguousarray(w_in[:, :, 3840:3848].reshape(NL, 8, 128, 8).transpose(0, 2, 1, 3)).reshape(NL, 128, 64)
    wo = w_out.reshape(NL, 8, 128, 8, 128).transpose(0, 3, 2, 1, 4)
    pw = np.asarray(inputs["cf_pw"], np.float32)[:NL].reshape(NL, 2, 128, 256).transpose(0, 2, 1, 3)
    cols = np.zeros((128, NL, NCOL), np.float32)
    g = np.asarray(inputs["norm_g"], np.float32)[:NL]
    cols[:, :, C_G:C_G + 8] = g.reshape(NL, 8, 128).transpose(2, 0, 1)
    for name, c0 in (("cf_dw_b", C_CFB), ("cf_ln_g", C_LNG), ("cf_ln_b", C_LNB)):
        a = np.asarray(inputs[name], np.float32)[:NL]
        cols[:, :, c0:c0 + 2] = a.reshape(NL, 2, 128).transpose(2, 0, 1)
    cfw = np.asarray(inputs["cf_dw"], np.float32)[:NL]
    cols[:, :, C_CFW:C_CFW + 62] = cfw.reshape(NL, 31, 2, 128).transpose(3, 0, 2, 1).reshape(128, NL, 62)
    scw = np.asarray(inputs["sc_dw"], np.float32)[:NL]
    cols[:, :, C_SCW:C_SCW + 6] = scw.reshape(NL, 3, 2, 128).transpose(3, 0, 2, 1).reshape(128, NL, 6)
    for name, c0 in (("q_norm_g", C_GQ), ("k_norm_g", C_GK)):
        a = np.asarray(inputs[name], np.float32)[:NL]
        cols[:, :, c0:c0 + 4] = a.reshape(NL, 4, 128).transpose(2, 0, 1)
    bf = np.asarray(inputs["b_f"], np.float32)[:NL]
    bfb = np.broadcast_to(np.tile(bf, (1, 4))[None], (128, NL, 32)).copy()
    cst = np.zeros((128, 512), np.float32)
    ii = np.arange(128)
    cst[:, 0:128] = np.eye(128, dtype=np.float32)
    cst[:, 128:256] = -(ii[:, None] <= ii[None, :]).astype(np.float32)
    cst[:, 256:384] = (ii[:, None] // 64 == ii[None, :] // 64).astype(np.float32)
    cst[:, 384:512] = np.where(ii[:, None] <= ii[None, :], 0.0, -30000.0).astype(np.float32)
    return dict(wfm=np.ascontiguousarray(wfm), wv=wv, wf=wf, wo=np.ascontiguousarray(wo), pw=np.ascontiguousarray(pw),
                cols=cols, bfb=bfb, cst=cst)


_CACHE = {}


def run(inputs, NSEQ, NBLK, NL, n_cores, x_shards):
    key = (NSEQ, NBLK, NL)
    if key not in _CACHE:
        nc, P = build(NSEQ, NBLK, NL)
        with ExitStack() as stack:
            info = P.finalize(stack)
        _CACHE[key] = (nc, info)
    nc, info = _CACHE[key]
    shared = prep_shared(inputs, NL)
    in_maps = [dict(shared, x=np.ascontiguousarray(xs)) for xs in x_shards]
    res = run_bass_kernel_spmd(nc, in_maps, core_ids=list(range(n_cores)))
    return [r["y"] for r in res.results], info


def kernel(**inputs):
    x = np.asarray(inputs["x"], np.float32)
    n = 8
    shards = [x[2 * c:2 * c + 2] for c in range(n)]
    outs, _ = run(inputs, 2, 4, 2, n, shards)
    return np.concatenate(outs, axis=0).astype(np.float32)
```

```python
import numpy as np
from contextlib import ExitStack
import concourse.bass as bass
import concourse.mybir as mybir
from concourse.bass_utils import run_bass_kernel_spmd

F32 = mybir.dt.float32
BF16 = mybir.dt.bfloat16
AF = mybir.ActivationFunctionType
ALU = mybir.AluOpType

EPS = 1e-6
C_G, C_CFB, C_LNG, C_LNB, C_CFW, C_SCW, C_GQ, C_GK, NCOL = 0, 8, 10, 12, 14, 76, 82, 86, 90
D_CFWH, D_SCWH, D_GQ8, D_LNGH, D_LNBH, ND = 0, 62, 68, 72, 74, 76
NSLOT = 3
NFT = 8
NBT = 6
NDMASEM = 25
LA = 3


class Prog:
    ENGS = ["pe", "act", "dve", "pool", "sp"]

    def __init__(self, nc):
        self.nc = nc
        self.eng = {"pe": nc.tensor, "act": nc.scalar, "dve": nc.vector, "pool": nc.gpsimd, "sp": nc.sync}
        self.ops = []

    def add(self, eng, fn, reads=(), writes=(), dma=False):
        self.ops.append(dict(eng=eng, fn=fn, reads=list(reads), writes=list(writes), dma=dma))

    def finalize(self, stack):
        ops = self.ops
        last_w, readers = {}, {}
        for i, op in enumerate(ops):
            deps = set()
            for k in op["reads"]:
                if k in last_w:
                    deps.add(last_w[k])
            for k in op["writes"]:
                if k in last_w:
                    deps.add(last_w[k])
                for r in readers.get(k, ()):
                    deps.add(r)
            deps.discard(i)
            keep = set()
            for d in deps:
                dop = ops[d]
                if (not dop["dma"]) and dop["eng"] == "pe" and op["eng"] == "pe" and not op["dma"]:
                    continue
                keep.add(d)
            latest = {}
            kept2 = set()
            for d in keep:
                dop = ops[d]
                if dop["dma"]:
                    kept2.add(d)
                else:
                    latest[dop["eng"]] = max(latest.get(dop["eng"], -1), d)
            kept2 |= set(latest.values())
            op["deps"] = kept2
            for k in op["reads"]:
                readers.setdefault(k, []).append(i)
            for k in op["writes"]:
                last_w[k] = i
                readers[k] = []
        targets = set()
        for op in ops:
            targets |= op["deps"]
        cnt = {e: 0 for e in self.ENGS}
        dma_use = [0] * NDMASEM
        rr = 0
        for i, op in enumerate(ops):
            if op["dma"]:
                s = rr % NDMASEM
                rr += 1
                dma_use[s] += 1
                op["sem"] = ("dma", s)
                op["val"] = 16 * dma_use[s]
                op["prev"] = 16 * (dma_use[s] - 1)
            elif i in targets:
                cnt[op["eng"]] += 1
                op["sem"] = ("eng", op["eng"])
                op["val"] = cnt[op["eng"]]
            else:
                op["sem"] = None
        sems = {}

        def sem(key):
            if key not in sems:
                sems[key] = stack.enter_context(self.nc.semaphore("s_%s_%s" % key))
            return sems[key]

        waited = {e: {} for e in self.ENGS}
        nwait = 0
        for op in ops:
            e = op["eng"]
            E = self.eng[e]
            need = {}
            for d in op["deps"]:
                dop = ops[d]
                key = dop["sem"]
                need[key] = max(need.get(key, 0), dop["val"])
            if op["dma"] and op["prev"] > 0:
                need[op["sem"]] = max(need.get(op["sem"], 0), op["prev"])
            for key, val in need.items():
                if waited[e].get(key, 0) < val:
                    E.wait_ge(sem(key), val)
                    waited[e][key] = val
                    nwait += 1
            ins = op["fn"]()
            if op["sem"] is not None and ins is not None:
                ins.then_inc(sem(op["sem"]), 16 if op["dma"] else 1)
        return dict(n_ops=len(ops), n_wait=nwait, cnt=cnt)


class TPool:
    def __init__(self, tiles, name):
        self.tiles = tiles
        self.name = name
        self.free = list(range(len(tiles)))

    def alloc(self):
        assert self.free, "temp pool %s exhausted" % self.name
        i = self.free.pop(0)
        return self.tiles[i], (self.name, i)

    def release(self, key):
        assert key[0] == self.name and key[1] not in self.free
        self.free.append(key[1])


class Rot:
    def __init__(self, items):
        self.items = items
        self.i = 0

    def get(self):
        it = self.items[self.i % len(self.items)]
        self.i += 1
        return it


def chunk_order():
    o = []
    o += [("cfa", 0), ("cfg", 0), ("cfa", 1), ("cfg", 1)]
    o += [("k", j) for j in range(4)]
    o += [("q", j) for j in range(4)]
    o += [("cfz", 0), ("cfz", 1)]
    o += [("scC", 0), ("scx", 0), ("scC", 1), ("scx", 1)]
    o += [("scB", 0), ("scz", 0), ("scB", 1), ("scz", 1)]
    o += [("az", j) for j in range(4)]
    return o


_COL0 = {"cfa": 0, "cfg": 256, "cfz": 512, "scB": 768, "scC": 1024, "scx": 1280, "scz": 1536,
         "q": 1792, "k": 2304, "v": 2816, "az": 3328, "f": 3840}


def build(NSEQ, NBLK, NL):
    S = NBLK * 512
    NTT = NBLK * 4
    nc = bass.Bass("TRN2", target_bir_lowering=False)

    def dram(name, shape, dtype=F32, kind="ExternalInput"):
        return nc.dram_tensor(name, shape, dtype, kind=kind).ap()

    x_d = dram("x", [NSEQ, S, 1024])
    wfm_d = dram("wfm", [NL, 26, 128, 8, 128])
    wv_d = dram("wv", [NL, 128, 8, 512])
    wf_d = dram("wf", [NL, 128, 64])
    wo_d = dram("wo", [NL, 8, 128, 8, 128])
    pw_d = dram("pw", [NL, 128, 2, 256])
    cols_d = dram("cols", [128, NL, NCOL])
    bfb_d = dram("bfb", [128, NL, 32])
    cst_d = dram("cst", [128, 512])
    y_d = dram("y", [NSEQ, S, 1024], kind="ExternalOutput")
    scr_d = dram("scr", [2, 8, 3, 512], BF16, kind="Internal")
    wfm_b = dram("wfm_b", [NL, 26, 128, 1024], BF16, kind="Internal")
    wo_b = dram("wo_b", [NL, 8, 128, 1024], BF16, kind="Internal")
    wv_b = dram("wv_b", [NL, 128, 4096], BF16, kind="Internal")
    pw_b = dram("pw_b", [NL, 128, 512], BF16, kind="Internal")

    def A(name, shape, dtype):
        return nc.alloc_sbuf_tensor("sb_" + name, shape, dtype)

    xT = A("xT", [128, 8, S], F32)
    KA = A("KA", [128, 8, S], BF16)
    V = A("V", [128, NTT, 8, 128], BF16)
    QA = A("QA", [128, 8, 512], BF16)
    HMs = [A("HM0", [128, 8, 512], BF16), A("HM1", [128, 8, 512], BF16)]
    HMfs = [h_.bitcast(F32).reshape([128, 2, 1024]) for h_ in HMs]
    XSs = [[hf_[:, 0, :], hf_[:, 1, :]] for hf_ in HMfs]
    XSKs = [[[("HM", p_, kc, hf) for kc in range(4) for hf in (0, 1)], [("HM", p_, kc, hf) for kc in range(4, 8) for hf in (0, 1)]]
            for p_ in (0, 1)]
    WR = [A("WR%d" % i, [128, 8, 128], BF16) for i in range(NSLOT)]
    WVv = A("WVv", [128, 8, 512], BF16)
    WVf = A("WVf", [128, 8, 8], BF16)
    WFs = A("WFs", [128, 64], F32)
    PW = A("PW", [128, 2, 256], BF16)
    U0 = A("U0", [128, 2, 542], BF16)
    S0 = A("S0", [128, 2, 514], BF16)
    ZC = A("ZC", [128, 2, 512], BF16)
    GS = A("GS", [128, 2, 512], BF16)
    ZA = A("ZA", [128, 4, 512], BF16)
    FTbig = A("FT", [128, NFT, 512], F32)
    ft = TPool([FTbig[:, i, :] for i in range(NFT)], "FT")
    FTflat = FTbig.reshape([128, NFT // 2, 1024])
    FTb16 = FTbig.bitcast(BF16)
    bt = TPool([A("BT%d" % i, [128, 512], BF16) for i in range(NBT)], "BT")
    cols = A("cols", [128, NL, NCOL], F32)
    DC = A("DC", [128, NL, ND], F32)
    bfb = A("bfb", [128, NL, 32], F32)
    cstf = A("cstf", [128, 512], F32)
    cstb = A("cstb", [128, 512], BF16)
    onesb = A("onesb", [128, 128], BF16)
    nonesf = A("nonesf", [128, 128], F32)
    CB = A("CB", [128, 2], F32)
    ps = [nc.alloc_psum_tensor("ps%d" % i, [128, 512], F32) for i in range(8)]
    acc = Rot([(ps[i], ("ps", i)) for i in range(4)])
    misc = Rot([(ps[i], ("ps", i)) for i in (4, 5)])
    stat = Rot([(ps[i], ("ps", i)) for i in (6, 7)])

    ident_f = cstf[:, 0:128]
    negtri_f = cstf[:, 128:256]
    ident_b = cstb[:, 0:128]
    bdiag_b = cstb[:, 256:384]
    maskneg_b = cstb[:, 384:512]

    P = Prog(nc)
    V_ = nc.vector
    G_ = nc.gpsimd
    S_ = nc.scalar
    T_ = nc.tensor

    def mm(out, lhsT, rhs, start, stop, reads, writes):
        P.add("pe", lambda: T_.matmul(out, lhsT, rhs, start=start, stop=stop), reads, writes)

    hctx = dict(p=0)

    def hmk(kc):
        return [("HM", hctx["p"], kc, 0), ("HM", hctx["p"], kc, 1)]

    P.add("sp", lambda: nc.sync.dma_start(out=cols[:, :, :], in_=cols_d), [], ["cols"], dma=True)
    P.add("sp", lambda: nc.sync.dma_start(out=bfb[:, :, :], in_=bfb_d), [], ["bfb"], dma=True)
    P.add("sp", lambda: nc.sync.dma_start(out=cstf[:, :], in_=cst_d), [], ["cstf"], dma=True)
    P.add("dve", lambda: V_.tensor_copy(out=cstb[:, :], in_=cstf[:, :]), ["cstf"], ["cstb"])
    P.add("pool", lambda: G_.memset(onesb[:, :], 1.0), [], ["onesb"])
    P.add("pool", lambda: G_.memset(nonesf[:, :], -1.0), [], ["nonesf"])
    P.add("pool", lambda: G_.memset(V[:, :, :, :], 1.0), [], [("V", tt) for tt in range(NTT)])
    P.add("pool", lambda: G_.memset(QA[64:70, :, :], -1.0), [], [("QAc",)])
    P.add("pool", lambda: G_.memset(KA[64:70, :, :], 1.0), [], [("KAc",)])
    for (dst, src, n, sc) in ((D_CFWH, C_CFW, 62, 0.5), (D_SCWH, C_SCW, 6, 1.0), (D_GQ8, C_GQ, 4, 0.125),
                              (D_LNGH, C_LNG, 2, 0.5), (D_LNBH, C_LNB, 2, 0.5)):
        P.add("dve", lambda dst=dst, src=src, n=n, sc=sc: V_.tensor_scalar(
            out=DC[:, :, dst:dst + n], in0=cols[:, :, src:src + n], scalar1=sc, scalar2=None, op0=ALU.mult),
            ["cols"], [("DC", dst)])
    dc_all = [("DC", d) for d in (D_CFWH, D_SCWH, D_GQ8, D_LNGH, D_LNBH)]

    jobs = []
    for l in range(NL):
        for c in range(26):
            jobs.append((wfm_d[l, c].rearrange("p a b -> p (a b)"), wfm_b[l, c], 1024, ("wsc", l, c)))
        for m in range(8):
            jobs.append((wo_d[l, m].rearrange("p a b -> p (a b)"), wo_b[l, m], 1024, ("wsc", l, 26 + m)))
        for k in range(4):
            jobs.append((wv_d[l][:, 2 * k:2 * k + 2, :].rearrange("p a b -> p (a b)"), wv_b[l][:, 1024 * k:1024 * (k + 1)],
                         1024, ("wvsc", l, k)))
        jobs.append((pw_d[l].rearrange("p a b -> p (a b)"), pw_b[l], 512, ("pwsc", l)))
    WRflat = [w.reshape([128, 1024]) for w in WR]
    cast_rot = ["pool", "act", "dve"]
    NST = NFT // 2

    def emit_load(n):
        src, dst, ne, key = jobs[n]
        j = n % NST
        P.add("sp", lambda: nc.sync.dma_start(out=FTflat[:, j, 0:ne], in_=src), [], [("FT", 2 * j), ("FT", 2 * j + 1)], dma=True)

    def emit_cast_store(n):
        src, dst, ne, key = jobs[n]
        j = n % NST
        slot = n % NSLOT
        en = cast_rot[n % 3]
        o_ = WRflat[slot][:, 0:ne]
        i_ = FTflat[:, j, 0:ne]
        if en == "pool":
            fn = lambda: G_.tensor_copy(out=o_, in_=i_)
        elif en == "act":
            fn = lambda: S_.copy(out=o_, in_=i_)
        else:
            fn = lambda: V_.tensor_copy(out=o_, in_=i_)
        P.add(en, fn, [("FT", 2 * j), ("FT", 2 * j + 1)], [("WR", slot)])
        P.add("sp", lambda: nc.sync.dma_start(out=dst, in_=o_), [("WR", slot)], [key], dma=True)

    NJ0 = 39 if (NL > 1 and NBLK >= 2) else len(jobs)
    PRE = min(3, NST - 1)
    for n in range(NJ0 + PRE):
        if n < NJ0:
            emit_load(n)
        if n - PRE >= 0:
            emit_cast_store(n - PRE)

    Vf32 = V.bitcast(F32).reshape([128, NTT * 512])
    Vb16 = V.reshape([128, NTT * 1024])
    stg_f = Vf32[:, (NTT - 4) * 512:(NTT - 4) * 512 + 1024]
    stg_fk = [("V", NTT - 4), ("V", NTT - 3)]
    stg_b = [Vb16[:, (NTT - 2) * 1024:(NTT - 1) * 1024], Vb16[:, (NTT - 1) * 1024:NTT * 1024]]
    stg_bk = [[("V", NTT - 2)], [("V", NTT - 1)]]

    def late_gen():
        late = jobs[NJ0:]

        def A(n):
            src, dst, ne, key = late[n]
            P.add("sp", lambda: nc.sync.dma_start(out=stg_f[:, 0:ne], in_=src), [], stg_fk, dma=True)

        def B(n):
            src, dst, ne, key = late[n]
            P.add("act", lambda: S_.copy(out=stg_b[n % 2][:, 0:ne], in_=stg_f[:, 0:ne]), stg_fk, stg_bk[n % 2])

        def C(n):
            src, dst, ne, key = late[n]
            P.add("sp", lambda: nc.sync.dma_start(out=dst, in_=stg_b[n % 2][:, 0:ne]), stg_bk[n % 2], [key], dma=True)

        if not late:
            return
        A(0)
        yield
        for n in range(len(late)):
            B(n)
            yield
            if n + 1 < len(late):
                A(n + 1)
                yield
            C(n)
            yield
        P.add("pool", lambda: G_.memset(V[:, NTT - 4:NTT, :, 64:128], 1.0), [], [("V", tt) for tt in range(NTT - 4, NTT)])

    lstate = dict(gen=late_gen(), active=False, done=False)

    def late_step(n=1):
        if not lstate["active"] or lstate["done"]:
            return
        for _ in range(n):
            try:
                next(lstate["gen"])
            except StopIteration:
                lstate["done"] = True
                return

    def late_flush():
        if lstate["done"]:
            return
        for _ in lstate["gen"]:
            pass
        lstate["done"] = True

    chunks = []
    for s in range(NSEQ):
        sb_ = [(l, b) for l in range(NL) for b in range(NBLK)]
        for n_, (l, b) in enumerate(sb_):
            if n_ == 0:
                for c in range(4):
                    chunks.append((wfm_b[l, c], ("wsc", l, c)))
            for c in range(4, 26):
                chunks.append((wfm_b[l, c], ("wsc", l, c)))
            if n_ + 1 < len(sb_):
                l2 = sb_[n_ + 1][0]
                for c in range(4):
                    chunks.append((wfm_b[l2, c], ("wsc", l2, c)))
            for m in range(8):
                chunks.append((wo_b[l, m], ("wsc", l, 26 + m)))
    cstate = dict(cons=0, issue=0)

    def get_chunk():
        while cstate["issue"] < min(len(chunks), cstate["cons"] + NSLOT):
            i = cstate["issue"]
            slot = i % NSLOT
            P.add("sp", lambda slot=slot, i=i: nc.sync.dma_start(out=WRflat[slot][:, :], in_=chunks[i][0]),
                  [chunks[i][1]], [("WR", slot)], dma=True)
            cstate["issue"] += 1
        slot = cstate["cons"] % NSLOT
        cstate["cons"] += 1
        late_step()
        return WR[slot], ("WR", slot)

    def proj_chunk(pp=None, rot=None):
        W, kW = get_chunk()
        pa, ka = (rot or acc).get()
        pp = hctx["p"] if pp is None else pp
        for kc in range(8):
            mm(pa[:, :], W[:, kc, :], HMs[pp][:, kc, :], kc == 0, kc == 7,
               [kW, ("HM", pp, kc, 0), ("HM", pp, kc, 1)], [ka])
        return pa, ka

    cv = dict(key=None, gen=None, U1=None)

    def make_conv(lt):
        U1 = []

        def conv_gen():
            for i in (0, 1):
                AA, kAA = ft.alloc()
                AB, kAB = ft.alloc()
                accs = [(AA, kAA), (AB, kAB)]
                for k in range(31):
                    At, kAt = accs[k % 2]
                    wc = DC[:, lt, D_CFWH + i * 31 + k:D_CFWH + i * 31 + k + 1]
                    src = U0[:, i, k:k + 512]
                    rk = [("U0", i), ("U0h", i), ("DC", D_CFWH)]
                    if k < 2:
                        P.add("dve", lambda At=At, src=src, wc=wc: V_.tensor_scalar(
                            out=At[:, :], in0=src, scalar1=wc, scalar2=None, op0=ALU.mult), rk, [kAt])
                    else:
                        P.add("dve", lambda At=At, src=src, wc=wc: V_.scalar_tensor_tensor(
                            out=At[:, :], in0=src, scalar=wc, in1=At[:, :], op0=ALU.mult, op1=ALU.add), rk + [kAt], [kAt])
                    yield
                P.add("dve", lambda AA=AA, AB=AB, i=i: V_.scalar_tensor_tensor(
                    out=AA[:, :], in0=AA[:, :], scalar=cols[:, lt, C_CFB + i:C_CFB + i + 1], in1=AB[:, :],
                    op0=ALU.add, op1=ALU.add), [kAA, kAB, "cols"], [kAA])
                ft.release(kAB)
                U1.append((AA, kAA))
                P.add("dve", lambda i=i: V_.tensor_copy(out=U0[:, i, 0:30], in_=U0[:, i, 512:542]), [("U0", i)], [("U0h", i)])
                yield

        return conv_gen(), U1

    def drain(gen, n):
        if gen is None:
            return
        for _ in range(n):
            try:
                next(gen)
            except StopIteration:
                return

    def emit_cf_pair(i, pp, rot):
        pa, ka = proj_chunk(pp, rot)
        pg, kg = proj_chunk(pp, rot)
        P.add("act", lambda: S_.activation(out=U0[:, i, 30:542], in_=pg[:, :], func=AF.Tanh, scale=0.5), [kg], [("U0", i)])
        P.add("dve", lambda: V_.scalar_tensor_tensor(
            out=U0[:, i, 30:542], in0=U0[:, i, 30:542], scalar=1.0, in1=pa[:, :], op0=ALU.add, op1=ALU.mult),
            [("U0", i), ka], [("U0", i)])


    ev = dict(i=0)

    def evac_copy(out, in_, reads, writes):
        ev["i"] += 1
        if ev["i"] % 2 == 0:
            P.add("act", lambda: S_.copy(out=out, in_=in_), reads, writes)
        else:
            P.add("dve", lambda: V_.tensor_copy(out=out, in_=in_), reads, writes)

    deferred = []

    def flush_deferred(n=100):
        while deferred and n > 0:
            fn, rd, wr = deferred.pop(0)
            P.add("sp", fn, rd, wr, dma=True)
            n -= 1

    def emit_A(l, b, p):
        HM = HMs[p]
        t0 = b * 512
        tts = [4 * b + i for i in range(4)]
        blk = slice(t0, t0 + 512)
        xk = lambda kc: [("xT", kc, tt) for tt in tts]
        hk = lambda kc: [("HM", p, kc, 0), ("HM", p, kc, 1)]
        P.add("act", lambda: S_.activation(out=HM[:, :, :], in_=xT[:, :, blk], func=AF.Square),
              [k for kc in range(8) for k in xk(kc)], [k for kc in range(8) for k in hk(kc)])
        pst, kst = stat.get()
        for kc in range(8):
            mm(pst[:, :], onesb[:, :], HM[:, kc, :], kc == 0, kc == 7, hk(kc) + ["onesb"], [kst])
        RS, kRS = ft.alloc()
        P.add("act", lambda: S_.activation(out=RS[:, :], in_=pst[:, :], func=AF.Ln, bias=EPS, scale=1.0 / 1024),
              [kst], [kRS])
        P.add("act", lambda: S_.activation(out=RS[:, :], in_=RS[:, :], func=AF.Exp, scale=-0.5), [kRS], [kRS])
        for kc in range(8):
            P.add("dve", lambda kc=kc: V_.scalar_tensor_tensor(
                out=HM[:, kc, :], in0=xT[:, kc, blk], scalar=cols[:, l, C_G + kc:C_G + kc + 1], in1=RS[:, :],
                op0=ALU.mult, op1=ALU.mult), xk(kc) + [kRS, "cols"], hk(kc))
        ft.release(kRS)

    def block(s, l, b, last_layer, p, nxt):
        hctx["p"] = p
        HM = HMs[p]
        XS = XSs[p]
        XSK = XSKs[p]
        t0 = b * 512
        tts = [4 * b + i for i in range(4)]
        blk = slice(t0, t0 + 512)
        xk = lambda kc: [("xT", kc, tt) for tt in tts]

        if cv["key"] == (s, l, b):
            cg, U1 = cv["gen"], cv["U1"]
        else:
            if b == 0:
                P.add("pool", lambda: G_.memset(U0[:, :, 0:30], 0.0), [], [("U0h", 0), ("U0h", 1)])
            for i in (0, 1):
                emit_cf_pair(i, p, acc)
            cg, U1 = make_conv(l)
        cv["key"] = None

        def drain_conv(n):
            drain(cg, n)

        def tanh_gate(pz, kz, out_ap, wkeys):
            P.add("act", lambda: S_.activation(out=out_ap, in_=pz[:, :], func=AF.Silu), [kz], wkeys)

        an = {}

        def a_sq():
            if nxt is None:
                return
            l2, b2 = nxt
            HM2 = HMs[1 - p]
            blk2 = slice(b2 * 512, b2 * 512 + 512)
            tts2 = [4 * b2 + i for i in range(4)]
            xk2 = lambda kc: [("xT", kc, tt) for tt in tts2]
            hk2 = lambda kc: [("HM", 1 - p, kc, 0), ("HM", 1 - p, kc, 1)]
            an["c"] = (l2, HM2, blk2, xk2, hk2)

        def a_sq_part(q):
            if nxt is None:
                return
            l2, HM2, blk2, xk2, hk2 = an["c"]
            for kc in (2 * q, 2 * q + 1):
                P.add("act", lambda kc=kc: S_.activation(out=HM2[:, kc, :], in_=xT[:, kc, blk2], func=AF.Square),
                      xk2(kc), hk2(kc))

        def a_mm():
            if nxt is None:
                return
            l2, HM2, blk2, xk2, hk2 = an["c"]
            pst, kst = stat.get()
            for kc in range(8):
                mm(pst[:, :], onesb[:, :], HM2[:, kc, :], kc == 0, kc == 7, hk2(kc) + ["onesb"], [kst])
            an["pst"] = (pst, kst)

        def a_rs():
            if nxt is None:
                return
            pst, kst = an["pst"]
            RS, kRS = ft.alloc()
            P.add("act", lambda: S_.activation(out=RS[:, :], in_=pst[:, :], func=AF.Ln, bias=EPS, scale=1.0 / 1024),
                  [kst], [kRS])
            P.add("act", lambda: S_.activation(out=RS[:, :], in_=RS[:, :], func=AF.Exp, scale=-0.5), [kRS], [kRS])
            an["rs"] = (RS, kRS)

        def a_ht():
            if nxt is None:
                return
            l2, HM2, blk2, xk2, hk2 = an["c"]
            RS, kRS = an["rs"]
            for kc in range(8):
                P.add("dve", lambda kc=kc: V_.scalar_tensor_tensor(
                    out=HM2[:, kc, :], in0=xT[:, kc, blk2], scalar=cols[:, l2, C_G + kc:C_G + kc + 1], in1=RS[:, :],
                    op0=ALU.mult, op1=ALU.mult), xk2(kc) + [kRS, "cols"], hk2(kc))
            ft.release(kRS)

        pF, kF = ps[7], ("ps", 7)
        for i, tt in enumerate(tts):
            pv, kv = misc.get()
            for kc in range(8):
                mm(pv[:, :], HM[:, kc, i * 128:(i + 1) * 128], WVv[:, kc, :], kc == 0, kc == 7, hmk(kc) + ["WVv"], [kv])
            for kc in range(8):
                mm(pF[:, i * 8:(i + 1) * 8], HM[:, kc, i * 128:(i + 1) * 128], WVf[:, kc, :], kc == 0, kc == 7,
                   hmk(kc) + ["WVf"], [kF])
            evac_copy(V[:, tt, :, 0:64], pv[:, :].rearrange("p (h d) -> p h d", h=8), [kv], [("V", tt)])
        FZ, kFZ = ft.alloc()
        P.add("dve", lambda: V_.tensor_tensor(out=FZ[:, 0:32], in0=pF[:, 0:32], in1=bfb[:, l, :], op=ALU.add),
              [kF, "bfb"], [kFZ])
        P.add("act", lambda: S_.activation(out=FZ[:, 0:32], in_=FZ[:, 0:32], func=AF.Exp, scale=-1.0), [kFZ], [kFZ])
        P.add("act", lambda: S_.activation(out=FZ[:, 0:32], in_=FZ[:, 0:32], func=AF.Ln, bias=1.0), [kFZ], [kFZ])
        cdma_w, cdma_r = [], []

        def flush_cdma(lst):
            while lst:
                it = lst.pop(0)
                if callable(it):
                    it()
                else:
                    P.add("sp", it[0], it[1], it[2], dma=True)

        def fpath_tail():
            pC, kC = ps[6], ("ps", 6)
            for i in range(4):
                for j in range(i + 1):
                    rhs = negtri_f if j == i else nonesf[:, :]
                    mm(pC[0:8, i * 128:(i + 1) * 128], FZ[:, j * 8:(j + 1) * 8], rhs, j == 0, j == i,
                       [kFZ, "cstf", "nonesf"], [kC])
            CC, kCC = ft.alloc()
            P.add("dve", lambda: V_.tensor_scalar(out=CC[0:8, :], in0=pC[0:8, :], scalar1=CB[0:8, 0:1], scalar2=None,
                                                  op0=ALU.add), [kC, "CB"], [kCC])
            ft.release(kFZ)
            P.add("dve", lambda: V_.tensor_copy(out=CB[0:8, 0:1], in_=CC[0:8, 511:512]), [kCC], ["CB"])
            HI, kHI = bt.alloc()
            MID, kMID = bt.alloc()
            LO, kLO = bt.alloc()
            R1, kR1 = ft.alloc()
            R2, kR2 = ft.alloc()
            P.add("dve", lambda: V_.tensor_copy(out=HI[0:8, :], in_=CC[0:8, :]), [kCC], [kHI])
            P.add("dve", lambda: V_.tensor_tensor(out=R1[0:8, :], in0=CC[0:8, :], in1=HI[0:8, :], op=ALU.subtract),
                  [kCC, kHI], [kR1])
            P.add("dve", lambda: V_.tensor_copy(out=MID[0:8, :], in_=R1[0:8, :]), [kR1], [kMID])
            P.add("dve", lambda: V_.tensor_tensor(out=R2[0:8, :], in0=R1[0:8, :], in1=MID[0:8, :], op=ALU.subtract),
                  [kR1, kMID], [kR2])
            P.add("dve", lambda: V_.tensor_copy(out=LO[0:8, :], in_=R2[0:8, :]), [kR2], [kLO])
            ft.release(kCC)
            ft.release(kR1)
            ft.release(kR2)
            sc = scr_d[b % 2]
            for j, (tile_, key_) in enumerate(((HI, kHI), (MID, kMID), (LO, kLO))):
                cdma_w.append((lambda j=j, tile_=tile_: nc.sync.dma_start(out=sc[:, j, :], in_=tile_[0:8, :]),
                               [key_], [("scr", b % 2, j)]))
            cdma_w.append(lambda: (bt.release(kHI), bt.release(kMID), bt.release(kLO)))
            scr_keys = [("scr", b % 2, j) for j in range(3)]
            cdma_r.append((lambda: nc.sync.dma_start(out=QA[64:67, :, :], in_=sc.rearrange("h j t -> j h t")),
                           scr_keys + [("QAc",)], [("QAaug",)]))
            cdma_r.append((lambda: nc.sync.dma_start(out=KA[67:70, :, blk], in_=sc.rearrange("h j t -> j h t")),
                           scr_keys + [("KAc",)], [("KAaug", b)]))

        def qk_s1(which, j):
            pa, ka = proj_chunk()
            SQ, kSQ = bt.alloc()
            P.add("act", lambda: S_.activation(out=SQ[:, :], in_=pa[:, :], func=AF.Square), [ka], [kSQ])
            return (which, j, pa, ka, SQ, kSQ)

        def qk_s2(st_):
            which, j, QR, kQR, SQ, kSQ = st_
            pm, km = stat.get()
            mm(pm[:, :], bdiag_b, SQ[:, :], True, True, [kSQ, "cstb"], [km])
            RQ, kRQ = ft.alloc()
            P.add("act", lambda: S_.activation(out=RQ[:, :], in_=pm[:, :], func=AF.Ln, bias=EPS, scale=1.0 / 64),
                  [km], [kRQ])
            P.add("act", lambda: S_.activation(out=RQ[:, :], in_=RQ[:, :], func=AF.Exp, scale=-0.5), [kRQ], [kRQ])
            for par in (0, 1):
                h = 2 * j + par
                r0 = 64 * par
                if which == "q":
                    dst = QA[0:64, h, :]
                    wkey = ("QA", h)
                    gc = DC[r0:r0 + 64, l, D_GQ8 + j:D_GQ8 + j + 1]
                    gk = ("DC", D_GQ8)
                else:
                    dst = KA[0:64, h, blk]
                    wkey = ("KA", h, b)
                    gc = cols[r0:r0 + 64, l, C_GK + j:C_GK + j + 1]
                    gk = "cols"
                P.add("dve", lambda dst=dst, gc=gc, r0=r0: V_.scalar_tensor_tensor(
                    out=dst, in0=QR[r0:r0 + 64, :], scalar=gc, in1=RQ[r0:r0 + 64, :], op0=ALU.mult, op1=ALU.mult),
                    [kQR, kRQ, gk], [wkey])
            ft.release(kRQ)
            bt.release(kSQ)

        pend = None
        for which, j in [("k", j) for j in range(4)] + [("q", j) for j in range(4)]:
            cur = qk_s1(which, j)
            if pend is not None:
                qk_s2(pend)
            pend = cur
            if which == "k" and j == 1:
                fpath_tail()
            drain_conv(1)

        flush_deferred()
        flush_cdma(cdma_w)
        for i in (0, 1):
            pz, kz = proj_chunk()
            if pend is not None:
                qk_s2(pend)
                pend = None
            tanh_gate(pz, kz, ZC[:, i, :], [("ZC", i)])
            drain_conv(1)
        for i in (0, 1):
            pc, kc_ = proj_chunk()
            px, kx = proj_chunk()
            CX, kCX = ft.alloc()
            P.add("act", lambda pc=pc, CX=CX: S_.copy(out=CX[:, :], in_=pc[:, :]), [kc_], [kCX])
            P.add("dve", lambda i=i, px=px, CX=CX: V_.tensor_tensor(out=S0[:, i, 2:514], in0=CX[:, :], in1=px[:, :], op=ALU.mult),
                  [kCX, kx], [("S0", i)])
            ft.release(kCX)
            drain_conv(2)
        flush_cdma(cdma_r)
        for i in (0, 1):
            pb, kb_ = proj_chunk()
            pz, kz = proj_chunk()
            T2, kT2 = ft.alloc()
            tanh_gate(pz, kz, T2[:, :], [kT2])
            P.add("dve", lambda i=i, pb=pb, T2=T2: V_.tensor_tensor(out=GS[:, i, :], in0=T2[:, :], in1=pb[:, :], op=ALU.mult),
                  [kT2, kb_], [("GS", i)])
            ft.release(kT2)
            drain_conv(2)
        a_sq()
        for j in range(4):
            pz, kz = proj_chunk()
            tanh_gate(pz, kz, ZA[:, j, :], [("ZA", j)])
            a_sq_part(j)
            drain_conv(1)
        a_mm()
        a_rs()

        ln = {}

        def ln1():
            pmu, kmu = stat.get()
            pm2, km2 = stat.get()
            for i in (0, 1):
                Tb, kTb = bt.alloc()
                Tq, kTq = bt.alloc()
                P.add("act", lambda i=i, Tb=Tb: S_.copy(out=Tb[:, :], in_=U1[i][0][:, :]), [U1[i][1]], [kTb])
                P.add("act", lambda i=i, Tq=Tq: S_.activation(out=Tq[:, :], in_=U1[i][0][:, :], func=AF.Square), [U1[i][1]], [kTq])
                mm(pmu[:, :], onesb[:, :], Tb[:, :], i == 0, i == 1, [kTb, "onesb"], [kmu])
                mm(pm2[:, :], onesb[:, :], Tq[:, :], i == 0, i == 1, [kTq, "onesb"], [km2])
                bt.release(kTb)
                bt.release(kTq)
            ln["st"] = (pmu, kmu, pm2, km2)

        def ln2a():
            pmu, kmu, pm2, km2 = ln["st"]
            MEAN, kME = ft.alloc()
            MSQ, kMS = ft.alloc()
            if b >= 2:
                P.add("dve", lambda: V_.tensor_scalar(out=MEAN[:, :], in0=pmu[:, :], scalar1=1.0 / 256, scalar2=None, op0=ALU.mult),
                      [kmu], [kME])
                P.add("dve", lambda: V_.tensor_tensor(out=MSQ[:, :], in0=MEAN[:, :], in1=MEAN[:, :], op=ALU.mult), [kME], [kMS])
            else:
                P.add("act", lambda: S_.activation(out=MEAN[:, :], in_=pmu[:, :], func=AF.Identity, scale=1.0 / 256), [kmu], [kME])
                P.add("act", lambda: S_.activation(out=MSQ[:, :], in_=pmu[:, :], func=AF.Square, scale=1.0 / 256), [kmu], [kMS])
            ln["m"] = (MEAN, kME, MSQ, kMS)

        def ln2b():
            pmu, kmu, pm2, km2 = ln["st"]
            MEAN, kME, MSQ, kMS = ln["m"]
            VAR, kVA = MSQ, kMS
            P.add("dve", lambda: V_.scalar_tensor_tensor(out=VAR[:, :], in0=pm2[:, :], scalar=1.0 / 256, in1=MSQ[:, :],
                                                         op0=ALU.mult, op1=ALU.subtract), [km2, kMS], [kVA])
            P.add("dve", lambda: V_.tensor_scalar(out=VAR[:, :], in0=VAR[:, :], scalar1=0.0, scalar2=EPS,
                                                  op0=ALU.max, op1=ALU.add), [kVA], [kVA])
            ln["v"] = (VAR, kVA)

        def ln2c():
            VAR, kVA = ln["v"]
            P.add("act", lambda: S_.activation(out=VAR[:, :], in_=VAR[:, :], func=AF.Ln), [kVA], [kVA])

        def ln2c2():
            VAR, kVA = ln["v"]
            P.add("act", lambda: S_.activation(out=VAR[:, :], in_=VAR[:, :], func=AF.Exp, scale=-0.5), [kVA], [kVA])

        def ln2d():
            MEAN, kME, MSQ, kMS = ln["m"]
            VAR, kVA = ln["v"]
            for i in (0, 1):
                XN, kXN = U1[i]
                P.add("dve", lambda XN=XN: V_.tensor_tensor(out=XN[:, :], in0=XN[:, :], in1=MEAN[:, :], op=ALU.subtract),
                      [kXN, kME], [kXN])
                P.add("dve", lambda XN=XN: V_.tensor_tensor(out=XN[:, :], in0=XN[:, :], in1=VAR[:, :], op=ALU.mult),
                      [kXN, kVA], [kXN])
            ft.release(kME)
            ft.release(kVA)

        def ln2e():
            pass

        def ln2f():
            SL = []
            for i in (0, 1):
                XN, kXN = U1[i]
                SLt, kSL = bt.alloc()
                P.add("act", lambda XN=XN, SLt=SLt, i=i: S_.activation(
                    out=SLt[:, :], in_=XN[:, :], func=AF.Silu, bias=cols[:, l, C_LNB + i:C_LNB + i + 1],
                    scale=cols[:, l, C_LNG + i:C_LNG + i + 1]), [kXN, "cols"], [kSL])
                ft.release(kXN)
                SL.append((SLt, kSL))
            ln["SL"] = SL

        def ln2g():
            pass

        def ln3a():
            SL = ln["SL"]
            pps = []
            for co in (0, 1):
                ppo, kpo = stat.get()
                for ci in (0, 1):
                    mm(ppo[:, :], PW[:, ci, co * 128:(co + 1) * 128], SL[ci][0][:, :], ci == 0, ci == 1, [SL[ci][1], "PW"], [kpo])
                pps.append((ppo, kpo))
            for i in (0, 1):
                bt.release(SL[i][1])
            ln["pp"] = pps

        def ln3b():
            for co in (0, 1):
                ppo, kpo = ln["pp"][co]
                P.add("dve", lambda co=co, ppo=ppo: V_.scalar_tensor_tensor(
                    out=HM[:, co, :], in0=ppo[:, :], scalar=1.0, in1=ZC[:, co, :], op0=ALU.mult, op1=ALU.mult),
                    [kpo, ("ZC", co)], hmk(co))

        def sc_stage():
            for i in (0, 1):
                SA, kSA = ft.alloc()
                SB, kSB = ft.alloc()
                w = lambda k: DC[:, l, D_SCWH + i * 3 + k:D_SCWH + i * 3 + k + 1]
                rk = [("S0", i), ("S0h", i), ("DC", D_SCWH)]
                P.add("dve", lambda i=i, SA=SA, w0=w(0): V_.tensor_scalar(out=SA[:, :], in0=S0[:, i, 0:512], scalar1=w0, scalar2=None,
                                                                        op0=ALU.mult), rk, [kSA])
                P.add("dve", lambda i=i, SA=SA, SB=SB, w1=w(1): V_.scalar_tensor_tensor(
                    out=SB[:, :], in0=S0[:, i, 1:513], scalar=w1, in1=SA[:, :], op0=ALU.mult, op1=ALU.add), rk + [kSA], [kSB])
                P.add("dve", lambda i=i, SA=SA, SB=SB, w2=w(2): V_.scalar_tensor_tensor(
                    out=SA[:, :], in0=S0[:, i, 2:514], scalar=w2, in1=SB[:, :], op0=ALU.mult, op1=ALU.add), rk + [kSB], [kSA])
                P.add("dve", lambda i=i, SA=SA: V_.tensor_tensor(out=HM[:, 2 + i, :], in0=SA[:, :], in1=GS[:, i, :], op=ALU.mult),
                      [kSA, ("GS", i)], hmk(2 + i))
                P.add("dve", lambda i=i: V_.tensor_copy(out=S0[:, i, 0:2], in_=S0[:, i, 512:514]), [("S0", i)], [("S0h", i)])
                ft.release(kSA)
                ft.release(kSB)

        def ln1_act():
            tl = []
            for i in (0, 1):
                _, kT = ft.alloc()
                Tb = FTb16[:, kT[1], 0:512]
                Tq = FTb16[:, kT[1], 512:1024]
                if b >= 2:
                    P.add("dve", lambda i=i, Tb=Tb: V_.tensor_copy(out=Tb, in_=U1[i][0][:, :]), [U1[i][1]], [kT])
                    P.add("dve", lambda i=i, Tq=Tq: V_.tensor_tensor(out=Tq, in0=U1[i][0][:, :], in1=U1[i][0][:, :], op=ALU.mult),
                          [U1[i][1], kT], [kT])
                else:
                    P.add("act", lambda i=i, Tb=Tb: S_.copy(out=Tb, in_=U1[i][0][:, :]), [U1[i][1]], [kT])
                    P.add("act", lambda i=i, Tq=Tq: S_.activation(out=Tq, in_=U1[i][0][:, :], func=AF.Square), [U1[i][1], kT], [kT])
                tl.append((Tb, Tq, kT))
            ln["tl"] = tl

        def ln1_mm():
            pmu, kmu = stat.get()
            pm2, km2 = stat.get()
            for i, (Tb, Tq, kT) in enumerate(ln["tl"]):
                mm(pmu[:, :], onesb[:, :], Tb, i == 0, i == 1, [kT, "onesb"], [kmu])
                mm(pm2[:, :], onesb[:, :], Tq, i == 0, i == 1, [kT, "onesb"], [km2])
                ft.release(kT)
            ln["st"] = (pmu, kmu, pm2, km2)

        def taps6():
            drain_conv(6)

        def taps_all():
            drain_conv(1000)

        def pf_halo():
            if nxt is not None and nxt[1] == 0:
                P.add("pool", lambda: G_.memset(U0[:, :, 0:30], 0.0), [], [("U0h", 0), ("U0h", 1)])

        def pf_cf0():
            if nxt is not None:
                emit_cf_pair(0, 1 - p, stat)

        def pf_cf1():
            if nxt is not None:
                emit_cf_pair(1, 1 - p, stat)

        def pf_conv():
            if nxt is not None:
                g_, u_ = make_conv(nxt[0])
                cv["key"] = (s, nxt[0], nxt[1])
                cv["gen"] = g_
                cv["U1"] = u_

        NSL = 32
        slot_fns = {0: [sc_stage, taps6], 1: [taps6], 2: [taps6], 3: [taps6], 4: [taps6], 5: [taps6],
                    6: [a_ht, taps6], 7: [taps_all], 8: [ln1_act], 10: [ln1_mm], 12: [ln2a], 13: [ln2b], 15: [ln2c], 16: [ln2c2],
                    17: [ln2d], 19: [ln2e, pf_halo], 20: [pf_cf0, ln2f, pf_cf1], 23: [ln2g], 25: [ln3a],
                    27: [ln3b], 28: [pf_conv]}

        steps = [(h, kb) for h in range(8) for kb in range(4 * b + 4)]
        nsteps = len(steps)
        stb = [(ps[i], ("ps", i)) for i in range(4)]
        pts = {}
        nrm = {}

        def emit_score(idx):
            h, kb = steps[idx]
            st, kst_ = stb[idx % 4]
            i = kb - 4 * b
            n0 = 128 * i if i > 0 else 0
            rd = [("KA", h, kb // 4), ("KAaug", kb // 4), ("KAc",), ("QA", h), ("QAaug",), ("QAc",)]
            mm(st[:, n0:512], KA[0:70, h, kb * 128:(kb + 1) * 128], QA[0:70, h, n0:512], True, i < 0, rd, [kst_])
            if i >= 0:
                mm(st[:, n0:n0 + 128], ident_b, maskneg_b, False, True, ["cstb"], [kst_])
            PTt, kPT = bt.alloc()
            P.add("act", lambda: S_.activation(out=PTt[:, n0:512], in_=st[:, n0:512], func=AF.Exp), [kst_], [kPT])
            pts[idx] = (PTt, kPT, n0)

        def n1(h):
            po, kpo = ps[4 + h % 2], ("ps", 4 + h % 2)
            if h % 2 == 0:
                NUM, kNU = ft.alloc()
                DEN, kDE = ft.alloc()
                nrm["pair"] = (NUM, kNU, DEN, kDE)
                if b >= 2:
                    P.add("dve", lambda: V_.tensor_copy(out=NUM[0:64, :], in_=po[0:64, :]), [kpo], [kNU])
                else:
                    P.add("act", lambda: S_.copy(out=NUM[0:64, :], in_=po[0:64, :]), [kpo], [kNU])
                P.add("act", lambda: S_.activation(out=DEN[0:64, :], in_=po[64:128, :], func=AF.Ln), [kpo], [kDE])
            else:
                NUM, kNU, DEN, kDE = nrm["pair"]
                P.add("dve", lambda: V_.tensor_copy(out=NUM[64:128, :], in_=po[0:64, :]), [kpo], [kNU])
                P.add("act", lambda: S_.activation(out=DEN[64:128, :], in_=po[64:128, :], func=AF.Ln), [kpo], [kDE])
                nrm[h // 2] = (NUM, kNU, DEN, kDE)

        def n2(j):
            NUM, kNU, DEN, kDE = nrm[j]
            P.add("act", lambda: S_.activation(out=DEN[:, :], in_=DEN[:, :], func=AF.Exp, scale=-1.0), [kDE], [kDE])
            P.add("dve", lambda: V_.tensor_tensor(out=NUM[:, :], in0=NUM[:, :], in1=DEN[:, :], op=ALU.mult), [kNU, kDE], [kNU])

        def n3(j):
            NUM, kNU, DEN, kDE = nrm.pop(j)
            P.add("dve", lambda: V_.tensor_tensor(out=HM[:, 4 + j, :], in0=NUM[:, :], in1=ZA[:, j, :], op=ALU.mult),
                  [kNU, ("ZA", j)], [("HM", p, 4 + j, 0), ("HM", p, 4 + j, 1)])
            ft.release(kNU)
            ft.release(kDE)

        def emit_pv(idx):
            h, kb = steps[idx]
            PTt, kPT, n0 = pts.pop(idx)
            po, kpo = ps[4 + h % 2], ("ps", 4 + h % 2)
            last = kb == 4 * b + 3
            mm(po[:, n0:512], V[:, kb, h, :], PTt[:, n0:512], kb == 0, last, [("V", kb), kPT], [kpo])
            bt.release(kPT)
            if last:
                n1(h)
                if h % 2 == 0 and h >= 2:
                    pend_n2.append(h // 2 - 1)
                if h % 2 == 1 and h >= 3:
                    while pend_n2:
                        n2(pend_n2.pop(0))
                    n3(h // 2 - 1)

        fired = set()
        pend_n2 = []
        for idx in range(nsteps + LA):
            if idx < nsteps:
                emit_score(idx)
            j = idx - LA
            if j >= 0:
                emit_pv(j)
                k = ((j + 1) * NSL) // nsteps - 1
                for kk in range(k + 1):
                    if kk not in fired and (kk + 1) * nsteps <= (j + 1) * NSL:
                        fired.add(kk)
                        late_step()
                        if pend_n2:
                            n2(pend_n2.pop(0))
                        for fn_ in slot_fns.get(kk, ()):
                            fn_()
        while pend_n2:
            n2(pend_n2.pop(0))
        n2(3)
        n3(3)
        for kk in range(NSL):
            if kk not in fired:
                fired.add(kk)
                for fn_ in slot_fns.get(kk, ()):
                    fn_()

        for m in range(8):
            pa, ka = proj_chunk()
            P.add("dve", lambda m=m, pa=pa: V_.tensor_tensor(out=xT[:, m, blk], in0=xT[:, m, blk], in1=pa[:, :], op=ALU.add),
                  xk(m) + [ka], xk(m))
            if cv["key"] is not None:
                drain(cv["gen"], 2)

        if last_layer:
            for i, tt in enumerate(tts):
                XO = XS[i % 2]
                if i >= 2:
                    flush_deferred(1)
                for half in (0, 1):
                    m0 = 4 * half
                    pt_, kpt = misc.get()
                    for q in range(4):
                        P.add("pe", lambda q=q, pt_=pt_, m0=m0, tt=tt: T_.transpose(
                            pt_[:, q * 128:(q + 1) * 128], xT[:, m0 + q, tt * 128:(tt + 1) * 128], ident_f),
                            [("xT", m0 + q, tt), "cstf"], [kpt])
                    evac_copy(XO[:, m0 * 128:(m0 + 4) * 128], pt_[:, :], [kpt], XSK[i % 2][4 * half * 2:(4 * half + 4) * 2] if False else [("XO", p, i % 2, half)] + XSK[i % 2])
                deferred.append((lambda XO=XO, tt=tt, s=s: nc.sync.dma_start(out=y_d[s, tt * 128:(tt + 1) * 128, :], in_=XO),
                                 [("XO", p, i % 2, 0), ("XO", p, i % 2, 1)] + XSK[i % 2], [("y", s, tt)] + XSK[i % 2]))

    for s in range(NSEQ):
        flush_deferred()
        for tt in range(NTT):
            XI = XSs[0][tt % 2]
            XIK = XSKs[0][tt % 2]
            P.add("sp", lambda XI=XI, tt=tt, s=s: nc.sync.dma_start(out=XI, in_=x_d[s, tt * 128:(tt + 1) * 128, :]),
                  [], XIK, dma=True)
            for half in (0, 1):
                m0 = 4 * half
                pt_, kpt = misc.get()
                for q in range(4):
                    P.add("pe", lambda q=q, pt_=pt_, XI=XI, m0=m0: T_.transpose(
                        pt_[:, q * 128:(q + 1) * 128], XI[:, (m0 + q) * 128:(m0 + q + 1) * 128], ident_f),
                        XIK + ["cstf"], [kpt])
                evac_copy(xT[:, m0:m0 + 4, tt * 128:(tt + 1) * 128], pt_[:, :].rearrange("p (a b) -> p a b", a=4),
                          [kpt], [("xT", m0 + q, tt) for q in range(4)])
        seqblocks = [(l, b) for l in range(NL) for b in range(NBLK)]
        emit_A(0, 0, 0)
        for n_, (l, b) in enumerate(seqblocks):
            if b == 0:
                P.add("sp", lambda l=l: nc.sync.dma_start(out=WVv.reshape([128, 4096])[:, :], in_=wv_b[l]),
                      [("wvsc", l, k) for k in range(4)], ["WVv"], dma=True)
                P.add("sp", lambda l=l: nc.sync.dma_start(out=PW.reshape([128, 512])[:, :], in_=pw_b[l]), [("pwsc", l)], ["PW"], dma=True)
                P.add("sp", lambda l=l: nc.sync.dma_start(out=WFs[:, :], in_=wf_d[l]), [], ["WFs"], dma=True)
                P.add("dve", lambda: V_.tensor_copy(out=WVf.reshape([128, 64])[:, :], in_=WFs[:, :]), ["WFs"], ["WVf"])
                P.add("pool", lambda: G_.memset(CB[:, :], 0.0), [], ["CB"])
                P.add("pool", lambda: G_.memset(S0[:, :, 0:2], 0.0), [], [("S0h", 0), ("S0h", 1)])
            nxt = seqblocks[n_ + 1] if n_ + 1 < len(seqblocks) else None
            if s == 0 and l == 0 and b <= NBLK - 2:
                lstate["active"] = True
            elif not lstate["done"]:
                lstate["active"] = True
                late_flush()
            block(s, l, b, l == NL - 1, n_ % 2, nxt)
            lstate["active"] = False

    flush_deferred()
    ykeys = [("y", s, tt) for s in range(NSEQ) for tt in range(NTT)]
    P.add("sp", lambda: None, ykeys, [])
    return nc, P


def prep_shared(inputs, NL):
    w_in = np.asarray(inputs["w_in"], np.float32)[:NL]
    w_out = np.asarray(inputs["w_out"], np.float32)[:NL]
    order = chunk_order()
    wfm = np.empty((NL, 26, 128, 8, 128), np.float32)
    for c, (kind, idx) in enumerate(order):
        c0 = _COL0[kind] + idx * 128
        blk = w_in[:, :, c0:c0 + 128].reshape(NL, 8, 128, 128)
        wfm[:, c] = blk.transpose(0, 2, 1, 3)
    wv = np.ascontiguousarray(w_in[:, :, 2816:3328].reshape(NL, 8, 128, 512).transpose(0, 2, 1, 3))
    wf = np.ascontiguousarray(w_in[:, :, 3840:3848].reshape(NL, 8, 128, 8).transpose(0, 2, 1, 3)).reshape(NL, 128, 64)
    wo = w_out.reshape(NL, 8, 128, 8, 128).transpose(0, 3, 2, 1, 4)
    pw = np.asarray(inputs["cf_pw"], np.float32)[:NL].reshape(NL, 2, 128, 256).transpose(0, 2, 1, 3)
    cols = np.zeros((128, NL, NCOL), np.float32)
    g = np.asarray(inputs["norm_g"], np.float32)[:NL]
    cols[:, :, C_G:C_G + 8] = g.reshape(NL, 8, 128).transpose(2, 0, 1)
    for name, c0 in (("cf_dw_b", C_CFB), ("cf_ln_g", C_LNG), ("cf_ln_b", C_LNB)):
        a = np.asarray(inputs[name], np.float32)[:NL]
        cols[:, :, c0:c0 + 2] = a.reshape(NL, 2, 128).transpose(2, 0, 1)
    cfw = np.asarray(inputs["cf_dw"], np.float32)[:NL]
    cols[:, :, C_CFW:C_CFW + 62] = cfw.reshape(NL, 31, 2, 128).transpose(3, 0, 2, 1).reshape(128, NL, 62)
    scw = np.asarray(inputs["sc_dw"], np.float32)[:NL]
    cols[:, :, C_SCW:C_SCW + 6] = scw.reshape(NL, 3, 2, 128).transpose(3, 0, 2, 1).reshape(128, NL, 6)
    for name, c0 in (("q_norm_g", C_GQ), ("k_norm_g", C_GK)):
        a = np.asarray(inputs[name], np.float32)[:NL]
        cols[:, :, c0:c0 + 4] = a.reshape(NL, 4, 128).transpose(2, 0, 1)
    bf = np.asarray(inputs["b_f"], np.float32)[:NL]
    bfb = np.broadcast_to(np.tile(bf, (1, 4))[None], (128, NL, 32)).copy()
    cst = np.zeros((128, 512), np.float32)
    ii = np.arange(128)
    cst[:, 0:128] = np.eye(128, dtype=np.float32)
    cst[:, 128:256] = -(ii[:, None] <= ii[None, :]).astype(np.float32)
    cst[:, 256:384] = (ii[:, None] // 64 == ii[None, :] // 64).astype(np.float32)
    cst[:, 384:512] = np.where(ii[:, None] <= ii[None, :], 0.0, -30000.0).astype(np.float32)
    return dict(wfm=np.ascontiguousarray(wfm), wv=wv, wf=wf, wo=np.ascontiguousarray(wo), pw=np.ascontiguousarray(pw),
                cols=cols, bfb=bfb, cst=cst)


_CACHE = {}


def run(inputs, NSEQ, NBLK, NL, n_cores, x_shards):
    key = (NSEQ, NBLK, NL)
    if key not in _CACHE:
        nc, P = build(NSEQ, NBLK, NL)
        with ExitStack() as stack:
            info = P.finalize(stack)
        _CACHE[key] = (nc, info)
    nc, info = _CACHE[key]
    shared = prep_shared(inputs, NL)
    in_maps = [dict(shared, x=np.ascontiguousarray(xs)) for xs in x_shards]
    res = run_bass_kernel_spmd(nc, in_maps, core_ids=list(range(n_cores)))
    return [r["y"] for r in res.results], info


def kernel(**inputs):
    x = np.asarray(inputs["x"], np.float32)
    n = 8
    shards = [x[2 * c:2 * c + 2] for c in range(n)]
    outs, _ = run(inputs, 2, 4, 2, n, shards)
    return np.concatenate(outs, axis=0).astype(np.float32)
```
